# Optimizing a Trainium2 kernel written in Bass

```python
import math
import jax, jax.numpy as jnp
from jax import lax
import numpy as np

D_MODEL = 1024
BATCH = 16
SEQ = 256
DEPTH = 2
DEC_BATCH = 2
DEC_SEQ = 4096
PAST_LEN = 256

GRID_W = 64
Q_BLOCK = 128
HEAD_DIM = 64
ROPE_THETA = 10000.0
A_HEADS = 4
A_KV_HEADS = 2
A_GROUP = A_HEADS // A_KV_HEADS
A_WIDTH = A_HEADS * HEAD_DIM
B_HEADS = 4
B_QK_DIM = 64
B_V_DIM = 2 * B_QK_DIM
B_WIDTH = B_HEADS * B_V_DIM
C_GROUPS = 4
C_GROUP_DIM = 64
C_WIDTH = C_GROUPS * C_GROUP_DIM
MIX_WIDTH = A_WIDTH + B_WIDTH + C_WIDTH
IN_SIZES = [A_HEADS * HEAD_DIM, A_KV_HEADS * HEAD_DIM, A_KV_HEADS * HEAD_DIM,
            B_HEADS * 2 * B_QK_DIM, B_HEADS * 2 * B_QK_DIM, B_HEADS * B_V_DIM, C_WIDTH]
IN_WIDTH = sum(IN_SIZES)
IN_SPLITS = [int(s) for s in np.cumsum(IN_SIZES)[:-1]]
FFN_DIM = 2816
N_MOD = 9
ALPHA = (2.0 * DEPTH) ** 0.25
BETA = (8.0 * DEPTH) ** -0.25
LN_EPS = 1e-5
RMS_EPS = 1e-6

kernel_name = "hybrid_diffusion_prefix_trunk_step"


def layer_norm(x, g, b):
    xf = x.astype(jnp.float32)
    mu = jnp.mean(xf, -1, keepdims=True)
    xc = xf - mu
    var = jnp.mean(xc * xc, -1, keepdims=True)
    return (xc * lax.rsqrt(var + LN_EPS) * g + b).astype(x.dtype)


def rms_norm(x, g):
    xf = x.astype(jnp.float32)
    y = xf * lax.rsqrt(jnp.mean(xf * xf, -1, keepdims=True) + RMS_EPS)
    return (y * g).astype(x.dtype)


def rope_tables(length):
    rows = length // GRID_W
    row = jnp.repeat(jnp.arange(rows), GRID_W).astype(jnp.float32)
    col = jnp.tile(jnp.arange(GRID_W), rows).astype(jnp.float32)
    half = HEAD_DIM // 2
    inv = 1.0 / (ROPE_THETA ** (jnp.arange(0, half, 2, dtype=jnp.float32) / half))
    ar = row[:, None] * inv
    ac = col[:, None] * inv
    cos = jnp.concatenate([jnp.cos(ar), jnp.cos(ar), jnp.cos(ac), jnp.cos(ac)], -1)
    sin = jnp.concatenate([jnp.sin(ar), jnp.sin(ar), jnp.sin(ac), jnp.sin(ac)], -1)
    return cos, sin


def _rot_half(y):
    q = y.shape[-1] // 2
    return jnp.concatenate([-y[..., q:], y[..., :q]], -1)


def apply_rope(x, cos, sin):
    half = HEAD_DIM // 2
    rot = jnp.concatenate([_rot_half(x[..., :half]), _rot_half(x[..., half:])], -1)
    out = x.astype(jnp.float32) * cos[:, None, :] + rot.astype(jnp.float32) * sin[:, None, :]
    return out.astype(x.dtype)


def sweep_queries(fn, q):
    b, length = q.shape[0], q.shape[1]
    nb = length // Q_BLOCK
    qb = jnp.moveaxis(q.reshape((b, nb, Q_BLOCK) + q.shape[2:]), 1, 0)
    out = jnp.moveaxis(lax.map(fn, qb), 0, 1)
    return out.reshape((b, length) + out.shape[3:])


def gqa_attention(q, k, v):
    scale = HEAD_DIM ** -0.5
    def blk(qb):
        s = jnp.einsum('bqhgd,bkhd->bhgqk', qb, k).astype(jnp.float32) * scale
        p = jax.nn.softmax(s, axis=-1)
        return jnp.einsum('bhgqk,bkhd->bqhgd', p.astype(v.dtype), v)
    return sweep_queries(blk, q)


def diff_attention(q, k, v, lam):
    scale = B_QK_DIM ** -0.5
    def blk(qb):
        s = jnp.einsum('bqhcd,bkhcd->bhcqk', qb, k).astype(jnp.float32) * scale
        p = jax.nn.softmax(s, axis=-1)
        w = p[:, :, 0] - lam * p[:, :, 1]
        return jnp.einsum('bhqk,bkhe->bqhe', w.astype(v.dtype), v)
    return sweep_queries(blk, q)


def fourier_mix(u):
    b, length, _ = u.shape
    ug = u.reshape(b, length, C_GROUPS, C_GROUP_DIM).astype(jnp.float32)
    f = jnp.fft.fft2(ug, axes=(1, 3), norm='ortho').real
    return f.reshape(b, length, C_WIDTH).astype(u.dtype)


def token_mix(h, w_in, g_qa, g_ka, lam, lam_init, g_subln, w_fourier, w_out, ctx):
    b, length, _ = h.shape
    proj = h @ w_in
    qa, ka, va, qb, kb, vb, uc = jnp.split(proj, IN_SPLITS, axis=-1)
    qa = rms_norm(qa.reshape(b, length, A_HEADS, HEAD_DIM), g_qa)
    ka = rms_norm(ka.reshape(b, length, A_KV_HEADS, HEAD_DIM), g_ka)
    va = va.reshape(b, length, A_KV_HEADS, HEAD_DIM)
    qb = qb.reshape(b, length, B_HEADS * 2, B_QK_DIM)
    kb = kb.reshape(b, length, B_HEADS * 2, B_QK_DIM)
    vb = vb.reshape(b, length, B_HEADS, B_V_DIM)
    if ctx is None:
        new = (ka, va, kb.reshape(b, length, B_HEADS, 2 * B_QK_DIM), vb)
        ka_all, va_all = ka, va
        kb_all = kb.reshape(b, length, B_HEADS, 2, B_QK_DIM)
        vb_all = vb
    else:
        cos, sin = rope_tables(length)
        qa = apply_rope(qa, cos, sin)
        ka = apply_rope(ka, cos, sin)
        qb = apply_rope(qb, cos, sin)
        kb = apply_rope(kb, cos, sin)
        c_ka, c_va, c_kb, c_vb = ctx
        lc = c_ka.shape[1]
        ka_all = jnp.concatenate([c_ka, ka], 1)
        va_all = jnp.concatenate([c_va, va], 1)
        kb_all = jnp.concatenate([c_kb.reshape(b, lc, B_HEADS, 2, B_QK_DIM),
                                  kb.reshape(b, length, B_HEADS, 2, B_QK_DIM)], 1)
        vb_all = jnp.concatenate([c_vb, vb], 1)
        new = None
    o_a = gqa_attention(qa.reshape(b, length, A_KV_HEADS, A_GROUP, HEAD_DIM), ka_all, va_all)
    o_a = o_a.reshape(b, length, A_WIDTH)
    o_b = diff_attention(qb.reshape(b, length, B_HEADS, 2, B_QK_DIM), kb_all, vb_all, lam)
    o_b = (rms_norm(o_b, g_subln) * (1.0 - lam_init)).reshape(b, length, B_WIDTH)
    o_c = fourier_mix(uc) @ w_fourier
    out = jnp.concatenate([o_a, o_b, o_c], -1) @ w_out
    return out, new


def swiglu(h, w_gu, w_down):
    g, u = jnp.split(h @ w_gu, 2, axis=-1)
    return (jax.nn.silu(g) * u) @ w_down


def trunk(x, cvec, caches, w_mod, b_mod, w_in, g_qa, g_ka, lam_q1, lam_k1, lam_q2, lam_k2,
          g_subln, w_fourier, w_out, w_ffn1_gu, w_ffn1_down, w_ffn2_gu, w_ffn2_down, ln_g, ln_b):
    new_list = []
    for l in range(DEPTH):
        mod = jax.nn.silu(cvec) @ w_mod[l] + b_mod[l]
        sh1, sc1, gt1, sh2, sc2, gt2, sh3, sc3, gt3 = [m[:, None, :] for m in jnp.split(mod, N_MOD, -1)]
        lam_init = 0.8 - 0.6 * math.exp(-0.3 * l)
        lam = (jnp.exp(jnp.sum(lam_q1[l].astype(jnp.float32) * lam_k1[l].astype(jnp.float32)))
               - jnp.exp(jnp.sum(lam_q2[l].astype(jnp.float32) * lam_k2[l].astype(jnp.float32)))
               + lam_init)
        f1 = swiglu(x * (1 + sc1) + sh1, w_ffn1_gu[l], w_ffn1_down[l])
        x = layer_norm(ALPHA * x + 0.5 * gt1 * f1, ln_g[l, 0], ln_b[l, 0])
        ctx = None if caches is None else tuple(cch[:, l] for cch in caches)
        mo, new = token_mix(x * (1 + sc2) + sh2, w_in[l], g_qa[l], g_ka[l], lam, lam_init,
                            g_subln[l], w_fourier[l], w_out[l], ctx)
        x = layer_norm(ALPHA * x + gt2 * mo, ln_g[l, 1], ln_b[l, 1])
        f2 = swiglu(x * (1 + sc3) + sh3, w_ffn2_gu[l], w_ffn2_down[l])
        x = layer_norm(ALPHA * x + 0.5 * gt3 * f2, ln_g[l, 2], ln_b[l, 2])
        if new is not None:
            new_list.append(new)
    return x, new_list


def setup_inputs(seed: int = 0) -> dict:
    key = jax.random.key(seed)
    ks = jax.random.split(key, 32)
    f32 = jnp.float32
    nrm = lambda k, s: jax.random.normal(k, s, f32)
    d = D_MODEL
    return {
        "x_prompt": nrm(ks[0], (BATCH, SEQ, d)),
        "x_sample": nrm(ks[1], (DEC_BATCH, DEC_SEQ, d)),
        "cache_a_k": nrm(ks[2], (DEC_BATCH, DEPTH, PAST_LEN, A_KV_HEADS, HEAD_DIM)),
        "cache_a_v": nrm(ks[3], (DEC_BATCH, DEPTH, PAST_LEN, A_KV_HEADS, HEAD_DIM)),
        "cache_b_k": nrm(ks[4], (DEC_BATCH, DEPTH, PAST_LEN, B_HEADS, 2 * B_QK_DIM)),
        "cache_b_v": nrm(ks[5], (DEC_BATCH, DEPTH, PAST_LEN, B_HEADS, B_V_DIM)),
        "c": nrm(ks[6], (DEC_BATCH, d)),
        "c_ctx": nrm(ks[7], (d,)),
        "w_mod": nrm(ks[8], (DEPTH, d, N_MOD * d)) * (0.5 * d ** -0.5),
        "b_mod": nrm(ks[9], (DEPTH, N_MOD * d)) * 0.01,
        "w_in": nrm(ks[10], (DEPTH, d, IN_WIDTH)) * d ** -0.5,
        "g_qa": 1.0 + 0.02 * nrm(ks[11], (DEPTH, HEAD_DIM)),
        "g_ka": 1.0 + 0.02 * nrm(ks[12], (DEPTH, HEAD_DIM)),
        "lam_q1": 0.1 * nrm(ks[13], (DEPTH, B_QK_DIM)),
        "lam_k1": 0.1 * nrm(ks[14], (DEPTH, B_QK_DIM)),
        "lam_q2": 0.1 * nrm(ks[15], (DEPTH, B_QK_DIM)),
        "lam_k2": 0.1 * nrm(ks[16], (DEPTH, B_QK_DIM)),
        "g_subln": 1.0 + 0.02 * nrm(ks[17], (DEPTH, B_V_DIM)),
        "w_fourier": nrm(ks[18], (DEPTH, C_WIDTH, C_WIDTH)) * C_WIDTH ** -0.5,
        "w_out": nrm(ks[19], (DEPTH, MIX_WIDTH, d)) * (BETA * MIX_WIDTH ** -0.5),
        "w_ffn1_gu": nrm(ks[20], (DEPTH, d, 2 * FFN_DIM)) * d ** -0.5,
        "w_ffn1_down": nrm(ks[21], (DEPTH, FFN_DIM, d)) * (BETA * FFN_DIM ** -0.5),
        "w_ffn2_gu": nrm(ks[22], (DEPTH, d, 2 * FFN_DIM)) * d ** -0.5,
        "w_ffn2_down": nrm(ks[23], (DEPTH, FFN_DIM, d)) * (BETA * FFN_DIM ** -0.5),
        "ln_g": 1.0 + 0.02 * nrm(ks[24], (DEPTH, 3, d)),
        "ln_b": 0.02 * nrm(ks[25], (DEPTH, 3, d)),
    }


def reference(x_prompt, x_sample, cache_a_k, cache_a_v, cache_b_k, cache_b_v, c, c_ctx,
              w_mod, b_mod, w_in, g_qa, g_ka, lam_q1, lam_k1, lam_q2, lam_k2, g_subln,
              w_fourier, w_out, w_ffn1_gu, w_ffn1_down, w_ffn2_gu, w_ffn2_down, ln_g, ln_b):
    weights = (w_mod, b_mod, w_in, g_qa, g_ka, lam_q1, lam_k1, lam_q2, lam_k2, g_subln,
               w_fourier, w_out, w_ffn1_gu, w_ffn1_down, w_ffn2_gu, w_ffn2_down, ln_g, ln_b)
    y_prompt, new = trunk(x_prompt, c_ctx[None, :], None, *weights)
    new_a_k = jnp.stack([n[0] for n in new], axis=1)
    new_a_v = jnp.stack([n[1] for n in new], axis=1)
    new_b_k = jnp.stack([n[2] for n in new], axis=1)
    new_b_v = jnp.stack([n[3] for n in new], axis=1)
    y_sample, _ = trunk(x_sample, c, (cache_a_k, cache_a_v, cache_b_k, cache_b_v), *weights)
    return (y_prompt, y_sample, new_a_k, new_a_v, new_b_k, new_b_v)
```

```python
import math
import numpy as np
import ml_dtypes
from contextlib import ExitStack
import concourse.bass as bass
import concourse.mybir as mybir
from concourse.bass_utils import run_bass_kernel_spmd

F32 = mybir.dt.float32
BF16 = mybir.dt.bfloat16
AF = mybir.ActivationFunctionType
ALU = mybir.AluOpType
AX = mybir.AxisListType

ENGS = ('pe', 'act', 'dve', 'pool', 'sp')


class Buf:
    __slots__ = ('name', 'w', 'r')

    def __init__(self, name=''):
        self.name = name
        self.w = None
        self.r = []


class Op:
    __slots__ = ('eng', 'fn', 'deps', 'idx', 'signal', 'count', 'is_dma', 'sem', 'val',
                 'waits', 'known_after', 'id', 'inc', 'own')

    def __init__(self, eng, fn, is_dma, inc=16):
        self.eng = eng
        self.fn = fn
        self.is_dma = is_dma
        self.deps = []
        self.signal = False
        self.count = 0
        self.sem = None
        self.val = 0
        self.waits = []
        self.known_after = None
        self.inc = inc
        self.own = None


class Prog:
    def __init__(self, n_dma_sems=12):
        self.ops = {e: [] for e in ENGS}
        self.all = []
        self.n_dma_sems = n_dma_sems

    def add(self, eng, fn, r=(), w=(), dma=False, inc=16):
        op = Op(eng, fn, dma, inc)
        op.id = len(self.all)
        deps = {}
        for b in r:
            if b.w is not None:
                deps[b.w.id] = b.w
        for b in w:
            if b.w is not None:
                deps[b.w.id] = b.w
            for x in b.r:
                deps[x.id] = x
        for b in r:
            b.r.append(op)
        for b in w:
            b.w = op
            b.r = []
        op.deps = list(deps.values())
        op.idx = len(self.ops[eng])
        self.ops[eng].append(op)
        self.all.append(op)
        return op

    def dma(self, queue, fn, r=(), w=(), inc=16):
        return self.add(queue, fn, r, w, dma=True, inc=inc)

    def barrier(self, bufs):
        pass

    def plan(self):
        known = {e: {} for e in ENGS}
        waited_dma = {e: set() for e in ENGS}
        ring = {e: [None] * self.n_dma_sems for e in ENGS}
        ring_pos = {e: 0 for e in ENGS}
        ring_val = {e: [0] * self.n_dma_sems for e in ENGS}
        for op in self.all:
            e = op.eng
            kn = known[e]
            waits = []
            changed = False
            for d in sorted(op.deps, key=lambda d: (not d.is_dma and d.eng == e)):
                if d.is_dma:
                    if d.id in waited_dma[e]:
                        continue
                    waited_dma[e].add(d.id)
                    waits.append(('dma', d))
                    if d.known_after:
                        for k, v in d.known_after.items():
                            if kn.get(k, -1) < v:
                                if not changed:
                                    kn = dict(kn)
                                    changed = True
                                kn[k] = v
                else:
                    if d.eng == e and e == 'pe' and not op.is_dma:
                        continue
                    if kn.get(d.eng, -1) >= d.idx:
                        continue
                    d.signal = True
                    waits.append(('cmp', d))
                    if not changed:
                        kn = dict(kn)
                        changed = True
                    kn[d.eng] = d.idx
                    if d.known_after:
                        for k, v in d.known_after.items():
                            if kn.get(k, -1) < v:
                                kn[k] = v
            if op.is_dma and op.own is not None:
                op.sem = op.own
                op.val = op.inc
            elif op.is_dma:
                pos = ring_pos[e]
                prev = ring[e][pos]
                if prev is not None and prev.id not in waited_dma[e]:
                    waited_dma[e].add(prev.id)
                    waits.append(('dma', prev))
                ring[e][pos] = op
                ring_val[e][pos] += op.inc
                op.sem = (e, pos)
                op.val = ring_val[e][pos]
                ring_pos[e] = (pos + 1) % self.n_dma_sems
            known[e] = kn
            op.waits = waits
            op.known_after = kn
        for e in ENGS:
            c = 0
            for op in self.ops[e]:
                if op.is_dma:
                    continue
                if op.signal:
                    c += 1
                    op.count = c

    def emit(self, nc, block, sems, dma_sems):
        engobj = {'pe': block.tensor, 'act': block.scalar, 'dve': block.vector,
                  'pool': block.gpsimd, 'sp': block.sync}
        for e in ENGS:
            ops = self.ops[e]
            if not ops:
                continue

            def body(eng, ops=ops, e=e):
                for op in ops:
                    wl = {}
                    for kind, d in op.waits:
                        if kind == 'dma':
                            s = dma_sems[d.sem]
                            v = d.val
                        else:
                            s = sems[d.eng]
                            v = d.count
                        key = id(s)
                        if key not in wl or wl[key][1] < v:
                            wl[key] = (s, v)
                    for s, v in wl.values():
                        eng.wait_ge(s, v)
                    ins = op.fn(eng)
                    if op.is_dma:
                        ins.then_inc(dma_sems[op.sem], op.inc)
                    elif op.signal:
                        ins.then_inc(sems[e], 1)
            engobj[e](body)

    def final_waits(self, ops):
        return ops

U8 = mybir.dt.uint8
ALPHA = (2.0 * 2) ** 0.25
LN_EPS = 1e-5
RMS_EPS = 1e-6
FF = 2816
LAM_INIT = [0.8 - 0.6 * math.exp(-0.3 * l) for l in range(2)]
RG = [[0, 1, 2, 3], [4, 5, 6, 7]]
MIXSUB = 9
NCORES = 8
PSUB = 99
GSEL = -1
DBGLIM = -1
MIXDBG = 0


def build(stage=99):
    nc = bass.Bass("TRN2", target_bir_lowering=False)

    def din(name, shape, dt=F32):
        return nc.dram_tensor(name, list(shape), dt, kind="ExternalInput").ap()

    def dout(name, shape, dt=F32):
        return nc.dram_tensor(name, list(shape), dt, kind="ExternalOutput").ap()

    xin = din("xin", [1536, 1024]); cvT_d = din("cvT", [128, 8, 3]); wmod = din("wmod", [2, 1024, 2304]); sel_d = din("sel", [128, 2])
    bmodT_d = din("bmodT", [128, 2, 18])
    w_in = din("w_in", [2, 1024, 2304]); w_out = din("w_out", [2, 1024, 1024]); w_f = din("w_f", [2, 256, 256])
    w1gu = din("w1gu", [2, 1024, 5632]); w1d = din("w1d", [2, 2816, 1024])
    w2gu = din("w2gu", [2, 1024, 5632]); w2d = din("w2d", [2, 2816, 1024])
    lnT_d = din("lnT", [128, 2, 2, 3, 8]); gqk_d = din("gqk", [128, 2, 6, 64]); gsubT_d = din("gsubT", [128, 2])
    lamv_d = din("lamv", [128, 2, 4, 64]); ropec_d = din("ropec", [128, 8, 64]); ropes_d = din("ropes", [128, 8, 64])
    cak = din("cak", [2, 256, 128]); cav = din("cav", [2, 256, 128]); cbk = din("cbk", [2, 256, 512]); cbv = din("cbv", [2, 256, 512])
    CLd = din("CL", [4096, 1024], BF16); SLd = din("SL", [4096, 1024], BF16)
    C256d = din("C256", [256, 256], BF16); S256d = din("S256", [256, 256], BF16); BCSd = din("BCS", [4, 128, 128], BF16)
    identf_d = din("identf", [128, 128]); identb_d = din("identb", [128, 128], BF16); onesb_d = din("onesb", [128, 128], BF16)
    y = dout("y", [1536, 1024]); nak = dout("nak", [2, 512, 128]); nav = dout("nav", [2, 512, 128])
    nbk = dout("nbk", [2, 512, 512]); nbv = dout("nbv", [2, 512, 512])
    gin_kt = [nc.dram_tensor("gin_kt%d" % h, [384, 1024], BF16).ap() for h in range(2)]
    gout_kt = [nc.dram_tensor("gout_kt%d" % h, [1536, 1024], BF16).ap() for h in range(2)]
    gin_vu = [nc.dram_tensor("gin_vu%d" % h, [512, 1024], BF16).ap() for h in range(2)]
    gout_vu = [nc.dram_tensor("gout_vu%d" % h, [2048, 1024], BF16).ap() for h in range(2)]

    gin_mod = nc.dram_tensor("gin_mod", [128, 108], F32).ap()
    gout_mod = nc.dram_tensor("gout_mod", [512, 108], F32).ap()
    P = Prog()
    dbg = {'on': False, 'n': 0}
    _orig_add = P.add

    def _add_dbg(eng, fn, r=(), w=(), dma=False, inc=16):
        if dbg['on'] and DBGLIM >= 0:
            dbg['n'] += 1
            if dbg['n'] > DBGLIM:
                return None
        return _orig_add(eng, fn, r, w, dma, inc)
    P.add = _add_dbg
    es = ExitStack()
    base0 = (nc.sbuf_base + 63) // 64 * 64
    avail = nc.sbuf_top - base0 - 64
    arena = es.enter_context(nc.sbuf_tensor("arena", [128, avail], U8))
    cur = [base0]
    lim = base0 + avail

    def esz(dt):
        return 4 if dt == F32 else 2

    def alloc(name, shape, dt, at=None):
        nb = int(np.prod(shape[1:])) * esz(dt)
        nb = (nb + 63) // 64 * 64
        if at is None:
            o = cur[0]
            cur[0] += nb
            assert cur[0] <= lim, (name, cur[0], lim)
        else:
            o = at
        t = nc.alloc_sbuf_tensor_at(name + "_%d" % o, list(shape), dt, offset=o)
        return t

    class Region:
        def __init__(self, size):
            self.base = cur[0]
            self.size = size
            cur[0] += size
            assert cur[0] <= lim, ("region", cur[0], lim)
            self.p = self.base
            self.n = 0

        def reset(self):
            self.p = self.base

        def alloc(self, name, shape, dt):
            nb = int(np.prod(shape[1:])) * esz(dt)
            nb = (nb + 63) // 64 * 64
            o = self.p
            self.p += nb
            assert self.p <= self.base + self.size, (name, self.p - self.base, self.size)
            self.n += 1
            return nc.alloc_sbuf_tensor_at("%s_r%d" % (name, self.n), list(shape), dt, offset=o)

    XRES = alloc("XRES", [128, 8, 1536], F32)
    HB = alloc("HB", [128, 8, 1536], BF16)
    WSr = Region(32768)
    WS = [WSr.alloc("WS%d" % i, [128, 8, 512], BF16) for i in range(4)]
    BIG = Region(56320)
    TMP = Region(20480)
    identf = alloc("identf", [128, 128], F32); identb = alloc("identb", [128, 128], BF16); onesb = alloc("onesb", [128, 128], BF16)
    modT = alloc("modT", [128, 2, 72, 2], F32); scT = alloc("scT", [128, 8, 3], BF16); cvT = alloc("cvT", [128, 8, 3], F32)
    modsl = alloc("modsl", [128, 108], F32); modall = alloc("modall", [128, 4, 108], F32); mtmp = alloc("mtmp", [128, 4, 18], F32); sel = alloc("sel", [128, 2], F32)
    bmodT = alloc("bmodT", [128, 2, 18], F32)
    lnT = alloc("lnT", [128, 2, 2, 3, 8], F32); lnA = alloc("lnA", [128, 2, 2, 3, 8], F32)
    gqk = alloc("gqk", [128, 2, 6, 64], F32); gsub = alloc("gsub", [128, 2], F32); neglam = alloc("neglam", [128, 2], F32)
    lamv = alloc("lamv", [128, 2, 4, 64], F32); lamt = alloc("lamt", [128, 4], F32); lamp = alloc("lamp", [128, 64], F32)
    ropec = alloc("ropec", [128, 8, 64], F32); ropes = alloc("ropes", [128, 8, 64], F32)
    KTc = alloc("KTc", [128, 6, 256], BF16); Vc = alloc("Vc", [128, 2, 768], BF16)
    C256 = alloc("C256", [128, 2, 256], BF16); S256 = alloc("S256", [128, 2, 256], BF16); BCS = alloc("BCS", [128, 4, 128], BF16)
    wf = alloc("wf", [128, 2, 256], BF16)
    epsln = alloc("epsln", [128, 1], F32); epsrms = alloc("epsrms", [128, 1], F32)
    ps = [nc.alloc_psum_tensor("ps%d" % i, [128, 512], F32) for i in range(8)]

    bps = [Buf("ps%d" % i) for i in range(8)]
    bx = [[Buf() for _ in range(3)] for _ in range(8)]
    bh = [[Buf() for _ in range(3)] for _ in range(8)]
    bws = [Buf() for _ in range(4)]
    bufs = {}

    def bf(name):
        if name not in bufs:
            bufs[name] = Buf(name)
        return bufs[name]

    def A(eng, fn, r=(), w=()):
        return P.add(eng, fn, r, w)

    def mm(out, lhsT, rhs, st, sp, r, w):
        return P.add('pe', lambda e: e.matmul(out, lhsT=lhsT, rhs=rhs, start=st, stop=sp), r, w)

    def tr(out, in_, ident, r, w):
        return P.add('pe', lambda e: e.transpose(out=out, in_=in_, identity=ident), r, w)

    def ld(q, dst, src, w, r=()):
        return P.dma(q, lambda e: e.dma_start(out=dst, in_=src), r=r, w=w)

    def fence():
        last = [P.ops[e][-1] for e in ENGS if P.ops[e]]
        for e in ('pe', 'act', 'dve', 'pool', 'sp'):
            op = P.add(e, lambda eng: eng.nop(), (), ())
            op.deps = list(last)
        return

    def tbs(tb):
        return slice(tb * 512, (tb + 1) * 512)

    def mod(l, i, c, v):
        return modT[:, l, i * 8 + c, v:v + 1]

    MIXT = HB
    outs = []
    wsc = [0]

    def next_ws():
        s = wsc[0] % 4
        wsc[0] += 1
        return s

    bankc = [0]
    unitc = [0]

    def next_bank():
        b = bankc[0] % 8
        bankc[0] += 1
        return b

    ld('sp', identf[:], identf_d[:, :], [bf('identf')]); ld('sp', identb[:], identb_d[:, :], [bf('identb')])
    ld('sp', onesb[:], onesb_d[:, :], [bf('onesb')]); ld('sp', cvT[:], cvT_d[:, :, :], [bf('cvT')])
    ld('sp', bmodT[:], bmodT_d[:, :, :], [bf('bmodT')]); ld('sp', sel[:], sel_d[:, :], [bf('sel')]); ld('sp', lnT[:], lnT_d[:, :, :, :, :], [bf('lnT')])
    ld('sp', gqk[:], gqk_d[:, :, :, :], [bf('gqk')]); ld('sp', gsub[:], gsubT_d[:, :], [bf('gsub')])
    ld('sp', lamv[:], lamv_d[:, :, :, :], [bf('lamv')]); ld('sp', ropec[:], ropec_d[:, :, :], [bf('rope')])
    ld('sp', ropes[:], ropes_d[:, :, :], [bf('rope')])
    ld('sp', C256[:], C256d.rearrange("(t p) n -> p t n", p=128), [bf('C256')])
    ld('sp', S256[:], S256d.rearrange("(t p) n -> p t n", p=128), [bf('S256')])
    ld('sp', BCS[:], BCSd.rearrange("k p n -> p k n"), [bf('BCS')])
    A('dve', lambda e: e.memset(epsln[:], LN_EPS), w=[bf('eps')])
    A('dve', lambda e: e.memset(epsrms[:], RMS_EPS), w=[bf('eps')])
    A('act', lambda e: e.activation(out=scT[:], in_=cvT[:], func=AF.Silu), r=[bf('cvT')], w=[bf('scT')])
    A('dve', lambda e: e.tensor_scalar(out=lnA[:], in0=lnT[:], scalar1=ALPHA, scalar2=None, op0=ALU.mult), r=[bf('lnT')], w=[bf('lnA')])
    for l in range(2):
        for j in range(2):
            A('dve', lambda e, l=l, j=j: e.tensor_tensor(out=lamp[:], in0=lamv[:, l, 2 * j, :], in1=lamv[:, l, 2 * j + 1, :], op=ALU.mult),
              r=[bf('lamv')], w=[bf('lamp')])
            A('dve', lambda e, l=l, j=j: e.tensor_reduce(out=lamt[:, 2 * l + j:2 * l + j + 1], in_=lamp[:], op=ALU.add, axis=AX.X),
              r=[bf('lamp')], w=[bf('lamt')])
    A('act', lambda e: e.activation(out=lamt[:], in_=lamt[:], func=AF.Exp), r=[bf('lamt')], w=[bf('lamt')])
    for l in range(2):
        A('dve', lambda e, l=l: e.tensor_tensor(out=neglam[:, l:l + 1], in0=lamt[:, 2 * l + 1:2 * l + 2], in1=lamt[:, 2 * l:2 * l + 1], op=ALU.subtract),
          r=[bf('lamt')], w=[bf('neglam')])
        A('dve', lambda e, l=l: e.tensor_scalar(out=neglam[:, l:l + 1], in0=neglam[:, l:l + 1], scalar1=-LAM_INIT[l], scalar2=None, op0=ALU.add),
          r=[bf('neglam')], w=[bf('neglam')])
        A('dve', lambda e, l=l: e.tensor_scalar(out=gsub[:, l:l + 1], in0=gsub[:, l:l + 1], scalar1=1.0 - LAM_INIT[l], scalar2=None, op0=ALU.mult),
          r=[bf('gsub')], w=[bf('gsub')])

    for l in range(2):
        wv = wmod[l].rearrange("(c p) n -> p c n", p=128)
        for (c0, ncol) in ((0, 512), (512, 512), (1024, 512), (1536, 512), (2048, 256)):
            s_ = next_ws()
            ld('pool', WS[s_][:, :, 0:ncol], wv[:, :, c0:c0 + ncol], [bws[s_]])
            for cc in range(ncol // 128):
                col = (l * 18 + c0 // 128 + cc) * 3
                for k in range(8):
                    mm(ps[0][:, col:col + 3], WS[s_][:, k, cc * 128:(cc + 1) * 128], scT[:, k, :], k == 0, k == 7,
                       [bws[s_], bf('scT')], [bps[0]])
    A('dve', lambda e: e.tensor_tensor(out=modsl[:].rearrange("p (m v) -> p m v", v=3), in0=ps[0][:, 0:108].rearrange("p (m v) -> p m v", v=3),
                                       in1=bmodT[:].rearrange("p l m -> p (l m)").unsqueeze(2).to_broadcast([128, 36, 3]), op=ALU.add),
      r=[bps[0], bf('bmodT')], w=[bf('modsl')])
    P.dma('sp', lambda e: e.dma_start(out=gin_mod[:, :], in_=modsl[:]), r=[bf('modsl')], w=[bf('gin_mod')])
    ccm = P.dma('pool', lambda e: e.collective_compute("AllGather", ALU.bypass, replica_groups=RG, ins=[gin_mod.opt()], outs=[gout_mod.opt()]),
                r=[bf('gin_mod')], w=[bf('gout_mod')], inc=1)
    ccm.own = ('cc', 8)
    P.add('pool', lambda e: e.nop(), [bf('gout_mod')], [bf('gout_mod')])
    ld('sp', modall[:], gout_mod.rearrange("(r p) n -> p r n", p=128), [bf('modall')], r=[bf('gout_mod')])
    modall5 = modall[:].rearrange("p r (l m v) -> p r l m v", l=2, m=18, v=3)
    for l in range(2):
        mv = [modT[:, l, :, vv].rearrange("p (r m) -> p r m", m=18) for vv in range(2)]
        A('dve', lambda e, l=l, mv=mv: e.tensor_copy(out=mv[0], in_=modall5[:, :, l, :, 0]), r=[bf('modall')], w=[bf('mod')])
        A('dve', lambda e, l=l: e.tensor_scalar(out=mtmp[:], in0=modall5[:, :, l, :, 1], scalar1=sel[:, 0:1], scalar2=None, op0=ALU.mult),
          r=[bf('modall'), bf('sel')], w=[bf('mtmp')])
        A('dve', lambda e, l=l, mv=mv: e.scalar_tensor_tensor(out=mv[1], in0=modall5[:, :, l, :, 2], scalar=sel[:, 1:2], in1=mtmp[:], op0=ALU.mult, op1=ALU.add),
          r=[bf('modall'), bf('sel'), bf('mtmp')], w=[bf('mod')])
        for i in (1, 4, 7):
            A('dve', lambda e, l=l, i=i: e.tensor_scalar(out=modT[:, l, i * 8:(i + 1) * 8, :], in0=modT[:, l, i * 8:(i + 1) * 8, :],
                                                         scalar1=1.0, scalar2=1.0 / ALPHA, op0=ALU.add, op1=ALU.mult),
              r=[bf('mod')], w=[bf('mod')])
        for i in (2, 8):
            A('dve', lambda e, l=l, i=i: e.tensor_scalar(out=modT[:, l, i * 8:(i + 1) * 8, :], in0=modT[:, l, i * 8:(i + 1) * 8, :],
                                                         scalar1=0.5, scalar2=None, op0=ALU.mult),
              r=[bf('mod')], w=[bf('mod')])

    TMP.reset()
    xstg = [TMP.alloc("xstg", [128, 1024], F32) for _ in range(2)]
    for t in range(12):
        st = xstg[t % 2]
        ld('sp', st[:], xin[t * 128:(t + 1) * 128, :], [bf('xstg%d' % (t % 2))])
        tb = t // 4
        for half in range(2):
            b = next_bank()
            for cc in range(4):
                c = half * 4 + cc
                tr(ps[b][:, cc * 128:(cc + 1) * 128], st[:, c * 128:(c + 1) * 128], identf[:],
                   [bf('xstg%d' % (t % 2)), bf('identf')], [bps[b]])
            dst = XRES[:, half * 4:half * 4 + 4, t * 128:(t + 1) * 128]
            src = ps[b][:].rearrange("p (c t) -> p c t", t=128)
            wl = [bx[half * 4 + cc][tb] for cc in range(4)]
            if half == 0:
                A('act', lambda e, dst=dst, src=src: e.activation(out=dst, in_=src, func=AF.Copy, scale=ALPHA), r=[bps[b]], w=wl)
            else:
                A('dve', lambda e, dst=dst, src=src: e.tensor_scalar(out=dst, in0=src, scalar1=ALPHA, scalar2=None, op0=ALU.mult), r=[bps[b]], w=wl)

    def layernorm(l, i, final):
        TMP.reset()
        mean_t = TMP.alloc("mean", [128, 512], F32); t1 = TMP.alloc("t1", [128, 512], F32)
        rstd_t = TMP.alloc("rstd", [128, 512], F32)
        tm2 = [TMP.alloc("tm2", [128, 512], F32) for _ in range(2)]
        ZSQ = TMP.alloc("zsq", [128, 8, 512], BF16)
        lnp = lnT if final else lnA
        for tb in range(3):
            for c in range(8):
                A('act', lambda e, c=c, tb=tb: e.activation(out=HB[:, c, tbs(tb)], in_=XRES[:, c, tbs(tb)], func=AF.Copy),
                  r=[bx[c][tb]], w=[bh[c][tb]])
                A('act', lambda e, c=c, tb=tb: e.activation(out=ZSQ[:, c, :], in_=XRES[:, c, tbs(tb)], func=AF.Square),
                  r=[bx[c][tb]], w=[bf('zsq%d' % c)])
            b_s = next_bank(); b_q = next_bank()
            for c in range(8):
                mm(ps[b_s][:], onesb[:], HB[:, c, tbs(tb)], c == 0, c == 7, [bh[c][tb], bf('onesb')], [bps[b_s]])
            for c in range(8):
                mm(ps[b_q][:], onesb[:], ZSQ[:, c, :], c == 0, c == 7, [bf('zsq%d' % c), bf('onesb')], [bps[b_q]])
            A('act', lambda e, b_s=b_s: e.activation(out=mean_t[:], in_=ps[b_s][:], func=AF.Copy, scale=1.0 / 1024), r=[bps[b_s]], w=[bf('mean')])
            A('dve', lambda e: e.tensor_tensor(out=t1[:], in0=mean_t[:], in1=mean_t[:], op=ALU.mult), r=[bf('mean')], w=[bf('t1')])
            A('dve', lambda e, b_q=b_q: e.scalar_tensor_tensor(out=t1[:], in0=ps[b_q][:], scalar=1.0 / 1024, in1=t1[:], op0=ALU.mult, op1=ALU.subtract),
              r=[bps[b_q], bf('t1')], w=[bf('t1')])
            A('act', lambda e: e.activation(out=t1[:], in_=t1[:], func=AF.Sqrt, bias=epsln[:, 0:1], scale=1.0), r=[bf('t1'), bf('eps')], w=[bf('t1')])
            A('dve', lambda e: e.reciprocal(out=rstd_t[:], in_=t1[:]), r=[bf('t1')], w=[bf('rstd')])
            for c in range(8):
                tm = tm2[c % 2]; btm = bf('tm2%d' % (c % 2))
                A('pool', lambda e, c=c, tb=tb, tm=tm: e.tensor_tensor(out=tm[:], in0=XRES[:, c, tbs(tb)], in1=mean_t[:], op=ALU.subtract),
                  r=[bx[c][tb], bf('mean')], w=[btm])
                A('dve', lambda e, tm=tm: e.tensor_tensor(out=tm[:], in0=tm[:], in1=rstd_t[:], op=ALU.mult), r=[btm, bf('rstd')], w=[btm])
                A('act', lambda e, c=c, tb=tb, tm=tm: e.activation(out=XRES[:, c, tbs(tb)], in_=tm[:], func=AF.Identity,
                                                                   scale=lnp[:, 0, l, i, c:c + 1], bias=lnp[:, 1, l, i, c:c + 1]),
                  r=[btm, bf('lnT'), bf('lnA')], w=[bx[c][tb]])

    def modulate(l, si):
        for tb in range(3):
            v = 0 if tb == 0 else 1
            for c in range(8):
                A('act', lambda e, c=c, tb=tb, v=v: e.activation(out=HB[:, c, tbs(tb)], in_=XRES[:, c, tbs(tb)], func=AF.Identity,
                                                                 scale=mod(l, 3 * si + 1, c, v), bias=mod(l, 3 * si, c, v)),
                  r=[bx[c][tb], bf('mod')], w=[bh[c][tb]])

    FGROUPS = ((0, 4), (4, 4), (8, 3))
    pf = {}

    def ffn_prefetch(l, si, wgu, wd):
        BIG.reset()
        AH = BIG.alloc("AH", [128, 11, 1536], BF16); WD = BIG.alloc("WD", [128, 11, 1024], BF16)
        bwd = [bf('wd%d' % g) for g in range(3)]
        wguv = wgu[l].rearrange("(c p) n -> p c n", p=128)
        wdv = wd[l].rearrange("(j p) n -> p j n", p=128)
        slots = []
        for gi, (a, n) in enumerate(FGROUPS[:2]):
            sg_ = next_ws(); su_ = next_ws()
            col0 = a * 128; ncol = n * 128
            ld('pool', WS[sg_][:, :, 0:ncol], wguv[:, :, col0:col0 + ncol], [bws[sg_]])
            ld('pool', WS[su_][:, :, 0:ncol], wguv[:, :, FF + col0:FF + col0 + ncol], [bws[su_]])
            slots.append((sg_, su_))
        for gi, (a, n) in enumerate(FGROUPS):
            ld('pool', WD[:, a:a + n, :], wdv[:, a:a + n, :], [bwd[gi]])
        pf[(l, si)] = (AH, WD, slots)

    def mix_prefetch(l):
        winv = w_in[l].rearrange("(c p) n -> p c n", p=128)
        for g in range(4):
            ld('pool', WS[g][:], winv[:, :, g * 512:(g + 1) * 512], [bws[g]])
        wsc[0] = 0
        pf[('mix', l)] = True

    def ffn(l, si, wgu, wd, ln_i, final, after=None):
        modulate(l, si)
        TMP.reset()
        pre = pf.pop((l, si), None)
        if pre is None:
            BIG.reset()
            AH = BIG.alloc("AH", [128, 11, 1536], BF16); WD = BIG.alloc("WD", [128, 11, 1024], BF16)
            pslots = []
        else:
            AH, WD, pslots = pre
        sgt = [TMP.alloc("sg", [128, 512], F32) for _ in range(4)]
        ba = [[bf('ah%d_%d' % (j, tb)) for tb in range(3)] for j in range(11)]
        bwd = [bf('wd%d' % g) for g in range(3)]
        groups = FGROUPS
        wguv = wgu[l].rearrange("(c p) n -> p c n", p=128)
        wdv = wd[l].rearrange("(j p) n -> p j n", p=128)
        prc = 0
        for half in range(2):
            j0 = half * 11
            if not (half == 0 and pre is not None):
                for gi, (a, n) in enumerate(groups):
                    ld('pool', WD[:, a:a + n, :], wdv[:, j0 + a:j0 + a + n, :], [bwd[gi]])
            for gi, (a, n) in enumerate(groups):
                col0 = (j0 + a) * 128; ncol = n * 128
                if half == 0 and gi < len(pslots):
                    sg_, su_ = pslots[gi]
                else:
                    sg_ = next_ws(); su_ = next_ws()
                    ld('pool', WS[sg_][:, :, 0:ncol], wguv[:, :, col0:col0 + ncol], [bws[sg_]])
                    ld('pool', WS[su_][:, :, 0:ncol], wguv[:, :, FF + col0:FF + col0 + ncol], [bws[su_]])
                for jj in range(n):
                    j = a + jj
                    for tb in range(3):
                        pr = prc % 4; prc += 1
                        gps = ps[2 * pr]; ups = ps[2 * pr + 1]
                        for k in range(8):
                            mm(gps[:], WS[sg_][:, k, jj * 128:(jj + 1) * 128], HB[:, k, tbs(tb)], k == 0, k == 7, [bws[sg_], bh[k][tb]], [bps[2 * pr]])
                        for k in range(8):
                            mm(ups[:], WS[su_][:, k, jj * 128:(jj + 1) * 128], HB[:, k, tbs(tb)], k == 0, k == 7, [bws[su_], bh[k][tb]], [bps[2 * pr + 1]])
                        A('act', lambda e, pr=pr, gps=gps: e.activation(out=sgt[pr][:], in_=gps[:], func=AF.Silu), r=[bps[2 * pr]], w=[bf('sg%d' % pr)])
                        A('dve', lambda e, pr=pr, ups=ups, j=j, tb=tb: e.tensor_tensor(out=AH[:, j, tbs(tb)], in0=sgt[pr][:], in1=ups[:], op=ALU.mult),
                          r=[bf('sg%d' % pr), bps[2 * pr + 1]], w=[ba[j][tb]])
            for tb in range(3):
                v = 0 if tb == 0 else 1
                for m in range(8):
                    b = next_bank()
                    for j in range(11):
                        mm(ps[b][:], WD[:, j, m * 128:(m + 1) * 128], AH[:, j, tbs(tb)], j == 0, j == 10, [bwd[min(j // 4, 2)], ba[j][tb]], [bps[b]])
                    A('dve', lambda e, b=b, m=m, tb=tb, v=v: e.scalar_tensor_tensor(out=XRES[:, m, tbs(tb)], in0=ps[b][:], scalar=mod(l, 3 * si + 2, m, v),
                                                                                 in1=XRES[:, m, tbs(tb)], op0=ALU.mult, op1=ALU.add),
                      r=[bps[b], bx[m][tb], bf('mod')], w=[bx[m][tb]])
        if after is not None:
            after()
        layernorm(l, ln_i, final)
        fence()

    def store_y():
        TMP.reset()
        ystg = [TMP.alloc("ystg", [128, 1024], F32) for _ in range(2)]
        for t in range(12):
            tb = t // 4
            st = ystg[t % 2]; bst = bf('ystg%d' % (t % 2))
            for half in range(2):
                b = next_bank()
                for cc in range(4):
                    c = half * 4 + cc
                    tr(ps[b][:, cc * 128:(cc + 1) * 128], XRES[:, c, t * 128:(t + 1) * 128], identf[:], [bx[c][tb], bf('identf')], [bps[b]])
                if half == 0:
                    A('act', lambda e, b=b, st=st: e.activation(out=st[:, 0:512], in_=ps[b][:], func=AF.Copy), r=[bps[b]], w=[bst])
                else:
                    A('dve', lambda e, b=b, st=st: e.tensor_copy(out=st[:, 512:1024], in_=ps[b][:]), r=[bps[b]], w=[bst])
            outs.append(P.dma('sp', lambda e, st=st, t=t: e.dma_start(out=y[t * 128:(t + 1) * 128, :], in_=st[:]), r=[bst]))

    def rope(src3, dst3, H, ts, rset, rbufs, wbufs):
        r1, r2, n1, n2 = rset
        cosb = ropec[:, ts, :].unsqueeze(1).to_broadcast([128, H, 64])
        ssv = ropes[:, ts, :].rearrange("p (a b i) -> p a b i", a=2, b=2)
        r1v = r1[:, 0:H * 64].rearrange("p (h d) -> p h d", d=64)
        r2v = r2[:, 0:H * 64].rearrange("p (h a b i) -> p h a b i", a=2, b=2, i=16)
        s5 = src3.rearrange("p h (a b i) -> p h a b i", a=2, b=2)
        A('dve', lambda e: e.tensor_tensor(out=r1v, in0=src3, in1=cosb, op=ALU.mult), r=rbufs + [bf('rope')], w=[bf(n1)])
        for blk in range(2):
            A('dve', lambda e, blk=blk: e.tensor_tensor(out=r2v[:, :, :, blk, :], in0=s5[:, :, :, 1 - blk, :],
                                                        in1=ssv[:, :, blk, :].unsqueeze(1).to_broadcast([128, H, 2, 16]), op=ALU.mult),
              r=rbufs + [bf('rope')], w=[bf(n2)])
        A('pool', lambda e: e.tensor_tensor(out=dst3, in0=r1v, in1=r2[:, 0:H * 64].rearrange("p (h d) -> p h d", d=64), op=ALU.add),
          r=[bf(n1), bf(n2)], w=wbufs)

    def attn_unit(l, kind, segs, oc, tok0, T, qz_eng='pool', pool_share=False):
        pT, rec, t1, t2, sq, Qzs, accP = T
        S = [0, 1, 2, 3]
        acc = [(4, 5), (6, 7)]
        nq = sum(sg[5] for sg in segs)
        u = unitc[0] % 2
        unitc[0] += 1
        Qz = Qzs[u]
        for (KT, V, nkt, Qc, col0, ncols) in segs:
            for g in range(2):
                if qz_eng == 'pool':
                    A('pool', lambda e, g=g, Qc=Qc, col0=col0, ncols=ncols: e.tensor_copy(out=Qz[g][g * 64:(g + 1) * 64, col0:col0 + ncols], in_=Qc[g * 64:(g + 1) * 64, :]),
                      r=[bf('QT'), bf('qzero')], w=[bf('Qz%d_%d' % (u, g))])
                else:
                    A('act', lambda e, g=g, Qc=Qc, col0=col0, ncols=ncols: e.activation(out=Qz[g][g * 64:(g + 1) * 64, col0:col0 + ncols], in_=Qc[g * 64:(g + 1) * 64, :], func=AF.Copy),
                      r=[bf('QT'), bf('qzero')], w=[bf('Qz%d_%d' % (u, g))])
        its = [(si, kt, g) for si, sg in enumerate(segs) for kt in range(sg[2]) for g in range(2)]
        LOOK = 3

        def emit_s(i):
            si, kt, g = its[i]
            KT, V, nkt, Qc, col0, ncols = segs[si]
            kap, kb_ = KT(kt)
            sb = S[i % 4]; pi = i % len(pT)
            mm(ps[sb][:, 0:ncols], kap, Qz[g][:, col0:col0 + ncols], True, True, kb_ + [bf('Qz%d_%d' % (u, g))], [bps[sb]])
            A('act', lambda e, sb=sb, pi=pi, ncols=ncols: e.activation(out=pT[pi][:, 0:ncols], in_=ps[sb][:, 0:ncols], func=AF.Exp, scale=0.125),
              r=[bps[sb]], w=[bf('pT%d' % pi)])

        def emit_pv(i):
            si, kt, g = its[i]
            KT, V, nkt, Qc, col0, ncols = segs[si]
            vap, vb_ = V(kt)
            pi = i % len(pT)
            ob, db = acc[g]
            mm(ps[ob][:, col0:col0 + ncols], vap, pT[pi][:, 0:ncols], kt == 0, kt == nkt - 1, vb_ + [bf('pT%d' % pi)], [bps[ob]])
            if pool_share and g == 1 and kt % 2 == 1:
                if kt == 1:
                    A('pool', lambda e, pi=pi, col0=col0, ncols=ncols: e.tensor_copy(out=accP[:, col0:col0 + ncols], in_=pT[pi][:, 0:ncols]),
                      r=[bf('pT%d' % pi)], w=[bf('accP')])
                else:
                    A('pool', lambda e, pi=pi, col0=col0, ncols=ncols: e.tensor_tensor(out=accP[:, col0:col0 + ncols], in0=accP[:, col0:col0 + ncols],
                                                                                   in1=pT[pi][:, 0:ncols], op=ALU.add),
                      r=[bf('pT%d' % pi)], w=[bf('accP')])
            elif kt == 0:
                A('dve', lambda e, db=db, pi=pi, col0=col0, ncols=ncols: e.tensor_copy(out=ps[db][:, col0:col0 + ncols], in_=pT[pi][:, 0:ncols]),
                  r=[bf('pT%d' % pi)], w=[bps[db]])
            else:
                A('dve', lambda e, db=db, pi=pi, col0=col0, ncols=ncols: e.tensor_tensor(out=ps[db][:, col0:col0 + ncols], in0=ps[db][:, col0:col0 + ncols],
                                                                                         in1=pT[pi][:, 0:ncols], op=ALU.add),
                  r=[bf('pT%d' % pi)], w=[bps[db]])
        for i in range(min(LOOK, len(its))):
            emit_s(i)
        for i in range(len(its)):
            emit_pv(i)
            if i + LOOK < len(its):
                emit_s(i + LOOK)
        dst = MIXT[:, oc, tok0:tok0 + nq]
        DB = [1, 2]
        if pool_share:
            A('dve', lambda e: e.tensor_tensor(out=ps[acc[1][1]][:, 0:nq], in0=ps[acc[1][1]][:, 0:nq], in1=accP[:, 0:nq], op=ALU.add),
              r=[bf('accP')], w=[bps[acc[1][1]]])
        for g in range(2):
            db = acc[g][1]
            A('act', lambda e, g=g, db=db: e.activation(out=pT[g][:, 0:nq], in_=ps[db][:, 0:nq], func=AF.Copy), r=[bps[db]], w=[bf('pT%d' % g)])
            mm(ps[DB[g]][:, 0:nq], onesb[:], pT[g][:, 0:nq], True, True, [bf('onesb'), bf('pT%d' % g)], [bps[DB[g]]])
        if kind == 'A':
            for g in range(2):
                ob, db = acc[g]
                sl = slice(g * 64, (g + 1) * 64)
                A('dve', lambda e, g=g, sl=sl: e.reciprocal(out=rec[sl, 0:nq], in_=ps[DB[g]][sl, 0:nq]), r=[bps[DB[g]]], w=[bf('rec%d' % g)])
                A('dve', lambda e, ob=ob, sl=sl: e.tensor_tensor(out=MIXT[sl, oc, tok0:tok0 + nq], in0=ps[ob][sl, 0:nq], in1=rec[sl, 0:nq], op=ALU.mult),
                  r=[bps[ob], bf('rec%d' % g)], w=[bf('MIXT%d_%d' % (oc, tok0))])
        else:
            A('dve', lambda e: e.reciprocal(out=rec[:, 0:nq], in_=ps[1][:, 0:nq]), r=[bps[1]], w=[bf('rec0')])
            A('dve', lambda e: e.tensor_tensor(out=t1[:, 0:nq], in0=ps[4][:, 0:nq], in1=rec[:, 0:nq], op=ALU.mult), r=[bps[4], bf('rec0')], w=[bf('at1')])
            A('dve', lambda e: e.reciprocal(out=rec[:, 0:nq], in_=ps[2][:, 0:nq]), r=[bps[2], bf('at1')], w=[bf('rec0')])
            A('dve', lambda e: e.tensor_tensor(out=t2[:, 0:nq], in0=ps[6][:, 0:nq], in1=rec[:, 0:nq], op=ALU.mult), r=[bps[6], bf('rec0')], w=[bf('at2')])
            A('dve', lambda e: e.scalar_tensor_tensor(out=t2[:, 0:nq], in0=t2[:, 0:nq], scalar=neglam[:, l:l + 1], in1=t1[:, 0:nq], op0=ALU.mult, op1=ALU.add),
              r=[bf('at1'), bf('at2'), bf('neglam')], w=[bf('at2')])
            A('act', lambda e: e.activation(out=sq[:, 0:nq], in_=t2[:, 0:nq], func=AF.Square), r=[bf('at2')], w=[bf('asq')])
            mm(ps[0][:, 0:nq], onesb[:], sq[:, 0:nq], True, True, [bf('onesb'), bf('asq')], [bps[0]])
            A('act', lambda e: e.activation(out=t1[:, 0:nq], in_=ps[0][:, 0:nq], func=AF.Sqrt, bias=epsrms[:, 0:1], scale=1.0 / 128),
              r=[bps[0], bf('eps')], w=[bf('at1')])
            A('dve', lambda e: e.reciprocal(out=rec[:, 0:nq], in_=t1[:, 0:nq]), r=[bf('at1')], w=[bf('rec0')])
            A('dve', lambda e: e.tensor_tensor(out=t2[:, 0:nq], in0=t2[:, 0:nq], in1=rec[:, 0:nq], op=ALU.mult), r=[bf('rec0'), bf('at2')], w=[bf('at2')])
            A('act', lambda e: e.activation(out=dst, in_=t2[:, 0:nq], func=AF.Identity, scale=gsub[:, l:l + 1]), r=[bf('at2'), bf('gsub')], w=[bf('MIXT%d_%d' % (oc, tok0))])

    def mixbufs(k, tb):
        if k >= 6:
            return [bf('MIXF%d_%d' % (k - 6, tb))]
        return [bf('MIXT%d_%d' % (k, tb * 512))]

    def mix(l):
        modulate(l, 1)
        BIG.reset(); TMP.reset()
        QT = BIG.alloc("QT", [128, 6, 1536], BF16)
        WOUT = BIG.alloc("WOUT", [128, 8, 1024], BF16)
        WX = BIG.alloc("WX", [128, 8, 256], BF16)
        KTL = BIG.alloc("KTL", [128, 6, 512], BF16)
        VL = BIG.alloc("VL", [128, 4, 1024], BF16)
        tsq = TMP.alloc("tsq", [128, 384], F32); tn = tsq
        bufs['tn'] = bf('tsq')
        ss6 = TMP.alloc("ss6", [128, 6], F32); sd6 = TMP.alloc("sd6", [128, 6], F32); rs6 = TMP.alloc("rs6", [128, 6], F32)
        r1 = TMP.alloc("r1", [128, 512], F32); r2 = TMP.alloc("r2", [128, 512], F32)
        qk_tok = TMP.alloc("qk_tok", [128, 1536], F32); vu_tok = TMP.alloc("vu_tok", [128, 1024], BF16)
        kt_stage = TMP.alloc("kt_stage", [128, 6, 128], BF16)
        f32a = TMP.alloc("f32a", [128, 512], F32); f32b = f32a; vtmp = TMP.alloc("vtmp", [128, 128], F32)
        r2b = TMP.alloc("r2b", [128, 512], F32)
        rsets = [(r1, r2, 'r1', 'r2'), (f32a, r2b, 'f32a', 'r2b')]
        rcnt = [0]

        def nr():
            rcnt[0] += 1
            return rsets[rcnt[0] % 2]

        winv = w_in[l].rearrange("(c p) n -> p c n", p=128)
        if not pf.pop(('mix', l), False):
            for g in range(4):
                ld('pool', WS[g][:], winv[:, :, g * 512:(g + 1) * 512], [bws[g]])
            wsc[0] = 0
        ld('pool', WX[:], winv[:, :, 2048:2304], [bf('WX')])
        cktf = BIG.alloc("cktf", [128, 768], F32)
        cavv = cav[l].rearrange("(t p) n -> p t n", p=128)
        for kvh in range(2):
            for dup in range(2):
                o = kvh * 128 + dup * 64
                ld('pool', Vc[:, :, o:o + 64], cavv[:, :, kvh * 64:(kvh + 1) * 64], [bf('Vc')])
        ld('pool', Vc[:, :, 256:768], cbv[l].rearrange("(t p) n -> p t n", p=128), [bf('Vc')])
        for t in range(2):
            for kvh in range(2):
                for dup in range(2):
                    o = kvh * 128 + dup * 64
                    ld('pool', cktf[:, o:o + 64], cak[l, t * 128:(t + 1) * 128, kvh * 64:(kvh + 1) * 64], [bf('ckt')])
            ld('pool', cktf[:, 256:768], cbk[l, t * 128:(t + 1) * 128, :], [bf('ckt')])
            for half in range(2):
                b = next_bank()
                nchunk = 4 if half == 0 else 2
                for cc in range(nchunk):
                    c = half * 4 + cc
                    tr(ps[b][:, cc * 128:(cc + 1) * 128], cktf[:, c * 128:(c + 1) * 128], identf[:], [bf('ckt'), bf('identf')], [bps[b]])
                A('dve', lambda e, b=b, t=t, half=half, nchunk=nchunk: e.tensor_copy(
                    out=KTc[:, half * 4:half * 4 + nchunk, t * 128:(t + 1) * 128],
                    in_=ps[b][:, 0:nchunk * 128].rearrange("p (c t) -> p c t", t=128)), r=[bps[b]], w=[bf('KTc')])
        ld('pool', WOUT[:], w_out[l].rearrange("(c p) n -> p c n", p=128), [bf('WOUT')])
        ld('pool', wf[:], w_f[l].rearrange("(c p) n -> p c n", p=128), [bf('wf')])

        BMAP = {0: 0, 3: 1, 4: 2, 1: 3, 2: 4}
        tn3 = tn[:].rearrange("p (h d) -> p h d", d=64)
        qa_dst = qk_tok[:, 0:256].rearrange("p (h d) -> p h d", d=64)
        ka_dst = qk_tok[:, 768:1024].rearrange("p (k u d) -> p k u d", k=2, u=2)
        qb_dst = qk_tok[:, 256:768]; kb_dst = qk_tok[:, 1024:1536]

        def proj_mm(t, groups):
            tb = t // 4
            tsl = slice(t * 128, (t + 1) * 128)
            for g in groups:
                b = BMAP[g]
                ncol = 512 if g < 4 else 256
                for k in range(8):
                    rhs = WS[g][:, k, :] if g < 4 else WX[:, k, :]
                    mm(ps[b][:, 0:ncol], HB[:, k, tsl], rhs, k == 0, k == 7, [bh[k][tb], bws[g] if g < 4 else bf('WX')], [bps[b]])

        def post_a(t):
            prompt = t < 4
            b0, b3, b4 = BMAP[0], BMAP[3], BMAP[4]
            vdst = VL[:, t, :] if prompt else vu_tok[:]
            bvd = bf('VL') if prompt else bf('vu_tok')
            A('act', lambda e: e.activation(out=tsq[:], in_=ps[b0][:, 0:384], func=AF.Square), r=[bps[b0]], w=[bf('tsq')])
            A('dve', lambda e: e.tensor_reduce(out=ss6[:], in_=tsq[:].rearrange("p (h d) -> p h d", d=64), op=ALU.add, axis=AX.X), r=[bf('tsq')], w=[bf('ss6')])
            A('act', lambda e: e.activation(out=sd6[:], in_=ss6[:], func=AF.Sqrt, bias=epsrms[:, 0:1], scale=1.0 / 64), r=[bf('ss6'), bf('eps')], w=[bf('sd6')])
            A('dve', lambda e: e.reciprocal(out=rs6[:], in_=sd6[:]), r=[bf('sd6')], w=[bf('rs6')])
            A('dve', lambda e: e.tensor_tensor(out=tn3, in0=ps[b0][:, 0:384].rearrange("p (h d) -> p h d", d=64),
                                               in1=rs6[:].unsqueeze(2).to_broadcast([128, 6, 64]), op=ALU.mult), r=[bps[b0], bf('rs6')], w=[bf('tn')])
            A('dve', lambda e: e.tensor_tensor(out=tn3, in0=tn3, in1=gqk[:, l], op=ALU.mult), r=[bf('tn'), bf('gqk')], w=[bf('tn')])
            if prompt:
                outs.append(P.dma('sp', lambda e, t=t: e.dma_start(out=nak[l, t * 128:(t + 1) * 128, :], in_=tn[:, 256:384]), r=[bf('tn')]))
                A('act', lambda e: e.activation(out=vtmp[:], in_=ps[b0][:, 384:512], func=AF.Copy), r=[bps[b0]], w=[bf('vtmp')])
                outs.append(P.dma('sp', lambda e, t=t: e.dma_start(out=nav[l, t * 128:(t + 1) * 128, :], in_=vtmp[:]), r=[bf('vtmp')]))
                A('pool', lambda e: e.tensor_copy(out=qa_dst, in_=tn3[:, 0:4, :]), r=[bf('tn')], w=[bf('qk_tok')])
                for u in range(2):
                    A('pool', lambda e, u=u: e.tensor_copy(out=ka_dst[:, :, u, :], in_=tn3[:, 4:6, :]), r=[bf('tn')], w=[bf('qk_tok')])
            else:
                rope(tn3[:, 0:4, :], qa_dst, 4, t - 4, nr(), [bf('tn')], [bf('qk_tok')])
                for u in range(2):
                    rope(tn3[:, 4:6, :], ka_dst[:, :, u, :], 2, t - 4, nr(), [bf('tn')], [bf('qk_tok')])
            va_dst = vdst[:, 0:256].rearrange("p (k u d) -> p k u d", k=2, u=2)
            A('dve', lambda e, va_dst=va_dst: e.tensor_copy(
                out=va_dst, in_=ps[b0][:, 384:512].rearrange("p (k d) -> p k d", d=64).unsqueeze(2).to_broadcast([128, 2, 2, 64])),
              r=[bps[b0]], w=[bvd])
            if prompt:
                A('act', lambda e: e.activation(out=f32b[:], in_=ps[b3][:], func=AF.Copy), r=[bps[b3]], w=[bf('f32a')])
                outs.append(P.dma('sp', lambda e, t=t: e.dma_start(out=nbv[l, t * 128:(t + 1) * 128, :], in_=f32b[:]), r=[bf('f32a')]))
                A('pool', lambda e, vdst=vdst: e.tensor_copy(out=vdst[:, 256:768], in_=f32b[:]), r=[bf('f32a')], w=[bvd])
            else:
                A('act', lambda e, vdst=vdst: e.activation(out=vdst[:, 256:768], in_=ps[b3][:], func=AF.Copy), r=[bps[b3]], w=[bvd])
            A('dve', lambda e, vdst=vdst: e.tensor_copy(out=vdst[:, 768:1024], in_=ps[b4][:, 0:256]), r=[bps[b4]], w=[bvd])

        def post_b(t):
            prompt = t < 4
            b1, b2 = BMAP[1], BMAP[2]
            if prompt:
                A('act', lambda e: e.activation(out=qb_dst, in_=ps[b1][:], func=AF.Copy), r=[bps[b1]], w=[bf('qk_tok')])
                A('act', lambda e: e.activation(out=f32a[:], in_=ps[b2][:], func=AF.Copy), r=[bps[b2]], w=[bf('f32a')])
                outs.append(P.dma('sp', lambda e, t=t: e.dma_start(out=nbk[l, t * 128:(t + 1) * 128, :], in_=f32a[:]), r=[bf('f32a')]))
                A('pool', lambda e: e.tensor_copy(out=kb_dst, in_=f32a[:]), r=[bf('f32a')], w=[bf('qk_tok')])
            else:
                rope(ps[b1][:].rearrange("p (h d) -> p h d", d=64), qb_dst.rearrange("p (h d) -> p h d", d=64), 8, t - 4, nr(), [bps[b1]], [bf('qk_tok')])
                rope(ps[b2][:].rearrange("p (h d) -> p h d", d=64), kb_dst.rearrange("p (h d) -> p h d", d=64), 8, t - 4, nr(), [bps[b2]], [bf('qk_tok')])

        def trans(t):
            prompt = t < 4
            tsl = slice(t * 128, (t + 1) * 128)
            for grp in range(3):
                b = 5 + grp
                for cc in range(4):
                    c = grp * 4 + cc
                    tr(ps[b][:, cc * 128:(cc + 1) * 128], qk_tok[:, c * 128:(c + 1) * 128], identf[:], [bf('qk_tok'), bf('identf')], [bps[b]])
                src = ps[b][:].rearrange("p (c t) -> p c t", t=128)
                if grp == 0:
                    A('act', lambda e, src=src, tsl=tsl: e.activation(out=QT[:, 0:4, tsl], in_=src, func=AF.Copy), r=[bps[b]], w=[bf('QT')])
                elif grp == 1:
                    A('dve', lambda e, src=src, tsl=tsl: e.tensor_copy(out=QT[:, 4:6, tsl], in_=src[:, 0:2, :]), r=[bps[b]], w=[bf('QT')])
                    kd = KTL[:, 0:2, tsl] if prompt else kt_stage[:, 0:2, :]
                    A('dve', lambda e, src=src, kd=kd: e.tensor_copy(out=kd, in_=src[:, 2:4, :]), r=[bps[b]], w=[bf('KTL') if prompt else bf('kt_stage')])
                else:
                    kd = KTL[:, 2:6, tsl] if prompt else kt_stage[:, 2:6, :]
                    A('act', lambda e, src=src, kd=kd: e.activation(out=kd, in_=src, func=AF.Copy), r=[bps[b]], w=[bf('KTL') if prompt else bf('kt_stage')])
            if not prompt:
                ts = t - 4
                for h in range(2):
                    P.dma('sp', lambda e, ts=ts, h=h: e.dma_start(out=gin_kt[h].rearrange("(c p) n -> p c n", p=128)[:, :, ts * 128:(ts + 1) * 128],
                                                                  in_=kt_stage[:, 3 * h:3 * h + 3, :]),
                          r=[bf('kt_stage')], w=[bf('gin_kt%d' % h)])
                hh, t4 = ts // 4, ts % 4
                P.dma('sp', lambda e, hh=hh, t4=t4: e.dma_start(out=gin_vu[hh][t4 * 128:(t4 + 1) * 128, :], in_=vu_tok[:]),
                      r=[bf('vu_tok')], w=[bf('gin_vu%d' % hh)])

        proj_mm(0, (0, 3, 4)); proj_mm(0, (1, 2))
        for t in range(12):
            post_a(t)
            if t + 1 < 12:
                proj_mm(t + 1, (0, 3, 4))
            post_b(t)
            trans(t)
            if t + 1 < 12:
                proj_mm(t + 1, (1, 2))
        fence()
        for h in range(2):
            cc = P.dma('pool', lambda e, h=h: e.collective_compute("AllGather", ALU.bypass, replica_groups=RG, ins=[gin_kt[h].opt()], outs=[gout_kt[h].opt()]),
                       r=[bf('gin_kt%d' % h)], w=[bf('gout_kt%d' % h)], inc=1)
            cc.own = ('cc', 4 * l + h)
            cc = P.dma('pool', lambda e, h=h: e.collective_compute("AllGather", ALU.bypass, replica_groups=RG, ins=[gin_vu[h].opt()], outs=[gout_vu[h].opt()]),
                       r=[bf('gin_vu%d' % h)], w=[bf('gout_vu%d' % h)], inc=1)
            cc.own = ('cc', 4 * l + 2 + h)
        gob = [bf('gout_kt0'), bf('gout_kt1'), bf('gout_vu0'), bf('gout_vu1')]
        WSr.reset(); TMP.reset()
        ktj = [WSr.alloc("ktj", [128, 4, 1024], BF16) for _ in range(2)]
        vj = [WSr.alloc("vj", [128, 32, 128], BF16) for _ in range(2)]
        pT = [TMP.alloc("pT", [128, 512], BF16) for _ in range(6)]
        T = (pT, TMP.alloc("rec", [128, 512], F32), TMP.alloc("at1", [128, 512], F32), TMP.alloc("at2", [128, 512], F32),
             TMP.alloc("asq", [128, 512], BF16),
             [[TMP.alloc("Qz", [128, 512], BF16) for _ in range(2)] for _ in range(2)],
             TMP.alloc("accP", [128, 512], F32))
        for u_ in range(2):
            for g_ in range(2):
                A('dve', lambda e, u_=u_, g_=g_: e.memset(T[5][u_][g_][:], 0.0), w=[bf('qzero')])
        for job in range(6):
            kind = 'A' if job < 2 else 'B'
            vcol = job * 128 if job < 2 else 256 + (job - 2) * 128
            segs = []
            for s_ in range(2):
                KTf = lambda kt, s_=s_, job=job: (KTL[:, job, s_ * 256 + kt * 128:s_ * 256 + (kt + 1) * 128], [bf('KTL')])
                Vf = lambda kt, s_=s_, vcol=vcol: (VL[:, s_ * 2 + kt, vcol:vcol + 128], [bf('VL')])
                segs.append((KTf, Vf, 2, QT[:, job, s_ * 256:(s_ + 1) * 256], s_ * 256, 256))
            attn_unit(l, kind, segs, job, 0, T, qz_eng='act')
        P.add('pool', lambda e: e.nop(), gob, gob)
        gk = [g_.rearrange("(r c p) n -> p r c n", c=3, p=128) for g_ in gout_kt]
        gv = [g_.rearrange("(t p) n -> p t n", p=128) for g_ in gout_vu]
        for job in range(6):
            kind = 'A' if job < 2 else 'B'
            vcol = job * 128 if job < 2 else 256 + (job - 2) * 128
            sl = job % 2
            ld('pool', ktj[sl][:], gk[job // 3][:, :, job % 3, :], [bf('ktj%d' % sl)], r=gob)
            for h in range(2):
                ld('pool', vj[sl][:, 16 * h:16 * h + 16, :], gv[h][:, :, vcol:vcol + 128], [bf('vj%d_%d' % (sl, h))], r=gob)

            def KTf(kt, job=job, sl=sl):
                if kt < 2:
                    return KTc[:, job, kt * 128:(kt + 1) * 128], [bf('KTc')]
                k2 = kt - 2
                h, r, t4 = k2 // 16, (k2 % 16) // 4, k2 % 4
                t = h * 4 + t4
                return ktj[sl][:, r, t * 128:(t + 1) * 128], [bf('ktj%d' % sl)]

            def Vf(kt, vcol=vcol, sl=sl):
                if kt < 2:
                    return Vc[:, kt, vcol:vcol + 128], [bf('Vc')]
                return vj[sl][:, kt - 2, :], [bf('vj%d_%d' % (sl, (kt - 2) // 16))]
            for qb in range(2):
                tok0 = 512 + qb * 512
                attn_unit(l, kind, [(KTf, Vf, 34, QT[:, job, tok0:tok0 + 512], 0, 512)], job, tok0, T, pool_share=True)
        fence()
        if MIXSUB <= 4:
            return
        WSr.reset(); TMP.reset()
        tabC = [WSr.alloc("tabC", [128, 4, 1024], BF16) for _ in range(2)]
        tabS = [WSr.alloc("tabS", [128, 4, 1024], BF16) for _ in range(2)]
        ug = TMP.alloc("ug", [128, 32, 256], BF16)
        XT = nc.alloc_sbuf_tensor_at("XT_%d" % l, [128, 2, 1536], BF16, offset=BIG.base)
        YT = nc.alloc_sbuf_tensor_at("YT_%d" % l, [128, 2, 1536], BF16, offset=BIG.base + 6144)
        FT = nc.alloc_sbuf_tensor_at("FT_%d" % l, [128, 2, 1536], BF16, offset=BIG.base + 12288)
        for h in range(2):
            ld('pool', ug[:, 16 * h:16 * h + 16, :], gv[h][:, :, 768:1024], [bf('ug%d' % h)], r=gob)
        CLv = CLd.rearrange("(t p) n -> p t n", p=128); SLv = SLd.rearrange("(t p) n -> p t n", p=128)
        for grp in range(8):
            sl = grp % 2
            ld('pool', tabC[sl][:], CLv[:, grp * 4:(grp + 1) * 4, :], [bf('tabC%d' % sl)])
            ld('pool', tabS[sl][:], SLv[:, grp * 4:(grp + 1) * 4, :], [bf('tabS%d' % sl)])
            for fc in range(2):
                for lb in range(2):
                    for ti, (tab, bt) in enumerate(((tabC[sl], 'tabC%d' % sl), (tabS[sl], 'tabS%d' % sl))):
                        b = ti * 4 + fc * 2 + lb
                        for i in range(4):
                            ui = (grp % 2) * 16 + (grp // 2) * 4 + i
                            mm(ps[b][:], ug[:, ui, fc * 128:(fc + 1) * 128], tab[:, i, lb * 512:(lb + 1) * 512],
                               grp == 0 and i == 0, grp == 7 and i == 3, [bf('ug%d' % (grp % 2)), bf(bt)], [bps[b]])
        for fc in range(2):
            for lb in range(2):
                A('act', lambda e, fc=fc, lb=lb: e.activation(out=XT[:, fc, 512 + lb * 512:1024 + lb * 512], in_=ps[fc * 2 + lb][:], func=AF.Copy),
                  r=[bps[fc * 2 + lb]], w=[bf('XT%d_%d' % (fc, 1 + lb))])
                A('dve', lambda e, fc=fc, lb=lb: e.tensor_copy(out=YT[:, fc, 512 + lb * 512:1024 + lb * 512], in_=ps[4 + fc * 2 + lb][:]),
                  r=[bps[4 + fc * 2 + lb]], w=[bf('YT%d_%d' % (fc, 1 + lb))])
        for s in range(2):
            for fc in range(2):
                bX = next_bank(); bY = next_bank()
                for lt in range(2):
                    mm(ps[bX][:, 0:256], VL[:, s * 2 + lt, 768 + fc * 128:768 + (fc + 1) * 128], C256[:, lt, :], lt == 0, lt == 1, [bf('VL'), bf('C256')], [bps[bX]])
                for lt in range(2):
                    mm(ps[bY][:, 0:256], VL[:, s * 2 + lt, 768 + fc * 128:768 + (fc + 1) * 128], S256[:, lt, :], lt == 0, lt == 1, [bf('VL'), bf('S256')], [bps[bY]])
                A('act', lambda e, s=s, fc=fc, bX=bX: e.activation(out=XT[:, fc, s * 256:(s + 1) * 256], in_=ps[bX][:, 0:256], func=AF.Copy), r=[bps[bX]], w=[bf('XT%d_0' % fc)])
                A('dve', lambda e, s=s, fc=fc, bY=bY: e.tensor_copy(out=YT[:, fc, s * 256:(s + 1) * 256], in_=ps[bY][:, 0:256]), r=[bps[bY]], w=[bf('YT%d_0' % fc)])
        for tb in range(3):
            ti = 2 if tb == 0 else 0
            for fc in range(2):
                b = next_bank()
                mm(ps[b][:], BCS[:, ti, :], XT[:, fc, tbs(tb)], True, False, [bf('BCS'), bf('XT%d_%d' % (fc, tb))], [bps[b]])
                mm(ps[b][:], BCS[:, ti + 1, :], YT[:, fc, tbs(tb)], False, True, [bf('BCS'), bf('YT%d_%d' % (fc, tb))], [bps[b]])
                A('act' if fc == 0 else 'dve',
                  (lambda e, b=b, fc=fc, tb=tb: e.activation(out=FT[:, fc, tbs(tb)], in_=ps[b][:], func=AF.Copy)) if fc == 0 else
                  (lambda e, b=b, fc=fc, tb=tb: e.tensor_copy(out=FT[:, fc, tbs(tb)], in_=ps[b][:])), r=[bps[b]], w=[bf('FT%d_%d' % (fc, tb))])
            for oc in range(2):
                b = next_bank()
                for fc in range(2):
                    mm(ps[b][:], wf[:, fc, oc * 128:(oc + 1) * 128], FT[:, fc, tbs(tb)], fc == 0, fc == 1, [bf('wf'), bf('FT%d_%d' % (fc, tb))], [bps[b]])
                A('act' if oc == 0 else 'dve',
                  (lambda e, b=b, oc=oc, tb=tb: e.activation(out=MIXT[:, 6 + oc, tbs(tb)], in_=ps[b][:], func=AF.Copy)) if oc == 0 else
                  (lambda e, b=b, oc=oc, tb=tb: e.tensor_copy(out=MIXT[:, 6 + oc, tbs(tb)], in_=ps[b][:])), r=[bps[b]], w=[bf('MIXF%d_%d' % (oc, tb))])
        if MIXDBG:
            for tb in range(3):
                for c in range(8):
                    A('dve', lambda e, c=c, tb=tb: e.tensor_copy(out=XRES[:, c, tbs(tb)], in_=MIXT[:, c, tbs(tb)]), r=mixbufs(c, tb) + [bx[c][tb]], w=[bx[c][tb]])
            fence()
            WSr.reset()
            for i in range(4):
                WS[i] = WSr.alloc("WS%d" % i, [128, 8, 512], BF16)
            return
        for tb in range(3):
            v = 0 if tb == 0 else 1
            for m in range(8):
                b = next_bank()
                for k in range(8):
                    mm(ps[b][:], WOUT[:, k, m * 128:(m + 1) * 128], MIXT[:, k, tbs(tb)], k == 0, k == 7, [bf('WOUT')] + mixbufs(k, tb), [bps[b]])
                A('dve', lambda e, b=b, m=m, tb=tb, v=v: e.scalar_tensor_tensor(out=XRES[:, m, tbs(tb)], in0=ps[b][:], scalar=mod(l, 5, m, v),
                                                                             in1=XRES[:, m, tbs(tb)], op0=ALU.mult, op1=ALU.add),
                  r=[bps[b], bx[m][tb], bf('mod')], w=[bx[m][tb]])
        fence()
        WSr.reset()
        for i in range(4):
            WS[i] = WSr.alloc("WS%d" % i, [128, 8, 512], BF16)
        ffn_prefetch(l, 2, w2gu, w2d)
        layernorm(l, 1, False)
        fence()

    ffn_prefetch(0, 0, w1gu, w1d)
    fence()
    done = False
    for l in range(2):
        if stage <= 4 * l + 0:
            break
        ffn(l, 0, w1gu, w1d, 0, stage == 4 * l + 1, after=(lambda l=l: mix_prefetch(l)))
        if stage <= 4 * l + 1:
            break
        mix(l)
        if stage <= 4 * l + 2:
            break
        ffn(l, 2, w2gu, w2d, 2, l == 1 or stage == 4 * l + 3, after=((lambda: ffn_prefetch(1, 0, w1gu, w1d)) if l == 0 else None))
        if stage <= 4 * l + 3:
            break
    store_y()
    fin = P.add('sp', lambda e: e.nop(), (), ())
    fin.deps = [o for o in outs if o is not None]
    P.plan()
    sems = {e: es.enter_context(nc.semaphore("s_" + e)) for e in ENGS}
    dsems = {}
    for e in ('pool', 'sp'):
        for i in range(P.n_dma_sems):
            dsems[(e, i)] = es.enter_context(nc.semaphore("d_%s%d" % (e, i)))
    for i in range(9):
        dsems[('cc', i)] = es.enter_context(nc.semaphore("cc%d" % i))
    block = es.enter_context(nc.Block())
    P.emit(nc, block, sems, dsems)
    es.close()
    return nc


_NC_CACHE = {}


def _rope_tables(length):
    rows = length // 64
    row = np.repeat(np.arange(rows), 64).astype(np.float32)
    col = np.tile(np.arange(64), rows).astype(np.float32)
    inv = (1.0 / (10000.0 ** (np.arange(0, 32, 2, dtype=np.float32) / 32.0))).astype(np.float32)
    ar = row[:, None] * inv
    ac = col[:, None] * inv
    cos = np.concatenate([np.cos(ar), np.cos(ar), np.cos(ac), np.cos(ac)], -1).astype(np.float32)
    sin = np.concatenate([np.sin(ar), np.sin(ar), np.sin(ac), np.sin(ac)], -1).astype(np.float32)
    sgn = np.where((np.arange(64) % 32) < 16, -1.0, 1.0).astype(np.float32)
    return cos, sin * sgn


def _consts():
    bfl = ml_dtypes.bfloat16
    l4 = np.arange(4096, dtype=np.int64)
    ph = (l4[:, None] * l4[None, :]) % 4096
    ang = ph.astype(np.float64) * (2.0 * np.pi / 4096)
    CL = np.cos(ang).astype(bfl)
    SL = np.sin(ang).astype(bfl)
    l2 = np.arange(256, dtype=np.int64)
    a2 = ((l2[:, None] * l2[None, :]) % 256).astype(np.float64) * (2.0 * np.pi / 256)
    C256 = np.cos(a2).astype(bfl)
    S256 = np.sin(a2).astype(bfl)
    c = np.arange(64, dtype=np.int64)
    a3 = ((c[:, None] * c[None, :]) % 64).astype(np.float64) * (2.0 * np.pi / 64)
    C64 = np.cos(a3)
    S64 = np.sin(a3)
    z = np.zeros((64, 64))
    bd = lambda m: np.block([[m, z], [z, m]])
    BCS = np.stack([bd(C64) / 512.0, -bd(S64) / 512.0, bd(C64) / 128.0, -bd(S64) / 128.0]).astype(bfl)
    return CL, SL, C256, S256, BCS


def kernel(x_prompt, x_sample, cache_a_k, cache_a_v, cache_b_k, cache_b_v, c, c_ctx,
           w_mod, b_mod, w_in, g_qa, g_ka, lam_q1, lam_k1, lam_q2, lam_k2, g_subln,
           w_fourier, w_out, w_ffn1_gu, w_ffn1_down, w_ffn2_gu, w_ffn2_down, ln_g, ln_b, _stage=99):
    f32 = np.float32
    A_ = lambda a: np.ascontiguousarray(np.asarray(a, dtype=f32))
    x_prompt = A_(x_prompt); x_sample = A_(x_sample)
    cache_a_k = A_(cache_a_k); cache_a_v = A_(cache_a_v); cache_b_k = A_(cache_b_k); cache_b_v = A_(cache_b_v)
    c = A_(c); c_ctx = A_(c_ctx)
    if _stage not in _NC_CACHE:
        _NC_CACHE[_stage] = build(_stage)
    nc = _NC_CACHE[_stage]
    CL, SL, C256, S256, BCS = _consts()
    cos, sins = _rope_tables(4096)
    bfl = ml_dtypes.bfloat16
    shared = {
        "w_in": A_(w_in), "w_out": A_(w_out), "w_f": A_(w_fourier),
        "w1gu": A_(w_ffn1_gu), "w1d": A_(w_ffn1_down), "w2gu": A_(w_ffn2_gu), "w2d": A_(w_ffn2_down),
        "lnT": np.ascontiguousarray(np.stack([A_(ln_g), A_(ln_b)]).reshape(2, 2, 3, 8, 128).transpose(4, 0, 1, 2, 3)),
        "gqk": np.ascontiguousarray(np.broadcast_to(
            np.concatenate([np.repeat(A_(g_qa)[:, None, :], 4, 1), np.repeat(A_(g_ka)[:, None, :], 2, 1)], 1)[None], (128, 2, 6, 64))),
        "gsubT": np.ascontiguousarray(A_(g_subln).T),
        "lamv": np.ascontiguousarray(np.broadcast_to(np.stack([A_(lam_q1), A_(lam_k1), A_(lam_q2), A_(lam_k2)], 1)[None], (128, 2, 4, 64))),
        "C256": C256, "S256": S256, "BCS": BCS,
        "identf": np.eye(128, dtype=f32), "identb": np.eye(128, dtype=f32).astype(bfl), "onesb": np.ones((128, 128), dtype=bfl),
    }
    w_mod_f = A_(w_mod)
    bmodT_full = A_(b_mod).reshape(2, 72, 128).transpose(2, 0, 1)
    in_maps = []
    for i in range(8):
        b, r = i // 4, i % 4
        xin = np.concatenate([x_prompt[2 * i], x_prompt[2 * i + 1], x_sample[b, r * 1024:(r + 1) * 1024]], 0)
        cv = np.stack([c_ctx, c[0], c[1]], 0)
        m = dict(shared)
        m["xin"] = np.ascontiguousarray(xin)
        m["cvT"] = np.ascontiguousarray(cv.reshape(3, 8, 128).transpose(2, 1, 0))
        m["wmod"] = np.ascontiguousarray(w_mod_f[:, :, 2304 * r:2304 * (r + 1)])
        m["bmodT"] = np.ascontiguousarray(bmodT_full[:, :, 18 * r:18 * (r + 1)])
        selv = np.zeros((128, 2), f32); selv[:, b] = 1.0
        m["sel"] = selv
        m["ropec"] = np.ascontiguousarray(cos[r * 1024:(r + 1) * 1024].reshape(8, 128, 64).transpose(1, 0, 2))
        m["ropes"] = np.ascontiguousarray(sins[r * 1024:(r + 1) * 1024].reshape(8, 128, 64).transpose(1, 0, 2))
        m["cak"] = np.ascontiguousarray(cache_a_k[b].reshape(2, 256, 128)); m["cav"] = np.ascontiguousarray(cache_a_v[b].reshape(2, 256, 128))
        m["cbk"] = np.ascontiguousarray(cache_b_k[b].reshape(2, 256, 512)); m["cbv"] = np.ascontiguousarray(cache_b_v[b].reshape(2, 256, 512))
        m["CL"] = np.ascontiguousarray(CL[:, r * 1024:(r + 1) * 1024]); m["SL"] = np.ascontiguousarray(SL[:, r * 1024:(r + 1) * 1024])
        in_maps.append(m)
    res = run_bass_kernel_spmd(nc, in_maps[:NCORES], core_ids=list(range(NCORES)))
    y_prompt = np.zeros((16, 256, 1024), f32); y_sample = np.zeros((2, 4096, 1024), f32)
    nak = np.zeros((16, 2, 256, 2, 64), f32); nav = np.zeros((16, 2, 256, 2, 64), f32)
    nbk = np.zeros((16, 2, 256, 4, 128), f32); nbv = np.zeros((16, 2, 256, 4, 128), f32)
    for i in range(NCORES):
        b, r = i // 4, i % 4
        o = res.results[i]
        yy = np.asarray(o["y"], dtype=f32)
        y_prompt[2 * i] = yy[0:256]; y_prompt[2 * i + 1] = yy[256:512]
        y_sample[b, r * 1024:(r + 1) * 1024] = yy[512:1536]
        for s in range(2):
            for l in range(2):
                nak[2 * i + s, l] = np.asarray(o["nak"])[l, s * 256:(s + 1) * 256].reshape(256, 2, 64)
                nav[2 * i + s, l] = np.asarray(o["nav"])[l, s * 256:(s + 1) * 256].reshape(256, 2, 64)
                nbk[2 * i + s, l] = np.asarray(o["nbk"])[l, s * 256:(s + 1) * 256].reshape(256, 4, 128)
                nbv[2 * i + s, l] = np.asarray(o["nbv"])[l, s * 256:(s + 1) * 256].reshape(256, 4, 128)
    return (y_prompt, y_sample, nak, nav, nbk, nbv)
```

```python
import math
import numpy as np
import ml_dtypes
from contextlib import ExitStack
import concourse.bass as bass
import concourse.mybir as mybir
from concourse.bass_utils import run_bass_kernel_spmd

F32 = mybir.dt.float32
BF16 = mybir.dt.bfloat16
AF = mybir.ActivationFunctionType
ALU = mybir.AluOpType
AX = mybir.AxisListType

ENGS = ('pe', 'act', 'dve', 'pool', 'sp')


class Buf:
    __slots__ = ('name', 'w', 'r')

    def __init__(self, name=''):
        self.name = name
        self.w = None
        self.r = []


class Op:
    __slots__ = ('eng', 'fn', 'deps', 'idx', 'signal', 'count', 'is_dma', 'sem', 'val',
                 'waits', 'known_after', 'id', 'inc', 'own')

    def __init__(self, eng, fn, is_dma, inc=16):
        self.eng = eng
        self.fn = fn
        self.is_dma = is_dma
        self.deps = []
        self.signal = False
        self.count = 0
        self.sem = None
        self.val = 0
        self.waits = []
        self.known_after = None
        self.inc = inc
        self.own = None


class Prog:
    def __init__(self, n_dma_sems=12):
        self.ops = {e: [] for e in ENGS}
        self.all = []
        self.n_dma_sems = n_dma_sems

    def add(self, eng, fn, r=(), w=(), dma=False, inc=16):
        op = Op(eng, fn, dma, inc)
        op.id = len(self.all)
        deps = {}
        for b in r:
            if b.w is not None:
                deps[b.w.id] = b.w
        for b in w:
            if b.w is not None:
                deps[b.w.id] = b.w
            for x in b.r:
                deps[x.id] = x
        for b in r:
            b.r.append(op)
        for b in w:
            b.w = op
            b.r = []
        op.deps = list(deps.values())
        op.idx = len(self.ops[eng])
        self.ops[eng].append(op)
        self.all.append(op)
        return op

    def dma(self, queue, fn, r=(), w=(), inc=16):
        return self.add(queue, fn, r, w, dma=True, inc=inc)

    def barrier(self, bufs):
        pass

    def plan(self):
        known = {e: {} for e in ENGS}
        waited_dma = {e: set() for e in ENGS}
        ring = {e: [None] * self.n_dma_sems for e in ENGS}
        ring_pos = {e: 0 for e in ENGS}
        ring_val = {e: [0] * self.n_dma_sems for e in ENGS}
        for op in self.all:
            e = op.eng
            kn = known[e]
            waits = []
            changed = False
            for d in sorted(op.deps, key=lambda d: (not d.is_dma and d.eng == e)):
                if d.is_dma:
                    if d.id in waited_dma[e]:
                        continue
                    waited_dma[e].add(d.id)
                    waits.append(('dma', d))
                    if d.known_after:
                        for k, v in d.known_after.items():
                            if kn.get(k, -1) < v:
                                if not changed:
                                    kn = dict(kn)
                                    changed = True
                                kn[k] = v
                else:
                    if d.eng == e and e == 'pe' and not op.is_dma:
                        continue
                    if kn.get(d.eng, -1) >= d.idx:
                        continue
                    d.signal = True
                    waits.append(('cmp', d))
                    if not changed:
                        kn = dict(kn)
                        changed = True
                    kn[d.eng] = d.idx
                    if d.known_after:
                        for k, v in d.known_after.items():
                            if kn.get(k, -1) < v:
                                kn[k] = v
            if op.is_dma and op.own is not None:
                op.sem = op.own
                op.val = op.inc
            elif op.is_dma:
                pos = ring_pos[e]
                prev = ring[e][pos]
                if prev is not None and prev.id not in waited_dma[e]:
                    waited_dma[e].add(prev.id)
                    waits.append(('dma', prev))
                ring[e][pos] = op
                ring_val[e][pos] += op.inc
                op.sem = (e, pos)
                op.val = ring_val[e][pos]
                ring_pos[e] = (pos + 1) % self.n_dma_sems
            known[e] = kn
            op.waits = waits
            op.known_after = kn
        for e in ENGS:
            c = 0
            for op in self.ops[e]:
                if op.is_dma:
                    continue
                if op.signal:
                    c += 1
                    op.count = c

    def emit(self, nc, block, sems, dma_sems):
        engobj = {'pe': block.tensor, 'act': block.scalar, 'dve': block.vector,
                  'pool': block.gpsimd, 'sp': block.sync}
        for e in ENGS:
            ops = self.ops[e]
            if not ops:
                continue

            def body(eng, ops=ops, e=e):
                for op in ops:
                    wl = {}
                    for kind, d in op.waits:
                        if kind == 'dma':
                            s = dma_sems[d.sem]
                            v = d.val
                        else:
                            s = sems[d.eng]
                            v = d.count
                        key = id(s)
                        if key not in wl or wl[key][1] < v:
                            wl[key] = (s, v)
                    for s, v in wl.values():
                        eng.wait_ge(s, v)
                    ins = op.fn(eng)
                    if op.is_dma:
                        ins.then_inc(dma_sems[op.sem], op.inc)
                    elif op.signal:
                        ins.then_inc(sems[e], 1)
            engobj[e](body)

    def final_waits(self, ops):
        return ops

U8 = mybir.dt.uint8
ALPHA = (2.0 * 2) ** 0.25
LN_EPS = 1e-5
RMS_EPS = 1e-6
FF = 2816
LAM_INIT = [0.8 - 0.6 * math.exp(-0.3 * l) for l in range(2)]
RG = [[0, 1, 2, 3], [4, 5, 6, 7]]
MIXSUB = 9
NCORES = 8
PSUB = 99
GSEL = -1
DBGLIM = -1
MIXDBG = 0


def build(stage=99):
    nc = bass.Bass("TRN2", target_bir_lowering=False)

    def din(name, shape, dt=F32):
        return nc.dram_tensor(name, list(shape), dt, kind="ExternalInput").ap()

    def dout(name, shape, dt=F32):
        return nc.dram_tensor(name, list(shape), dt, kind="ExternalOutput").ap()

    xin = din("xin", [1536, 1024]); cvT_d = din("cvT", [128, 8, 3]); wmod = din("wmod", [2, 1024, 2304]); sel_d = din("sel", [128, 2])
    bmodT_d = din("bmodT", [128, 2, 18])
    w_in = din("w_in", [2, 1024, 2304]); w_out = din("w_out", [2, 1024, 1024]); w_f = din("w_f", [2, 256, 256])
    w1gu = din("w1gu", [2, 1024, 5632]); w1d = din("w1d", [2, 2816, 1024])
    w2gu = din("w2gu", [2, 1024, 5632]); w2d = din("w2d", [2, 2816, 1024])
    lnT_d = din("lnT", [128, 2, 2, 3, 8]); gqk_d = din("gqk", [128, 2, 6, 64]); gsubT_d = din("gsubT", [128, 2])
    lamv_d = din("lamv", [128, 2, 4, 64]); ropec_d = din("ropec", [128, 8, 64]); ropes_d = din("ropes", [128, 8, 64])
    cak = din("cak", [2, 256, 128]); cav = din("cav", [2, 256, 128]); cbk = din("cbk", [2, 256, 512]); cbv = din("cbv", [2, 256, 512])
    CLd = din("CL", [4096, 1024], BF16); SLd = din("SL", [4096, 1024], BF16)
    C256d = din("C256", [256, 256], BF16); S256d = din("S256", [256, 256], BF16); BCSd = din("BCS", [4, 128, 128], BF16)
    identf_d = din("identf", [128, 128]); identb_d = din("identb", [128, 128], BF16); onesb_d = din("onesb", [128, 128], BF16)
    y = dout("y", [1536, 1024]); nak = dout("nak", [2, 512, 128]); nav = dout("nav", [2, 512, 128])
    nbk = dout("nbk", [2, 512, 512]); nbv = dout("nbv", [2, 512, 512])
    gin_kt = [nc.dram_tensor("gin_kt%d" % h, [384, 1024], BF16).ap() for h in range(2)]
    gout_kt = [nc.dram_tensor("gout_kt%d" % h, [1536, 1024], BF16).ap() for h in range(2)]
    gin_vu = [nc.dram_tensor("gin_vu%d" % h, [512, 1024], BF16).ap() for h in range(2)]
    gout_vu = [nc.dram_tensor("gout_vu%d" % h, [2048, 1024], BF16).ap() for h in range(2)]

    gin_mod = nc.dram_tensor("gin_mod", [128, 108], F32).ap()
    gout_mod = nc.dram_tensor("gout_mod", [512, 108], F32).ap()
    P = Prog()
    dbg = {'on': False, 'n': 0}
    _orig_add = P.add

    def _add_dbg(eng, fn, r=(), w=(), dma=False, inc=16):
        if dbg['on'] and DBGLIM >= 0:
            dbg['n'] += 1
            if dbg['n'] > DBGLIM:
                return None
        return _orig_add(eng, fn, r, w, dma, inc)
    P.add = _add_dbg
    es = ExitStack()
    base0 = (nc.sbuf_base + 63) // 64 * 64
    avail = nc.sbuf_top - base0 - 64
    arena = es.enter_context(nc.sbuf_tensor("arena", [128, avail], U8))
    cur = [base0]
    lim = base0 + avail

    def esz(dt):
        return 4 if dt == F32 else 2

    def alloc(name, shape, dt, at=None):
        nb = int(np.prod(shape[1:])) * esz(dt)
        nb = (nb + 63) // 64 * 64
        if at is None:
            o = cur[0]
            cur[0] += nb
            assert cur[0] <= lim, (name, cur[0], lim)
        else:
            o = at
        t = nc.alloc_sbuf_tensor_at(name + "_%d" % o, list(shape), dt, offset=o)
        return t

    class Region:
        def __init__(self, size):
            self.base = cur[0]
            self.size = size
            cur[0] += size
            assert cur[0] <= lim, ("region", cur[0], lim)
            self.p = self.base
            self.n = 0

        def reset(self):
            self.p = self.base

        def alloc(self, name, shape, dt):
            nb = int(np.prod(shape[1:])) * esz(dt)
            nb = (nb + 63) // 64 * 64
            o = self.p
            self.p += nb
            assert self.p <= self.base + self.size, (name, self.p - self.base, self.size)
            self.n += 1
            return nc.alloc_sbuf_tensor_at("%s_r%d" % (name, self.n), list(shape), dt, offset=o)

    XRES = alloc("XRES", [128, 8, 1536], F32)
    HB = alloc("HB", [128, 8, 1536], BF16)
    WSr = Region(32768)
    WS = [WSr.alloc("WS%d" % i, [128, 8, 512], BF16) for i in range(4)]
    BIG = Region(56320)
    TMP = Region(20480)
    identf = alloc("identf", [128, 128], F32); identb = alloc("identb", [128, 128], BF16); onesb = alloc("onesb", [128, 128], BF16)
    modT = alloc("modT", [128, 2, 72, 2], F32); scT = alloc("scT", [128, 8, 3], BF16); cvT = alloc("cvT", [128, 8, 3], F32)
    modsl = alloc("modsl", [128, 108], F32); modall = alloc("modall", [128, 4, 108], F32); mtmp = alloc("mtmp", [128, 4, 18], F32); sel = alloc("sel", [128, 2], F32)
    bmodT = alloc("bmodT", [128, 2, 18], F32)
    lnT = alloc("lnT", [128, 2, 2, 3, 8], F32); lnA = alloc("lnA", [128, 2, 2, 3, 8], F32)
    gqk = alloc("gqk", [128, 2, 6, 64], F32); gsub = alloc("gsub", [128, 2], F32); neglam = alloc("neglam", [128, 2], F32)
    lamv = alloc("lamv", [128, 2, 4, 64], F32); lamt = alloc("lamt", [128, 4], F32); lamp = alloc("lamp", [128, 64], F32)
    ropec = alloc("ropec", [128, 8, 64], F32); ropes = alloc("ropes", [128, 8, 64], F32)
    KTc = alloc("KTc", [128, 6, 256], BF16); Vc = alloc("Vc", [128, 2, 768], BF16)
    C256 = alloc("C256", [128, 2, 256], BF16); S256 = alloc("S256", [128, 2, 256], BF16); BCS = alloc("BCS", [128, 4, 128], BF16)
    wf = alloc("wf", [128, 2, 256], BF16)
    epsln = alloc("epsln", [128, 1], F32); epsrms = alloc("epsrms", [128, 1], F32)
    ps = [nc.alloc_psum_tensor("ps%d" % i, [128, 512], F32) for i in range(8)]

    bps = [Buf("ps%d" % i) for i in range(8)]
    bx = [[Buf() for _ in range(3)] for _ in range(8)]
    bh = [[Buf() for _ in range(3)] for _ in range(8)]
    bws = [Buf() for _ in range(4)]
    bufs = {}

    def bf(name):
        if name not in bufs:
            bufs[name] = Buf(name)
        return bufs[name]

    def A(eng, fn, r=(), w=()):
        return P.add(eng, fn, r, w)

    def mm(out, lhsT, rhs, st, sp, r, w):
        return P.add('pe', lambda e: e.matmul(out, lhsT=lhsT, rhs=rhs, start=st, stop=sp), r, w)

    def tr(out, in_, ident, r, w):
        return P.add('pe', lambda e: e.transpose(out=out, in_=in_, identity=ident), r, w)

    def ld(q, dst, src, w, r=()):
        return P.dma(q, lambda e: e.dma_start(out=dst, in_=src), r=r, w=w)

    def fence():
        last = [P.ops[e][-1] for e in ENGS if P.ops[e]]
        for e in ('pe', 'act', 'dve', 'pool', 'sp'):
            op = P.add(e, lambda eng: eng.nop(), (), ())
            op.deps = list(last)
        return

    def tbs(tb):
        return slice(tb * 512, (tb + 1) * 512)

    def mod(l, i, c, v):
        return modT[:, l, i * 8 + c, v:v + 1]

    MIXT = HB
    outs = []
    wsc = [0]

    def next_ws():
        s = wsc[0] % 4
        wsc[0] += 1
        return s

    bankc = [0]
    unitc = [0]

    def next_bank():
        b = bankc[0] % 8
        bankc[0] += 1
        return b

    ld('sp', identf[:], identf_d[:, :], [bf('identf')]); ld('sp', identb[:], identb_d[:, :], [bf('identb')])
    ld('sp', onesb[:], onesb_d[:, :], [bf('onesb')]); ld('sp', cvT[:], cvT_d[:, :, :], [bf('cvT')])
    ld('sp', bmodT[:], bmodT_d[:, :, :], [bf('bmodT')]); ld('sp', sel[:], sel_d[:, :], [bf('sel')]); ld('sp', lnT[:], lnT_d[:, :, :, :, :], [bf('lnT')])
    ld('sp', gqk[:], gqk_d[:, :, :, :], [bf('gqk')]); ld('sp', gsub[:], gsubT_d[:, :], [bf('gsub')])
    ld('sp', lamv[:], lamv_d[:, :, :, :], [bf('lamv')]); ld('sp', ropec[:], ropec_d[:, :, :], [bf('rope')])
    ld('sp', ropes[:], ropes_d[:, :, :], [bf('rope')])
    ld('sp', C256[:], C256d.rearrange("(t p) n -> p t n", p=128), [bf('C256')])
    ld('sp', S256[:], S256d.rearrange("(t p) n -> p t n", p=128), [bf('S256')])
    ld('sp', BCS[:], BCSd.rearrange("k p n -> p k n"), [bf('BCS')])
    A('dve', lambda e: e.memset(epsln[:], LN_EPS), w=[bf('eps')])
    A('dve', lambda e: e.memset(epsrms[:], RMS_EPS), w=[bf('eps')])
    A('act', lambda e: e.activation(out=scT[:], in_=cvT[:], func=AF.Silu), r=[bf('cvT')], w=[bf('scT')])
    A('dve', lambda e: e.tensor_scalar(out=lnA[:], in0=lnT[:], scalar1=ALPHA, scalar2=None, op0=ALU.mult), r=[bf('lnT')], w=[bf('lnA')])
    for l in range(2):
        for j in range(2):
            A('dve', lambda e, l=l, j=j: e.tensor_tensor(out=lamp[:], in0=lamv[:, l, 2 * j, :], in1=lamv[:, l, 2 * j + 1, :], op=ALU.mult),
              r=[bf('lamv')], w=[bf('lamp')])
            A('dve', lambda e, l=l, j=j: e.tensor_reduce(out=lamt[:, 2 * l + j:2 * l + j + 1], in_=lamp[:], op=ALU.add, axis=AX.X),
              r=[bf('lamp')], w=[bf('lamt')])
    A('act', lambda e: e.activation(out=lamt[:], in_=lamt[:], func=AF.Exp), r=[bf('lamt')], w=[bf('lamt')])
    for l in range(2):
        A('dve', lambda e, l=l: e.tensor_tensor(out=neglam[:, l:l + 1], in0=lamt[:, 2 * l + 1:2 * l + 2], in1=lamt[:, 2 * l:2 * l + 1], op=ALU.subtract),
          r=[bf('lamt')], w=[bf('neglam')])
        A('dve', lambda e, l=l: e.tensor_scalar(out=neglam[:, l:l + 1], in0=neglam[:, l:l + 1], scalar1=-LAM_INIT[l], scalar2=None, op0=ALU.add),
          r=[bf('neglam')], w=[bf('neglam')])
        A('dve', lambda e, l=l: e.tensor_scalar(out=gsub[:, l:l + 1], in0=gsub[:, l:l + 1], scalar1=1.0 - LAM_INIT[l], scalar2=None, op0=ALU.mult),
          r=[bf('gsub')], w=[bf('gsub')])

    for l in range(2):
        wv = wmod[l].rearrange("(c p) n -> p c n", p=128)
        for (c0, ncol) in ((0, 512), (512, 512), (1024, 512), (1536, 512), (2048, 256)):
            s_ = next_ws()
            ld('pool', WS[s_][:, :, 0:ncol], wv[:, :, c0:c0 + ncol], [bws[s_]])
            for cc in range(ncol // 128):
                col = (l * 18 + c0 // 128 + cc) * 3
                for k in range(8):
                    mm(ps[0][:, col:col + 3], WS[s_][:, k, cc * 128:(cc + 1) * 128], scT[:, k, :], k == 0, k == 7,
                       [bws[s_], bf('scT')], [bps[0]])
    A('dve', lambda e: e.tensor_tensor(out=modsl[:].rearrange("p (m v) -> p m v", v=3), in0=ps[0][:, 0:108].rearrange("p (m v) -> p m v", v=3),
                                       in1=bmodT[:].rearrange("p l m -> p (l m)").unsqueeze(2).to_broadcast([128, 36, 3]), op=ALU.add),
      r=[bps[0], bf('bmodT')], w=[bf('modsl')])
    P.dma('sp', lambda e: e.dma_start(out=gin_mod[:, :], in_=modsl[:]), r=[bf('modsl')], w=[bf('gin_mod')])
    ccm = P.dma('pool', lambda e: e.collective_compute("AllGather", ALU.bypass, replica_groups=RG, ins=[gin_mod.opt()], outs=[gout_mod.opt()]),
                r=[bf('gin_mod')], w=[bf('gout_mod')], inc=1)
    ccm.own = ('cc', 8)
    P.add('pool', lambda e: e.nop(), [bf('gout_mod')], [bf('gout_mod')])
    ld('sp', modall[:], gout_mod.rearrange("(r p) n -> p r n", p=128), [bf('modall')], r=[bf('gout_mod')])
    modall5 = modall[:].rearrange("p r (l m v) -> p r l m v", l=2, m=18, v=3)
    for l in range(2):
        mv = [modT[:, l, :, vv].rearrange("p (r m) -> p r m", m=18) for vv in range(2)]
        A('dve', lambda e, l=l, mv=mv: e.tensor_copy(out=mv[0], in_=modall5[:, :, l, :, 0]), r=[bf('modall')], w=[bf('mod')])
        A('dve', lambda e, l=l: e.tensor_scalar(out=mtmp[:], in0=modall5[:, :, l, :, 1], scalar1=sel[:, 0:1], scalar2=None, op0=ALU.mult),
          r=[bf('modall'), bf('sel')], w=[bf('mtmp')])
        A('dve', lambda e, l=l, mv=mv: e.scalar_tensor_tensor(out=mv[1], in0=modall5[:, :, l, :, 2], scalar=sel[:, 1:2], in1=mtmp[:], op0=ALU.mult, op1=ALU.add),
          r=[bf('modall'), bf('sel'), bf('mtmp')], w=[bf('mod')])
        for i in (1, 4, 7):
            A('dve', lambda e, l=l, i=i: e.tensor_scalar(out=modT[:, l, i * 8:(i + 1) * 8, :], in0=modT[:, l, i * 8:(i + 1) * 8, :],
                                                         scalar1=1.0, scalar2=1.0 / ALPHA, op0=ALU.add, op1=ALU.mult),
              r=[bf('mod')], w=[bf('mod')])
        for i in (2, 8):
            A('dve', lambda e, l=l, i=i: e.tensor_scalar(out=modT[:, l, i * 8:(i + 1) * 8, :], in0=modT[:, l, i * 8:(i + 1) * 8, :],
                                                         scalar1=0.5, scalar2=None, op0=ALU.mult),
              r=[bf('mod')], w=[bf('mod')])

    TMP.reset()
    xstg = [TMP.alloc("xstg", [128, 1024], F32) for _ in range(2)]
    for t in range(12):
        st = xstg[t % 2]
        ld('sp', st[:], xin[t * 128:(t + 1) * 128, :], [bf('xstg%d' % (t % 2))])
        tb = t // 4
        for half in range(2):
            b = next_bank()
            for cc in range(4):
                c = half * 4 + cc
                tr(ps[b][:, cc * 128:(cc + 1) * 128], st[:, c * 128:(c + 1) * 128], identf[:],
                   [bf('xstg%d' % (t % 2)), bf('identf')], [bps[b]])
            dst = XRES[:, half * 4:half * 4 + 4, t * 128:(t + 1) * 128]
            src = ps[b][:].rearrange("p (c t) -> p c t", t=128)
            wl = [bx[half * 4 + cc][tb] for cc in range(4)]
            if half == 0:
                A('act', lambda e, dst=dst, src=src: e.activation(out=dst, in_=src, func=AF.Copy, scale=ALPHA), r=[bps[b]], w=wl)
            else:
                A('dve', lambda e, dst=dst, src=src: e.tensor_scalar(out=dst, in0=src, scalar1=ALPHA, scalar2=None, op0=ALU.mult), r=[bps[b]], w=wl)

    def layernorm(l, i, final):
        TMP.reset()
        mean_t = TMP.alloc("mean", [128, 512], F32); t1 = TMP.alloc("t1", [128, 512], F32)
        rstd_t = TMP.alloc("rstd", [128, 512], F32)
        tm2 = [TMP.alloc("tm2", [128, 512], F32) for _ in range(2)]
        ZSQ = TMP.alloc("zsq", [128, 8, 512], BF16)
        lnp = lnT if final else lnA
        for tb in range(3):
            for c in range(8):
                A('act', lambda e, c=c, tb=tb: e.activation(out=HB[:, c, tbs(tb)], in_=XRES[:, c, tbs(tb)], func=AF.Copy),
                  r=[bx[c][tb]], w=[bh[c][tb]])
                A('act', lambda e, c=c, tb=tb: e.activation(out=ZSQ[:, c, :], in_=XRES[:, c, tbs(tb)], func=AF.Square),
                  r=[bx[c][tb]], w=[bf('zsq%d' % c)])
            b_s = next_bank(); b_q = next_bank()
            for c in range(8):
                mm(ps[b_s][:], onesb[:], HB[:, c, tbs(tb)], c == 0, c == 7, [bh[c][tb], bf('onesb')], [bps[b_s]])
            for c in range(8):
                mm(ps[b_q][:], onesb[:], ZSQ[:, c, :], c == 0, c == 7, [bf('zsq%d' % c), bf('onesb')], [bps[b_q]])
            A('act', lambda e, b_s=b_s: e.activation(out=mean_t[:], in_=ps[b_s][:], func=AF.Copy, scale=1.0 / 1024), r=[bps[b_s]], w=[bf('mean')])
            A('dve', lambda e: e.tensor_tensor(out=t1[:], in0=mean_t[:], in1=mean_t[:], op=ALU.mult), r=[bf('mean')], w=[bf('t1')])
            A('dve', lambda e, b_q=b_q: e.scalar_tensor_tensor(out=t1[:], in0=ps[b_q][:], scalar=1.0 / 1024, in1=t1[:], op0=ALU.mult, op1=ALU.subtract),
              r=[bps[b_q], bf('t1')], w=[bf('t1')])
            A('act', lambda e: e.activation(out=t1[:], in_=t1[:], func=AF.Sqrt, bias=epsln[:, 0:1], scale=1.0), r=[bf('t1'), bf('eps')], w=[bf('t1')])
            b_r = next_bank()
            A('dve', lambda e, b_r=b_r: e.reciprocal(out=ps[b_r][:], in_=t1[:]), r=[bf('t1')], w=[bps[b_r]])
            for c in range(8):
                tm = tm2[c % 2]; btm = bf('tm2%d' % (c % 2))
                if c in (2, 5, 7):
                    A('dve', lambda e, c=c, tb=tb, tm=tm, b_s=b_s: e.scalar_tensor_tensor(out=tm[:], in0=ps[b_s][:], scalar=-1.0 / 1024, in1=XRES[:, c, tbs(tb)],
                                                                                     op0=ALU.mult, op1=ALU.add),
                      r=[bx[c][tb], bps[b_s]], w=[btm])
                else:
                    A('pool', lambda e, c=c, tb=tb, tm=tm: e.tensor_tensor(out=tm[:], in0=XRES[:, c, tbs(tb)], in1=mean_t[:], op=ALU.subtract),
                      r=[bx[c][tb], bf('mean')], w=[btm])
                A('dve', lambda e, tm=tm, b_r=b_r: e.tensor_tensor(out=tm[:], in0=tm[:], in1=ps[b_r][:], op=ALU.mult), r=[btm, bps[b_r]], w=[btm])
                A('act', lambda e, c=c, tb=tb, tm=tm: e.activation(out=XRES[:, c, tbs(tb)], in_=tm[:], func=AF.Identity,
                                                                   scale=lnp[:, 0, l, i, c:c + 1], bias=lnp[:, 1, l, i, c:c + 1]),
                  r=[btm, bf('lnT'), bf('lnA')], w=[bx[c][tb]])

    def modulate(l, si):
        for tb in range(3):
            v = 0 if tb == 0 else 1
            for c in range(8):
                A('act', lambda e, c=c, tb=tb, v=v: e.activation(out=HB[:, c, tbs(tb)], in_=XRES[:, c, tbs(tb)], func=AF.Identity,
                                                                 scale=mod(l, 3 * si + 1, c, v), bias=mod(l, 3 * si, c, v)),
                  r=[bx[c][tb], bf('mod')], w=[bh[c][tb]])

    FGROUPS = ((0, 4), (4, 4), (8, 3))
    pf = {}

    def ffn_prefetch(l, si, wgu, wd):
        BIG.reset()
        AH = BIG.alloc("AH", [128, 11, 1536], BF16); WD = BIG.alloc("WD", [128, 11, 1024], BF16)
        bwd = [bf('wd%d' % g) for g in range(3)]
        wguv = wgu[l].rearrange("(c p) n -> p c n", p=128)
        wdv = wd[l].rearrange("(j p) n -> p j n", p=128)
        slots = []
        for gi, (a, n) in enumerate(FGROUPS[:2]):
            sg_ = next_ws(); su_ = next_ws()
            col0 = a * 128; ncol = n * 128
            ld('pool', WS[sg_][:, :, 0:ncol], wguv[:, :, col0:col0 + ncol], [bws[sg_]])
            ld('pool', WS[su_][:, :, 0:ncol], wguv[:, :, FF + col0:FF + col0 + ncol], [bws[su_]])
            slots.append((sg_, su_))
        for gi, (a, n) in enumerate(FGROUPS):
            ld('pool', WD[:, a:a + n, :], wdv[:, a:a + n, :], [bwd[gi]])
        pf[(l, si)] = (AH, WD, slots)

    def mix_prefetch(l):
        winv = w_in[l].rearrange("(c p) n -> p c n", p=128)
        for g in range(4):
            ld('pool', WS[g][:], winv[:, :, g * 512:(g + 1) * 512], [bws[g]])
        wsc[0] = 0
        pf[('mix', l)] = True

    def ffn(l, si, wgu, wd, ln_i, final, after=None):
        modulate(l, si)
        TMP.reset()
        pre = pf.pop((l, si), None)
        if pre is None:
            BIG.reset()
            AH = BIG.alloc("AH", [128, 11, 1536], BF16); WD = BIG.alloc("WD", [128, 11, 1024], BF16)
            pslots = []
        else:
            AH, WD, pslots = pre
        sgt = [TMP.alloc("sg", [128, 512], F32) for _ in range(4)]
        ba = [[bf('ah%d_%d' % (j, tb)) for tb in range(3)] for j in range(11)]
        bwd = [bf('wd%d' % g) for g in range(3)]
        groups = FGROUPS
        wguv = wgu[l].rearrange("(c p) n -> p c n", p=128)
        wdv = wd[l].rearrange("(j p) n -> p j n", p=128)
        prc = 0
        for half in range(2):
            j0 = half * 11
            if not (half == 0 and pre is not None):
                for gi, (a, n) in enumerate(groups):
                    ld('pool', WD[:, a:a + n, :], wdv[:, j0 + a:j0 + a + n, :], [bwd[gi]])
            for gi, (a, n) in enumerate(groups):
                col0 = (j0 + a) * 128; ncol = n * 128
                if half == 0 and gi < len(pslots):
                    sg_, su_ = pslots[gi]
                else:
                    sg_ = next_ws(); su_ = next_ws()
                    ld('pool', WS[sg_][:, :, 0:ncol], wguv[:, :, col0:col0 + ncol], [bws[sg_]])
                    ld('pool', WS[su_][:, :, 0:ncol], wguv[:, :, FF + col0:FF + col0 + ncol], [bws[su_]])
                for jj in range(n):
                    j = a + jj
                    for tb in range(3):
                        pr = prc % 4; prc += 1
                        gps = ps[2 * pr]; ups = ps[2 * pr + 1]
                        for k in range(8):
                            mm(gps[:], WS[sg_][:, k, jj * 128:(jj + 1) * 128], HB[:, k, tbs(tb)], k == 0, k == 7, [bws[sg_], bh[k][tb]], [bps[2 * pr]])
                        for k in range(8):
                            mm(ups[:], WS[su_][:, k, jj * 128:(jj + 1) * 128], HB[:, k, tbs(tb)], k == 0, k == 7, [bws[su_], bh[k][tb]], [bps[2 * pr + 1]])
                        A('act', lambda e, pr=pr, gps=gps: e.activation(out=sgt[pr][:], in_=gps[:], func=AF.Silu), r=[bps[2 * pr]], w=[bf('sg%d' % pr)])
                        A('dve', lambda e, pr=pr, ups=ups, j=j, tb=tb: e.tensor_tensor(out=AH[:, j, tbs(tb)], in0=sgt[pr][:], in1=ups[:], op=ALU.mult),
                          r=[bf('sg%d' % pr), bps[2 * pr + 1]], w=[ba[j][tb]])
            for tb in range(3):
                v = 0 if tb == 0 else 1
                for m in range(8):
                    b = next_bank()
                    for j in range(11):
                        mm(ps[b][:], WD[:, j, m * 128:(m + 1) * 128], AH[:, j, tbs(tb)], j == 0, j == 10, [bwd[min(j // 4, 2)], ba[j][tb]], [bps[b]])
                    A('dve', lambda e, b=b, m=m, tb=tb, v=v: e.scalar_tensor_tensor(out=XRES[:, m, tbs(tb)], in0=ps[b][:], scalar=mod(l, 3 * si + 2, m, v),
                                                                                 in1=XRES[:, m, tbs(tb)], op0=ALU.mult, op1=ALU.add),
                      r=[bps[b], bx[m][tb], bf('mod')], w=[bx[m][tb]])
        if after is not None:
            after()
        layernorm(l, ln_i, final)
        fence()

    def store_y():
        TMP.reset()
        ystg = [TMP.alloc("ystg", [128, 1024], F32) for _ in range(2)]
        for t in range(12):
            tb = t // 4
            st = ystg[t % 2]; bst = bf('ystg%d' % (t % 2))
            for half in range(2):
                b = next_bank()
                for cc in range(4):
                    c = half * 4 + cc
                    tr(ps[b][:, cc * 128:(cc + 1) * 128], XRES[:, c, t * 128:(t + 1) * 128], identf[:], [bx[c][tb], bf('identf')], [bps[b]])
                if half == 0:
                    A('act', lambda e, b=b, st=st: e.activation(out=st[:, 0:512], in_=ps[b][:], func=AF.Copy), r=[bps[b]], w=[bst])
                else:
                    A('dve', lambda e, b=b, st=st: e.tensor_copy(out=st[:, 512:1024], in_=ps[b][:]), r=[bps[b]], w=[bst])
            outs.append(P.dma('sp', lambda e, st=st, t=t: e.dma_start(out=y[t * 128:(t + 1) * 128, :], in_=st[:]), r=[bst]))

    def rope(src3, dst3, H, ts, rset, rbufs, wbufs):
        r1, r2, n1, n2 = rset
        cosb = ropec[:, ts, :].unsqueeze(1).to_broadcast([128, H, 64])
        ssv = ropes[:, ts, :].rearrange("p (a b i) -> p a b i", a=2, b=2)
        r1v = r1[:, 0:H * 64].rearrange("p (h d) -> p h d", d=64)
        r2v = r2[:, 0:H * 64].rearrange("p (h a b i) -> p h a b i", a=2, b=2, i=16)
        s5 = src3.rearrange("p h (a b i) -> p h a b i", a=2, b=2)
        A('dve', lambda e: e.tensor_tensor(out=r1v, in0=src3, in1=cosb, op=ALU.mult), r=rbufs + [bf('rope')], w=[bf(n1)])
        for blk in range(2):
            A('dve', lambda e, blk=blk: e.tensor_tensor(out=r2v[:, :, :, blk, :], in0=s5[:, :, :, 1 - blk, :],
                                                        in1=ssv[:, :, blk, :].unsqueeze(1).to_broadcast([128, H, 2, 16]), op=ALU.mult),
              r=rbufs + [bf('rope')], w=[bf(n2)])
        A('pool', lambda e: e.tensor_tensor(out=dst3, in0=r1v, in1=r2[:, 0:H * 64].rearrange("p (h d) -> p h d", d=64), op=ALU.add),
          r=[bf(n1), bf(n2)], w=wbufs)

    def attn_unit(l, kind, segs, oc, tok0, T, qz_eng='pool', pool_share=False):
        pT, rec, t1, t2, sq, Qzs, accP = T
        S = [0, 1, 2, 3]
        acc = [(4, 5), (6, 7)]
        nq = sum(sg[5] for sg in segs)
        u = unitc[0] % 2
        unitc[0] += 1
        Qz = Qzs[u]
        for (KT, V, nkt, Qc, col0, ncols) in segs:
            for g in range(2):
                if qz_eng == 'pool':
                    A('pool', lambda e, g=g, Qc=Qc, col0=col0, ncols=ncols: e.tensor_copy(out=Qz[g][g * 64:(g + 1) * 64, col0:col0 + ncols], in_=Qc[g * 64:(g + 1) * 64, :]),
                      r=[bf('QT'), bf('qzero')], w=[bf('Qz%d_%d' % (u, g))])
                else:
                    A('act', lambda e, g=g, Qc=Qc, col0=col0, ncols=ncols: e.activation(out=Qz[g][g * 64:(g + 1) * 64, col0:col0 + ncols], in_=Qc[g * 64:(g + 1) * 64, :], func=AF.Copy),
                      r=[bf('QT'), bf('qzero')], w=[bf('Qz%d_%d' % (u, g))])
        its = [(si, kt, g) for si, sg in enumerate(segs) for kt in range(sg[2]) for g in range(2)]
        LOOK = 3

        def emit_s(i):
            si, kt, g = its[i]
            KT, V, nkt, Qc, col0, ncols = segs[si]
            kap, kb_ = KT(kt)
            sb = S[i % 4]; pi = i % 4
            mm(ps[sb][:, 0:ncols], kap, Qz[g][:, col0:col0 + ncols], True, True, kb_ + [bf('Qz%d_%d' % (u, g))], [bps[sb]])
            A('act', lambda e, sb=sb, pi=pi, ncols=ncols: e.activation(out=pT[pi][:, 0:ncols], in_=ps[sb][:, 0:ncols], func=AF.Exp, scale=0.125),
              r=[bps[sb]], w=[bf('pT%d' % pi)])

        def emit_pv(i):
            si, kt, g = its[i]
            KT, V, nkt, Qc, col0, ncols = segs[si]
            vap, vb_ = V(kt)
            pi = i % 4
            ob, db = acc[g]
            mm(ps[ob][:, col0:col0 + ncols], vap, pT[pi][:, 0:ncols], kt == 0, kt == nkt - 1, vb_ + [bf('pT%d' % pi)], [bps[ob]])
            if pool_share and g == 1 and kt % 2 == 1:
                if kt == 1:
                    A('pool', lambda e, pi=pi, col0=col0, ncols=ncols: e.tensor_copy(out=accP[:, col0:col0 + ncols], in_=pT[pi][:, 0:ncols]),
                      r=[bf('pT%d' % pi)], w=[bf('accP')])
                else:
                    A('pool', lambda e, pi=pi, col0=col0, ncols=ncols: e.tensor_tensor(out=accP[:, col0:col0 + ncols], in0=accP[:, col0:col0 + ncols],
                                                                                   in1=pT[pi][:, 0:ncols], op=ALU.add),
                      r=[bf('pT%d' % pi)], w=[bf('accP')])
            elif kt == 0:
                A('dve', lambda e, db=db, pi=pi, col0=col0, ncols=ncols: e.tensor_copy(out=ps[db][:, col0:col0 + ncols], in_=pT[pi][:, 0:ncols]),
                  r=[bf('pT%d' % pi)], w=[bps[db]])
            else:
                A('dve', lambda e, db=db, pi=pi, col0=col0, ncols=ncols: e.tensor_tensor(out=ps[db][:, col0:col0 + ncols], in0=ps[db][:, col0:col0 + ncols],
                                                                                         in1=pT[pi][:, 0:ncols], op=ALU.add),
                  r=[bf('pT%d' % pi)], w=[bps[db]])
        for i in range(min(LOOK, len(its))):
            emit_s(i)
        for i in range(len(its)):
            emit_pv(i)
            if i + LOOK < len(its):
                emit_s(i + LOOK)
        dst = MIXT[:, oc, tok0:tok0 + nq]
        DB = [1, 2]
        if pool_share:
            A('dve', lambda e: e.tensor_tensor(out=ps[acc[1][1]][:, 0:nq], in0=ps[acc[1][1]][:, 0:nq], in1=accP[:, 0:nq], op=ALU.add),
              r=[bf('accP')], w=[bps[acc[1][1]]])
        for g in range(2):
            db = acc[g][1]
            A('act', lambda e, g=g, db=db: e.activation(out=pT[g][:, 0:nq], in_=ps[db][:, 0:nq], func=AF.Copy), r=[bps[db]], w=[bf('pT%d' % g)])
            mm(ps[DB[g]][:, 0:nq], onesb[:], pT[g][:, 0:nq], True, True, [bf('onesb'), bf('pT%d' % g)], [bps[DB[g]]])
        if kind == 'A':
            for g in range(2):
                ob, db = acc[g]
                sl = slice(g * 64, (g + 1) * 64)
                A('dve', lambda e, g=g, sl=sl: e.reciprocal(out=rec[sl, 0:nq], in_=ps[DB[g]][sl, 0:nq]), r=[bps[DB[g]]], w=[bf('rec%d' % g)])
                A('dve', lambda e, ob=ob, sl=sl: e.tensor_tensor(out=MIXT[sl, oc, tok0:tok0 + nq], in0=ps[ob][sl, 0:nq], in1=rec[sl, 0:nq], op=ALU.mult),
                  r=[bps[ob], bf('rec%d' % g)], w=[bf('MIXT%d_%d' % (oc, tok0))])
        else:
            A('dve', lambda e: e.reciprocal(out=rec[:, 0:nq], in_=ps[1][:, 0:nq]), r=[bps[1]], w=[bf('rec0')])
            A('dve', lambda e: e.tensor_tensor(out=t1[:, 0:nq], in0=ps[4][:, 0:nq], in1=rec[:, 0:nq], op=ALU.mult), r=[bps[4], bf('rec0')], w=[bf('at1')])
            A('dve', lambda e: e.reciprocal(out=rec[:, 0:nq], in_=ps[2][:, 0:nq]), r=[bps[2], bf('at1')], w=[bf('rec0')])
            A('dve', lambda e: e.tensor_tensor(out=t2[:, 0:nq], in0=ps[6][:, 0:nq], in1=rec[:, 0:nq], op=ALU.mult), r=[bps[6], bf('rec0')], w=[bf('at2')])
            A('dve', lambda e: e.scalar_tensor_tensor(out=t2[:, 0:nq], in0=t2[:, 0:nq], scalar=neglam[:, l:l + 1], in1=t1[:, 0:nq], op0=ALU.mult, op1=ALU.add),
              r=[bf('at1'), bf('at2'), bf('neglam')], w=[bf('at2')])
            A('act', lambda e: e.activation(out=sq[:, 0:nq], in_=t2[:, 0:nq], func=AF.Square), r=[bf('at2')], w=[bf('asq')])
            mm(ps[0][:, 0:nq], onesb[:], sq[:, 0:nq], True, True, [bf('onesb'), bf('asq')], [bps[0]])
            A('act', lambda e: e.activation(out=t1[:, 0:nq], in_=ps[0][:, 0:nq], func=AF.Sqrt, bias=epsrms[:, 0:1], scale=1.0 / 128),
              r=[bps[0], bf('eps')], w=[bf('at1')])
            A('dve', lambda e: e.reciprocal(out=rec[:, 0:nq], in_=t1[:, 0:nq]), r=[bf('at1')], w=[bf('rec0')])
            A('dve', lambda e: e.tensor_tensor(out=t2[:, 0:nq], in0=t2[:, 0:nq], in1=rec[:, 0:nq], op=ALU.mult), r=[bf('rec0'), bf('at2')], w=[bf('at2')])
            A('act', lambda e: e.activation(out=dst, in_=t2[:, 0:nq], func=AF.Identity, scale=gsub[:, l:l + 1]), r=[bf('at2'), bf('gsub')], w=[bf('MIXT%d_%d' % (oc, tok0))])

    def mixbufs(k, tb):
        if k >= 6:
            return [bf('MIXF%d_%d' % (k - 6, tb))]
        return [bf('MIXT%d_%d' % (k, tb * 512))]

    def mix(l):
        modulate(l, 1)
        BIG.reset(); TMP.reset()
        QT = BIG.alloc("QT", [128, 6, 1536], BF16)
        WOUT = BIG.alloc("WOUT", [128, 8, 1024], BF16)
        WX = BIG.alloc("WX", [128, 8, 256], BF16)
        KTL = BIG.alloc("KTL", [128, 6, 512], BF16)
        VL = BIG.alloc("VL", [128, 4, 1024], BF16)
        tsq = TMP.alloc("tsq", [128, 384], F32); tn = tsq
        bufs['tn'] = bf('tsq')
        ss6 = TMP.alloc("ss6", [128, 6], F32); sd6 = TMP.alloc("sd6", [128, 6], F32); rs6 = TMP.alloc("rs6", [128, 6], F32)
        r1 = TMP.alloc("r1", [128, 512], F32); r2 = TMP.alloc("r2", [128, 512], F32)
        qk_tok = TMP.alloc("qk_tok", [128, 1536], F32); vu_tok = TMP.alloc("vu_tok", [128, 1024], BF16)
        kt_stage = TMP.alloc("kt_stage", [128, 6, 128], BF16)
        f32a = TMP.alloc("f32a", [128, 512], F32); f32b = f32a; vtmp = TMP.alloc("vtmp", [128, 128], F32)
        r2b = TMP.alloc("r2b", [128, 512], F32)
        rsets = [(r1, r2, 'r1', 'r2'), (f32a, r2b, 'f32a', 'r2b')]
        rcnt = [0]

        def nr():
            rcnt[0] += 1
            return rsets[rcnt[0] % 2]

        winv = w_in[l].rearrange("(c p) n -> p c n", p=128)
        if not pf.pop(('mix', l), False):
            for g in range(4):
                ld('pool', WS[g][:], winv[:, :, g * 512:(g + 1) * 512], [bws[g]])
            wsc[0] = 0
        ld('pool', WX[:], winv[:, :, 2048:2304], [bf('WX')])
        cktf = BIG.alloc("cktf", [128, 768], F32)
        cavv = cav[l].rearrange("(t p) n -> p t n", p=128)
        for kvh in range(2):
            for dup in range(2):
                o = kvh * 128 + dup * 64
                ld('pool', Vc[:, :, o:o + 64], cavv[:, :, kvh * 64:(kvh + 1) * 64], [bf('Vc')])
        ld('pool', Vc[:, :, 256:768], cbv[l].rearrange("(t p) n -> p t n", p=128), [bf('Vc')])
        for t in range(2):
            for kvh in range(2):
                for dup in range(2):
                    o = kvh * 128 + dup * 64
                    ld('pool', cktf[:, o:o + 64], cak[l, t * 128:(t + 1) * 128, kvh * 64:(kvh + 1) * 64], [bf('ckt')])
            ld('pool', cktf[:, 256:768], cbk[l, t * 128:(t + 1) * 128, :], [bf('ckt')])
            for half in range(2):
                b = next_bank()
                nchunk = 4 if half == 0 else 2
                for cc in range(nchunk):
                    c = half * 4 + cc
                    tr(ps[b][:, cc * 128:(cc + 1) * 128], cktf[:, c * 128:(c + 1) * 128], identf[:], [bf('ckt'), bf('identf')], [bps[b]])
                A('dve', lambda e, b=b, t=t, half=half, nchunk=nchunk: e.tensor_copy(
                    out=KTc[:, half * 4:half * 4 + nchunk, t * 128:(t + 1) * 128],
                    in_=ps[b][:, 0:nchunk * 128].rearrange("p (c t) -> p c t", t=128)), r=[bps[b]], w=[bf('KTc')])
        ld('pool', WOUT[:], w_out[l].rearrange("(c p) n -> p c n", p=128), [bf('WOUT')])
        ld('pool', wf[:], w_f[l].rearrange("(c p) n -> p c n", p=128), [bf('wf')])

        BMAP = {0: 0, 3: 1, 4: 2, 1: 3, 2: 4}
        tn3 = tn[:].rearrange("p (h d) -> p h d", d=64)
        qa_dst = qk_tok[:, 0:256].rearrange("p (h d) -> p h d", d=64)
        ka_dst = qk_tok[:, 768:1024].rearrange("p (k u d) -> p k u d", k=2, u=2)
        qb_dst = qk_tok[:, 256:768]; kb_dst = qk_tok[:, 1024:1536]

        def proj_mm(t, groups):
            tb = t // 4
            tsl = slice(t * 128, (t + 1) * 128)
            for g in groups:
                b = BMAP[g]
                ncol = 512 if g < 4 else 256
                for k in range(8):
                    rhs = WS[g][:, k, :] if g < 4 else WX[:, k, :]
                    mm(ps[b][:, 0:ncol], HB[:, k, tsl], rhs, k == 0, k == 7, [bh[k][tb], bws[g] if g < 4 else bf('WX')], [bps[b]])

        def post_a(t):
            prompt = t < 4
            b0, b3, b4 = BMAP[0], BMAP[3], BMAP[4]
            vdst = VL[:, t, :] if prompt else vu_tok[:]
            bvd = bf('VL') if prompt else bf('vu_tok')
            A('act', lambda e: e.activation(out=tsq[:], in_=ps[b0][:, 0:384], func=AF.Square), r=[bps[b0]], w=[bf('tsq')])
            A('dve', lambda e: e.tensor_reduce(out=ss6[:], in_=tsq[:].rearrange("p (h d) -> p h d", d=64), op=ALU.add, axis=AX.X), r=[bf('tsq')], w=[bf('ss6')])
            A('act', lambda e: e.activation(out=sd6[:], in_=ss6[:], func=AF.Sqrt, bias=epsrms[:, 0:1], scale=1.0 / 64), r=[bf('ss6'), bf('eps')], w=[bf('sd6')])
            A('dve', lambda e: e.reciprocal(out=rs6[:], in_=sd6[:]), r=[bf('sd6')], w=[bf('rs6')])
            A('dve', lambda e: e.tensor_tensor(out=tn3, in0=ps[b0][:, 0:384].rearrange("p (h d) -> p h d", d=64),
                                               in1=rs6[:].unsqueeze(2).to_broadcast([128, 6, 64]), op=ALU.mult), r=[bps[b0], bf('rs6')], w=[bf('tn')])
            A('dve', lambda e: e.tensor_tensor(out=tn3, in0=tn3, in1=gqk[:, l], op=ALU.mult), r=[bf('tn'), bf('gqk')], w=[bf('tn')])
            if prompt:
                outs.append(P.dma('sp', lambda e, t=t: e.dma_start(out=nak[l, t * 128:(t + 1) * 128, :], in_=tn[:, 256:384]), r=[bf('tn')]))
                A('act', lambda e: e.activation(out=vtmp[:], in_=ps[b0][:, 384:512], func=AF.Copy), r=[bps[b0]], w=[bf('vtmp')])
                outs.append(P.dma('sp', lambda e, t=t: e.dma_start(out=nav[l, t * 128:(t + 1) * 128, :], in_=vtmp[:]), r=[bf('vtmp')]))
                A('pool', lambda e: e.tensor_copy(out=qa_dst, in_=tn3[:, 0:4, :]), r=[bf('tn')], w=[bf('qk_tok')])
                for u in range(2):
                    A('pool', lambda e, u=u: e.tensor_copy(out=ka_dst[:, :, u, :], in_=tn3[:, 4:6, :]), r=[bf('tn')], w=[bf('qk_tok')])
            else:
                rope(tn3[:, 0:4, :], qa_dst, 4, t - 4, nr(), [bf('tn')], [bf('qk_tok')])
                for u in range(2):
                    rope(tn3[:, 4:6, :], ka_dst[:, :, u, :], 2, t - 4, nr(), [bf('tn')], [bf('qk_tok')])
            va_dst = vdst[:, 0:256].rearrange("p (k u d) -> p k u d", k=2, u=2)
            A('dve', lambda e, va_dst=va_dst: e.tensor_copy(
                out=va_dst, in_=ps[b0][:, 384:512].rearrange("p (k d) -> p k d", d=64).unsqueeze(2).to_broadcast([128, 2, 2, 64])),
              r=[bps[b0]], w=[bvd])
            if prompt:
                A('act', lambda e: e.activation(out=f32b[:], in_=ps[b3][:], func=AF.Copy), r=[bps[b3]], w=[bf('f32a')])
                outs.append(P.dma('sp', lambda e, t=t: e.dma_start(out=nbv[l, t * 128:(t + 1) * 128, :], in_=f32b[:]), r=[bf('f32a')]))
                A('pool', lambda e, vdst=vdst: e.tensor_copy(out=vdst[:, 256:768], in_=f32b[:]), r=[bf('f32a')], w=[bvd])
            else:
                A('act', lambda e, vdst=vdst: e.activation(out=vdst[:, 256:768], in_=ps[b3][:], func=AF.Copy), r=[bps[b3]], w=[bvd])
            A('dve', lambda e, vdst=vdst: e.tensor_copy(out=vdst[:, 768:1024], in_=ps[b4][:, 0:256]), r=[bps[b4]], w=[bvd])

        def post_b(t):
            prompt = t < 4
            b1, b2 = BMAP[1], BMAP[2]
            if prompt:
                A('act', lambda e: e.activation(out=qb_dst, in_=ps[b1][:], func=AF.Copy), r=[bps[b1]], w=[bf('qk_tok')])
                A('act', lambda e: e.activation(out=f32a[:], in_=ps[b2][:], func=AF.Copy), r=[bps[b2]], w=[bf('f32a')])
                outs.append(P.dma('sp', lambda e, t=t: e.dma_start(out=nbk[l, t * 128:(t + 1) * 128, :], in_=f32a[:]), r=[bf('f32a')]))
                A('pool', lambda e: e.tensor_copy(out=kb_dst, in_=f32a[:]), r=[bf('f32a')], w=[bf('qk_tok')])
            else:
                rope(ps[b1][:].rearrange("p (h d) -> p h d", d=64), qb_dst.rearrange("p (h d) -> p h d", d=64), 8, t - 4, nr(), [bps[b1]], [bf('qk_tok')])
                rope(ps[b2][:].rearrange("p (h d) -> p h d", d=64), kb_dst.rearrange("p (h d) -> p h d", d=64), 8, t - 4, nr(), [bps[b2]], [bf('qk_tok')])

        def trans(t):
            prompt = t < 4
            tsl = slice(t * 128, (t + 1) * 128)
            for grp in range(3):
                b = 5 + grp
                for cc in range(4):
                    c = grp * 4 + cc
                    tr(ps[b][:, cc * 128:(cc + 1) * 128], qk_tok[:, c * 128:(c + 1) * 128], identf[:], [bf('qk_tok'), bf('identf')], [bps[b]])
                src = ps[b][:].rearrange("p (c t) -> p c t", t=128)
                if grp == 0:
                    A('act', lambda e, src=src, tsl=tsl: e.activation(out=QT[:, 0:4, tsl], in_=src, func=AF.Copy), r=[bps[b]], w=[bf('QT')])
                elif grp == 1:
                    A('dve', lambda e, src=src, tsl=tsl: e.tensor_copy(out=QT[:, 4:6, tsl], in_=src[:, 0:2, :]), r=[bps[b]], w=[bf('QT')])
                    kd = KTL[:, 0:2, tsl] if prompt else kt_stage[:, 0:2, :]
                    A('dve', lambda e, src=src, kd=kd: e.tensor_copy(out=kd, in_=src[:, 2:4, :]), r=[bps[b]], w=[bf('KTL') if prompt else bf('kt_stage')])
                else:
                    kd = KTL[:, 2:6, tsl] if prompt else kt_stage[:, 2:6, :]
                    A('act', lambda e, src=src, kd=kd: e.activation(out=kd, in_=src, func=AF.Copy), r=[bps[b]], w=[bf('KTL') if prompt else bf('kt_stage')])
            if not prompt:
                ts = t - 4
                for h in range(2):
                    P.dma('sp', lambda e, ts=ts, h=h: e.dma_start(out=gin_kt[h].rearrange("(c p) n -> p c n", p=128)[:, :, ts * 128:(ts + 1) * 128],
                                                                  in_=kt_stage[:, 3 * h:3 * h + 3, :]),
                          r=[bf('kt_stage')], w=[bf('gin_kt%d' % h)])
                hh, t4 = ts // 4, ts % 4
                P.dma('sp', lambda e, hh=hh, t4=t4: e.dma_start(out=gin_vu[hh][t4 * 128:(t4 + 1) * 128, :], in_=vu_tok[:]),
                      r=[bf('vu_tok')], w=[bf('gin_vu%d' % hh)])

        proj_mm(0, (0, 3, 4)); proj_mm(0, (1, 2))
        for t in range(12):
            post_a(t)
            if t + 1 < 12:
                proj_mm(t + 1, (0, 3, 4))
            post_b(t)
            trans(t)
            if t + 1 < 12:
                proj_mm(t + 1, (1, 2))
        fence()
        for h in range(2):
            cc = P.dma('pool', lambda e, h=h: e.collective_compute("AllGather", ALU.bypass, replica_groups=RG, ins=[gin_kt[h].opt()], outs=[gout_kt[h].opt()]),
                       r=[bf('gin_kt%d' % h)], w=[bf('gout_kt%d' % h)], inc=1)
            cc.own = ('cc', 4 * l + h)
            cc = P.dma('pool', lambda e, h=h: e.collective_compute("AllGather", ALU.bypass, replica_groups=RG, ins=[gin_vu[h].opt()], outs=[gout_vu[h].opt()]),
                       r=[bf('gin_vu%d' % h)], w=[bf('gout_vu%d' % h)], inc=1)
            cc.own = ('cc', 4 * l + 2 + h)
        gob = [bf('gout_kt0'), bf('gout_kt1'), bf('gout_vu0'), bf('gout_vu1')]
        WSr.reset(); TMP.reset()
        ktj = [WSr.alloc("ktj", [128, 4, 1024], BF16) for _ in range(2)]
        vj = [WSr.alloc("vj", [128, 32, 128], BF16) for _ in range(2)]
        pT = [TMP.alloc("pT", [128, 512], BF16) for _ in range(4)]
        T = (pT, TMP.alloc("rec", [128, 512], F32), TMP.alloc("at1", [128, 512], F32), TMP.alloc("at2", [128, 512], F32),
             TMP.alloc("asq", [128, 512], BF16),
             [[TMP.alloc("Qz", [128, 512], BF16) for _ in range(2)] for _ in range(2)],
             TMP.alloc("accP", [128, 512], F32))
        for u_ in range(2):
            for g_ in range(2):
                A('dve', lambda e, u_=u_, g_=g_: e.memset(T[5][u_][g_][:], 0.0), w=[bf('qzero')])
        for job in range(6):
            kind = 'A' if job < 2 else 'B'
            vcol = job * 128 if job < 2 else 256 + (job - 2) * 128
            segs = []
            for s_ in range(2):
                KTf = lambda kt, s_=s_, job=job: (KTL[:, job, s_ * 256 + kt * 128:s_ * 256 + (kt + 1) * 128], [bf('KTL')])
                Vf = lambda kt, s_=s_, vcol=vcol: (VL[:, s_ * 2 + kt, vcol:vcol + 128], [bf('VL')])
                segs.append((KTf, Vf, 2, QT[:, job, s_ * 256:(s_ + 1) * 256], s_ * 256, 256))
            attn_unit(l, kind, segs, job, 0, T, qz_eng='act')
        P.add('pool', lambda e: e.nop(), gob, gob)
        gk = [g_.rearrange("(r c p) n -> p r c n", c=3, p=128) for g_ in gout_kt]
        gv = [g_.rearrange("(t p) n -> p t n", p=128) for g_ in gout_vu]
        for job in range(6):
            kind = 'A' if job < 2 else 'B'
            vcol = job * 128 if job < 2 else 256 + (job - 2) * 128
            sl = job % 2
            ld('pool', ktj[sl][:], gk[job // 3][:, :, job % 3, :], [bf('ktj%d' % sl)], r=gob)
            for h in range(2):
                ld('pool', vj[sl][:, 16 * h:16 * h + 16, :], gv[h][:, :, vcol:vcol + 128], [bf('vj%d_%d' % (sl, h))], r=gob)

            def KTf(kt, job=job, sl=sl):
                if kt < 2:
                    return KTc[:, job, kt * 128:(kt + 1) * 128], [bf('KTc')]
                k2 = kt - 2
                h, r, t4 = k2 // 16, (k2 % 16) // 4, k2 % 4
                t = h * 4 + t4
                return ktj[sl][:, r, t * 128:(t + 1) * 128], [bf('ktj%d' % sl)]

            def Vf(kt, vcol=vcol, sl=sl):
                if kt < 2:
                    return Vc[:, kt, vcol:vcol + 128], [bf('Vc')]
                return vj[sl][:, kt - 2, :], [bf('vj%d_%d' % (sl, (kt - 2) // 16))]
            for qb in range(2):
                tok0 = 512 + qb * 512
                attn_unit(l, kind, [(KTf, Vf, 34, QT[:, job, tok0:tok0 + 512], 0, 512)], job, tok0, T, pool_share=True)
        fence()
        if MIXSUB <= 4:
            return
        WSr.reset(); TMP.reset()
        tabC = [WSr.alloc("tabC", [128, 4, 1024], BF16) for _ in range(2)]
        tabS = [WSr.alloc("tabS", [128, 4, 1024], BF16) for _ in range(2)]
        ug = TMP.alloc("ug", [128, 32, 256], BF16)
        XT = nc.alloc_sbuf_tensor_at("XT_%d" % l, [128, 2, 1536], BF16, offset=BIG.base)
        YT = nc.alloc_sbuf_tensor_at("YT_%d" % l, [128, 2, 1536], BF16, offset=BIG.base + 6144)
        FT = nc.alloc_sbuf_tensor_at("FT_%d" % l, [128, 2, 1536], BF16, offset=BIG.base + 12288)
        for h in range(2):
            ld('pool', ug[:, 16 * h:16 * h + 16, :], gv[h][:, :, 768:1024], [bf('ug%d' % h)], r=gob)
        CLv = CLd.rearrange("(t p) n -> p t n", p=128); SLv = SLd.rearrange("(t p) n -> p t n", p=128)
        for grp in range(8):
            sl = grp % 2
            ld('pool', tabC[sl][:], CLv[:, grp * 4:(grp + 1) * 4, :], [bf('tabC%d' % sl)])
            ld('pool', tabS[sl][:], SLv[:, grp * 4:(grp + 1) * 4, :], [bf('tabS%d' % sl)])
            for fc in range(2):
                for lb in range(2):
                    for ti, (tab, bt) in enumerate(((tabC[sl], 'tabC%d' % sl), (tabS[sl], 'tabS%d' % sl))):
                        b = ti * 4 + fc * 2 + lb
                        for i in range(4):
                            ui = (grp % 2) * 16 + (grp // 2) * 4 + i
                            mm(ps[b][:], ug[:, ui, fc * 128:(fc + 1) * 128], tab[:, i, lb * 512:(lb + 1) * 512],
                               grp == 0 and i == 0, grp == 7 and i == 3, [bf('ug%d' % (grp % 2)), bf(bt)], [bps[b]])
        for fc in range(2):
            for lb in range(2):
                A('act', lambda e, fc=fc, lb=lb: e.activation(out=XT[:, fc, 512 + lb * 512:1024 + lb * 512], in_=ps[fc * 2 + lb][:], func=AF.Copy),
                  r=[bps[fc * 2 + lb]], w=[bf('XT%d_%d' % (fc, 1 + lb))])
                A('dve', lambda e, fc=fc, lb=lb: e.tensor_copy(out=YT[:, fc, 512 + lb * 512:1024 + lb * 512], in_=ps[4 + fc * 2 + lb][:]),
                  r=[bps[4 + fc * 2 + lb]], w=[bf('YT%d_%d' % (fc, 1 + lb))])
        for s in range(2):
            for fc in range(2):
                bX = next_bank(); bY = next_bank()
                for lt in range(2):
                    mm(ps[bX][:, 0:256], VL[:, s * 2 + lt, 768 + fc * 128:768 + (fc + 1) * 128], C256[:, lt, :], lt == 0, lt == 1, [bf('VL'), bf('C256')], [bps[bX]])
                for lt in range(2):
                    mm(ps[bY][:, 0:256], VL[:, s * 2 + lt, 768 + fc * 128:768 + (fc + 1) * 128], S256[:, lt, :], lt == 0, lt == 1, [bf('VL'), bf('S256')], [bps[bY]])
                A('act', lambda e, s=s, fc=fc, bX=bX: e.activation(out=XT[:, fc, s * 256:(s + 1) * 256], in_=ps[bX][:, 0:256], func=AF.Copy), r=[bps[bX]], w=[bf('XT%d_0' % fc)])
                A('dve', lambda e, s=s, fc=fc, bY=bY: e.tensor_copy(out=YT[:, fc, s * 256:(s + 1) * 256], in_=ps[bY][:, 0:256]), r=[bps[bY]], w=[bf('YT%d_0' % fc)])
        for tb in range(3):
            ti = 2 if tb == 0 else 0
            for fc in range(2):
                b = next_bank()
                mm(ps[b][:], BCS[:, ti, :], XT[:, fc, tbs(tb)], True, False, [bf('BCS'), bf('XT%d_%d' % (fc, tb))], [bps[b]])
                mm(ps[b][:], BCS[:, ti + 1, :], YT[:, fc, tbs(tb)], False, True, [bf('BCS'), bf('YT%d_%d' % (fc, tb))], [bps[b]])
                A('act' if fc == 0 else 'dve',
                  (lambda e, b=b, fc=fc, tb=tb: e.activation(out=FT[:, fc, tbs(tb)], in_=ps[b][:], func=AF.Copy)) if fc == 0 else
                  (lambda e, b=b, fc=fc, tb=tb: e.tensor_copy(out=FT[:, fc, tbs(tb)], in_=ps[b][:])), r=[bps[b]], w=[bf('FT%d_%d' % (fc, tb))])
            for oc in range(2):
                b = next_bank()
                for fc in range(2):
                    mm(ps[b][:], wf[:, fc, oc * 128:(oc + 1) * 128], FT[:, fc, tbs(tb)], fc == 0, fc == 1, [bf('wf'), bf('FT%d_%d' % (fc, tb))], [bps[b]])
                A('act' if oc == 0 else 'dve',
                  (lambda e, b=b, oc=oc, tb=tb: e.activation(out=MIXT[:, 6 + oc, tbs(tb)], in_=ps[b][:], func=AF.Copy)) if oc == 0 else
                  (lambda e, b=b, oc=oc, tb=tb: e.tensor_copy(out=MIXT[:, 6 + oc, tbs(tb)], in_=ps[b][:])), r=[bps[b]], w=[bf('MIXF%d_%d' % (oc, tb))])
        if MIXDBG:
            for tb in range(3):
                for c in range(8):
                    A('dve', lambda e, c=c, tb=tb: e.tensor_copy(out=XRES[:, c, tbs(tb)], in_=MIXT[:, c, tbs(tb)]), r=mixbufs(c, tb) + [bx[c][tb]], w=[bx[c][tb]])
            fence()
            WSr.reset()
            for i in range(4):
                WS[i] = WSr.alloc("WS%d" % i, [128, 8, 512], BF16)
            return
        for tb in range(3):
            v = 0 if tb == 0 else 1
            for m in range(8):
                b = next_bank()
                for k in range(8):
                    mm(ps[b][:], WOUT[:, k, m * 128:(m + 1) * 128], MIXT[:, k, tbs(tb)], k == 0, k == 7, [bf('WOUT')] + mixbufs(k, tb), [bps[b]])
                A('dve', lambda e, b=b, m=m, tb=tb, v=v: e.scalar_tensor_tensor(out=XRES[:, m, tbs(tb)], in0=ps[b][:], scalar=mod(l, 5, m, v),
                                                                             in1=XRES[:, m, tbs(tb)], op0=ALU.mult, op1=ALU.add),
                  r=[bps[b], bx[m][tb], bf('mod')], w=[bx[m][tb]])
        fence()
        WSr.reset()
        for i in range(4):
            WS[i] = WSr.alloc("WS%d" % i, [128, 8, 512], BF16)
        ffn_prefetch(l, 2, w2gu, w2d)
        layernorm(l, 1, False)
        fence()

    ffn_prefetch(0, 0, w1gu, w1d)
    fence()
    done = False
    for l in range(2):
        if stage <= 4 * l + 0:
            break
        ffn(l, 0, w1gu, w1d, 0, stage == 4 * l + 1, after=(lambda l=l: mix_prefetch(l)))
        if stage <= 4 * l + 1:
            break
        mix(l)
        if stage <= 4 * l + 2:
            break
        ffn(l, 2, w2gu, w2d, 2, l == 1 or stage == 4 * l + 3, after=((lambda: ffn_prefetch(1, 0, w1gu, w1d)) if l == 0 else None))
        if stage <= 4 * l + 3:
            break
    store_y()
    fin = P.add('sp', lambda e: e.nop(), (), ())
    fin.deps = [o for o in outs if o is not None]
    P.plan()
    sems = {e: es.enter_context(nc.semaphore("s_" + e)) for e in ENGS}
    dsems = {}
    for e in ('pool', 'sp'):
        for i in range(P.n_dma_sems):
            dsems[(e, i)] = es.enter_context(nc.semaphore("d_%s%d" % (e, i)))
    for i in range(9):
        dsems[('cc', i)] = es.enter_context(nc.semaphore("cc%d" % i))
    block = es.enter_context(nc.Block())
    P.emit(nc, block, sems, dsems)
    es.close()
    return nc


_NC_CACHE = {}


def _rope_tables(length):
    rows = length // 64
    row = np.repeat(np.arange(rows), 64).astype(np.float32)
    col = np.tile(np.arange(64), rows).astype(np.float32)
    inv = (1.0 / (10000.0 ** (np.arange(0, 32, 2, dtype=np.float32) / 32.0))).astype(np.float32)
    ar = row[:, None] * inv
    ac = col[:, None] * inv
    cos = np.concatenate([np.cos(ar), np.cos(ar), np.cos(ac), np.cos(ac)], -1).astype(np.float32)
    sin = np.concatenate([np.sin(ar), np.sin(ar), np.sin(ac), np.sin(ac)], -1).astype(np.float32)
    sgn = np.where((np.arange(64) % 32) < 16, -1.0, 1.0).astype(np.float32)
    return cos, sin * sgn


def _consts():
    bfl = ml_dtypes.bfloat16
    l4 = np.arange(4096, dtype=np.int64)
    ph = (l4[:, None] * l4[None, :]) % 4096
    ang = ph.astype(np.float64) * (2.0 * np.pi / 4096)
    CL = np.cos(ang).astype(bfl)
    SL = np.sin(ang).astype(bfl)
    l2 = np.arange(256, dtype=np.int64)
    a2 = ((l2[:, None] * l2[None, :]) % 256).astype(np.float64) * (2.0 * np.pi / 256)
    C256 = np.cos(a2).astype(bfl)
    S256 = np.sin(a2).astype(bfl)
    c = np.arange(64, dtype=np.int64)
    a3 = ((c[:, None] * c[None, :]) % 64).astype(np.float64) * (2.0 * np.pi / 64)
    C64 = np.cos(a3)
    S64 = np.sin(a3)
    z = np.zeros((64, 64))
    bd = lambda m: np.block([[m, z], [z, m]])
    BCS = np.stack([bd(C64) / 512.0, -bd(S64) / 512.0, bd(C64) / 128.0, -bd(S64) / 128.0]).astype(bfl)
    return CL, SL, C256, S256, BCS


def kernel(x_prompt, x_sample, cache_a_k, cache_a_v, cache_b_k, cache_b_v, c, c_ctx,
           w_mod, b_mod, w_in, g_qa, g_ka, lam_q1, lam_k1, lam_q2, lam_k2, g_subln,
           w_fourier, w_out, w_ffn1_gu, w_ffn1_down, w_ffn2_gu, w_ffn2_down, ln_g, ln_b, _stage=99):
    f32 = np.float32
    A_ = lambda a: np.ascontiguousarray(np.asarray(a, dtype=f32))
    x_prompt = A_(x_prompt); x_sample = A_(x_sample)
    cache_a_k = A_(cache_a_k); cache_a_v = A_(cache_a_v); cache_b_k = A_(cache_b_k); cache_b_v = A_(cache_b_v)
    c = A_(c); c_ctx = A_(c_ctx)
    if _stage not in _NC_CACHE:
        _NC_CACHE[_stage] = build(_stage)
    nc = _NC_CACHE[_stage]
    CL, SL, C256, S256, BCS = _consts()
    cos, sins = _rope_tables(4096)
    bfl = ml_dtypes.bfloat16
    shared = {
        "w_in": A_(w_in), "w_out": A_(w_out), "w_f": A_(w_fourier),
        "w1gu": A_(w_ffn1_gu), "w1d": A_(w_ffn1_down), "w2gu": A_(w_ffn2_gu), "w2d": A_(w_ffn2_down),
        "lnT": np.ascontiguousarray(np.stack([A_(ln_g), A_(ln_b)]).reshape(2, 2, 3, 8, 128).transpose(4, 0, 1, 2, 3)),
        "gqk": np.ascontiguousarray(np.broadcast_to(
            np.concatenate([np.repeat(A_(g_qa)[:, None, :], 4, 1), np.repeat(A_(g_ka)[:, None, :], 2, 1)], 1)[None], (128, 2, 6, 64))),
        "gsubT": np.ascontiguousarray(A_(g_subln).T),
        "lamv": np.ascontiguousarray(np.broadcast_to(np.stack([A_(lam_q1), A_(lam_k1), A_(lam_q2), A_(lam_k2)], 1)[None], (128, 2, 4, 64))),
        "C256": C256, "S256": S256, "BCS": BCS,
        "identf": np.eye(128, dtype=f32), "identb": np.eye(128, dtype=f32).astype(bfl), "onesb": np.ones((128, 128), dtype=bfl),
    }
    w_mod_f = A_(w_mod)
    bmodT_full = A_(b_mod).reshape(2, 72, 128).transpose(2, 0, 1)
    in_maps = []
    for i in range(8):
        b, r = i // 4, i % 4
        xin = np.concatenate([x_prompt[2 * i], x_prompt[2 * i + 1], x_sample[b, r * 1024:(r + 1) * 1024]], 0)
        cv = np.stack([c_ctx, c[0], c[1]], 0)
        m = dict(shared)
        m["xin"] = np.ascontiguousarray(xin)
        m["cvT"] = np.ascontiguousarray(cv.reshape(3, 8, 128).transpose(2, 1, 0))
        m["wmod"] = np.ascontiguousarray(w_mod_f[:, :, 2304 * r:2304 * (r + 1)])
        m["bmodT"] = np.ascontiguousarray(bmodT_full[:, :, 18 * r:18 * (r + 1)])
        selv = np.zeros((128, 2), f32); selv[:, b] = 1.0
        m["sel"] = selv
        m["ropec"] = np.ascontiguousarray(cos[r * 1024:(r + 1) * 1024].reshape(8, 128, 64).transpose(1, 0, 2))
        m["ropes"] = np.ascontiguousarray(sins[r * 1024:(r + 1) * 1024].reshape(8, 128, 64).transpose(1, 0, 2))
        m["cak"] = np.ascontiguousarray(cache_a_k[b].reshape(2, 256, 128)); m["cav"] = np.ascontiguousarray(cache_a_v[b].reshape(2, 256, 128))
        m["cbk"] = np.ascontiguousarray(cache_b_k[b].reshape(2, 256, 512)); m["cbv"] = np.ascontiguousarray(cache_b_v[b].reshape(2, 256, 512))
        m["CL"] = np.ascontiguousarray(CL[:, r * 1024:(r + 1) * 1024]); m["SL"] = np.ascontiguousarray(SL[:, r * 1024:(r + 1) * 1024])
        in_maps.append(m)
    res = run_bass_kernel_spmd(nc, in_maps[:NCORES], core_ids=list(range(NCORES)))
    y_prompt = np.zeros((16, 256, 1024), f32); y_sample = np.zeros((2, 4096, 1024), f32)
    nak = np.zeros((16, 2, 256, 2, 64), f32); nav = np.zeros((16, 2, 256, 2, 64), f32)
    nbk = np.zeros((16, 2, 256, 4, 128), f32); nbv = np.zeros((16, 2, 256, 4, 128), f32)
    for i in range(NCORES):
        b, r = i // 4, i % 4
        o = res.results[i]
        yy = np.asarray(o["y"], dtype=f32)
        y_prompt[2 * i] = yy[0:256]; y_prompt[2 * i + 1] = yy[256:512]
        y_sample[b, r * 1024:(r + 1) * 1024] = yy[512:1536]
        for s in range(2):
            for l in range(2):
                nak[2 * i + s, l] = np.asarray(o["nak"])[l, s * 256:(s + 1) * 256].reshape(256, 2, 64)
                nav[2 * i + s, l] = np.asarray(o["nav"])[l, s * 256:(s + 1) * 256].reshape(256, 2, 64)
                nbk[2 * i + s, l] = np.asarray(o["nbk"])[l, s * 256:(s + 1) * 256].reshape(256, 4, 128)
                nbv[2 * i + s, l] = np.asarray(o["nbv"])[l, s * 256:(s + 1) * 256].reshape(256, 4, 128)
    return (y_prompt, y_sample, nak, nav, nbk, nbv)
```

```python
import math
import numpy as np
import ml_dtypes
from contextlib import ExitStack
import concourse.bass as bass
import concourse.mybir as mybir
from concourse.bass_utils import run_bass_kernel_spmd

F32 = mybir.dt.float32
BF16 = mybir.dt.bfloat16
AF = mybir.ActivationFunctionType
ALU = mybir.AluOpType
AX = mybir.AxisListType

ENGS = ('pe', 'act', 'dve', 'pool', 'sp')


class Buf:
    __slots__ = ('name', 'w', 'r')

    def __init__(self, name=''):
        self.name = name
        self.w = None
        self.r = []


class Op:
    __slots__ = ('eng', 'fn', 'deps', 'idx', 'signal', 'count', 'is_dma', 'sem', 'val',
                 'waits', 'known_after', 'id', 'inc', 'own')

    def __init__(self, eng, fn, is_dma, inc=16):
        self.eng = eng
        self.fn = fn
        self.is_dma = is_dma
        self.deps = []
        self.signal = False
        self.count = 0
        self.sem = None
        self.val = 0
        self.waits = []
        self.known_after = None
        self.inc = inc
        self.own = None


class Prog:
    def __init__(self, n_dma_sems=12):
        self.ops = {e: [] for e in ENGS}
        self.all = []
        self.n_dma_sems = n_dma_sems

    def add(self, eng, fn, r=(), w=(), dma=False, inc=16):
        op = Op(eng, fn, dma, inc)
        op.id = len(self.all)
        deps = {}
        for b in r:
            if b.w is not None:
                deps[b.w.id] = b.w
        for b in w:
            if b.w is not None:
                deps[b.w.id] = b.w
            for x in b.r:
                deps[x.id] = x
        for b in r:
            b.r.append(op)
        for b in w:
            b.w = op
            b.r = []
        op.deps = list(deps.values())
        op.idx = len(self.ops[eng])
        self.ops[eng].append(op)
        self.all.append(op)
        return op

    def dma(self, queue, fn, r=(), w=(), inc=16):
        return self.add(queue, fn, r, w, dma=True, inc=inc)

    def barrier(self, bufs):
        pass

    def plan(self):
        known = {e: {} for e in ENGS}
        waited_dma = {e: set() for e in ENGS}
        ring = {e: [None] * self.n_dma_sems for e in ENGS}
        ring_pos = {e: 0 for e in ENGS}
        ring_val = {e: [0] * self.n_dma_sems for e in ENGS}
        for op in self.all:
            e = op.eng
            kn = known[e]
            waits = []
            changed = False
            for d in sorted(op.deps, key=lambda d: (not d.is_dma and d.eng == e)):
                if d.is_dma:
                    if d.id in waited_dma[e]:
                        continue
                    waited_dma[e].add(d.id)
                    waits.append(('dma', d))
                    if d.known_after:
                        for k, v in d.known_after.items():
                            if kn.get(k, -1) < v:
                                if not changed:
                                    kn = dict(kn)
                                    changed = True
                                kn[k] = v
                else:
                    if d.eng == e and e == 'pe' and not op.is_dma:
                        continue
                    if kn.get(d.eng, -1) >= d.idx:
                        continue
                    d.signal = True
                    waits.append(('cmp', d))
                    if not changed:
                        kn = dict(kn)
                        changed = True
                    kn[d.eng] = d.idx
                    if d.known_after:
                        for k, v in d.known_after.items():
                            if kn.get(k, -1) < v:
                                kn[k] = v
            if op.is_dma and op.own is not None:
                op.sem = op.own
                op.val = op.inc
            elif op.is_dma:
                pos = ring_pos[e]
                prev = ring[e][pos]
                if prev is not None and prev.id not in waited_dma[e]:
                    waited_dma[e].add(prev.id)
                    waits.append(('dma', prev))
                ring[e][pos] = op
                ring_val[e][pos] += op.inc
                op.sem = (e, pos)
                op.val = ring_val[e][pos]
                ring_pos[e] = (pos + 1) % self.n_dma_sems
            known[e] = kn
            op.waits = waits
            op.known_after = kn
        for e in ENGS:
            c = 0
            for op in self.ops[e]:
                if op.is_dma:
                    continue
                if op.signal:
                    c += 1
                    op.count = c

    def emit(self, nc, block, sems, dma_sems):
        engobj = {'pe': block.tensor, 'act': block.scalar, 'dve': block.vector,
                  'pool': block.gpsimd, 'sp': block.sync}
        for e in ENGS:
            ops = self.ops[e]
            if not ops:
                continue

            def body(eng, ops=ops, e=e):
                for op in ops:
                    wl = {}
                    for kind, d in op.waits:
                        if kind == 'dma':
                            s = dma_sems[d.sem]
                            v = d.val
                        else:
                            s = sems[d.eng]
                            v = d.count
                        key = id(s)
                        if key not in wl or wl[key][1] < v:
                            wl[key] = (s, v)
                    for s, v in wl.values():
                        eng.wait_ge(s, v)
                    ins = op.fn(eng)
                    if op.is_dma:
                        ins.then_inc(dma_sems[op.sem], op.inc)
                    elif op.signal:
                        ins.then_inc(sems[e], 1)
            engobj[e](body)

    def final_waits(self, ops):
        return ops

U8 = mybir.dt.uint8
ALPHA = (2.0 * 2) ** 0.25
LN_EPS = 1e-5
RMS_EPS = 1e-6
FF = 2816
LAM_INIT = [0.8 - 0.6 * math.exp(-0.3 * l) for l in range(2)]
RG = [[0, 1, 2, 3], [4, 5, 6, 7]]
MIXSUB = 9
NCORES = 8
PSUB = 99
GSEL = -1
DBGLIM = -1
MIXDBG = 0


def build(stage=99):
    nc = bass.Bass("TRN2", target_bir_lowering=False)

    def din(name, shape, dt=F32):
        return nc.dram_tensor(name, list(shape), dt, kind="ExternalInput").ap()

    def dout(name, shape, dt=F32):
        return nc.dram_tensor(name, list(shape), dt, kind="ExternalOutput").ap()

    xin = din("xin", [1536, 1024]); cvT_d = din("cvT", [128, 8, 3]); wmod = din("wmod", [2, 1024, 2304]); sel_d = din("sel", [128, 2])
    bmodT_d = din("bmodT", [128, 2, 18])
    w_in = din("w_in", [2, 1024, 2304]); w_out = din("w_out", [2, 1024, 1024]); w_f = din("w_f", [2, 256, 256])
    w1gu = din("w1gu", [2, 1024, 5632]); w1d = din("w1d", [2, 2816, 1024])
    w2gu = din("w2gu", [2, 1024, 5632]); w2d = din("w2d", [2, 2816, 1024])
    lnT_d = din("lnT", [128, 2, 2, 3, 8]); gqk_d = din("gqk", [128, 2, 6, 64]); gsubT_d = din("gsubT", [128, 2])
    lamv_d = din("lamv", [128, 2, 4, 64]); ropec_d = din("ropec", [128, 8, 64]); ropes_d = din("ropes", [128, 8, 64])
    cak = din("cak", [2, 256, 128]); cav = din("cav", [2, 256, 128]); cbk = din("cbk", [2, 256, 512]); cbv = din("cbv", [2, 256, 512])
    CLd = din("CL", [4096, 1024], BF16); SLd = din("SL", [4096, 1024], BF16)
    C256d = din("C256", [256, 256], BF16); S256d = din("S256", [256, 256], BF16); BCSd = din("BCS", [4, 128, 128], BF16)
    identf_d = din("identf", [128, 128]); identb_d = din("identb", [128, 128], BF16); onesb_d = din("onesb", [128, 128], BF16)
    y = dout("y", [1536, 1024]); nak = dout("nak", [2, 512, 128]); nav = dout("nav", [2, 512, 128])
    nbk = dout("nbk", [2, 512, 512]); nbv = dout("nbv", [2, 512, 512])
    gin_kt = [nc.dram_tensor("gin_kt%d" % h, [384, 1024], BF16).ap() for h in range(2)]
    gout_kt = [nc.dram_tensor("gout_kt%d" % h, [1536, 1024], BF16).ap() for h in range(2)]
    gin_vu = [nc.dram_tensor("gin_vu%d" % h, [512, 1024], BF16).ap() for h in range(2)]
    gout_vu = [nc.dram_tensor("gout_vu%d" % h, [2048, 1024], BF16).ap() for h in range(2)]

    gin_mod = nc.dram_tensor("gin_mod", [128, 108], F32).ap()
    gout_mod = nc.dram_tensor("gout_mod", [512, 108], F32).ap()
    P = Prog()
    dbg = {'on': False, 'n': 0}
    _orig_add = P.add

    def _add_dbg(eng, fn, r=(), w=(), dma=False, inc=16):
        if dbg['on'] and DBGLIM >= 0:
            dbg['n'] += 1
            if dbg['n'] > DBGLIM:
                return None
        return _orig_add(eng, fn, r, w, dma, inc)
    P.add = _add_dbg
    es = ExitStack()
    base0 = (nc.sbuf_base + 63) // 64 * 64
    avail = nc.sbuf_top - base0 - 64
    arena = es.enter_context(nc.sbuf_tensor("arena", [128, avail], U8))
    cur = [base0]
    lim = base0 + avail

    def esz(dt):
        return 4 if dt == F32 else 2

    def alloc(name, shape, dt, at=None):
        nb = int(np.prod(shape[1:])) * esz(dt)
        nb = (nb + 63) // 64 * 64
        if at is None:
            o = cur[0]
            cur[0] += nb
            assert cur[0] <= lim, (name, cur[0], lim)
        else:
            o = at
        t = nc.alloc_sbuf_tensor_at(name + "_%d" % o, list(shape), dt, offset=o)
        return t

    class Region:
        def __init__(self, size):
            self.base = cur[0]
            self.size = size
            cur[0] += size
            assert cur[0] <= lim, ("region", cur[0], lim)
            self.p = self.base
            self.n = 0

        def reset(self):
            self.p = self.base

        def alloc(self, name, shape, dt):
            nb = int(np.prod(shape[1:])) * esz(dt)
            nb = (nb + 63) // 64 * 64
            o = self.p
            self.p += nb
            assert self.p <= self.base + self.size, (name, self.p - self.base, self.size)
            self.n += 1
            return nc.alloc_sbuf_tensor_at("%s_r%d" % (name, self.n), list(shape), dt, offset=o)

    XRES = alloc("XRES", [128, 8, 1536], F32)
    HB = alloc("HB", [128, 8, 1536], BF16)
    WSr = Region(32768)
    WS = [WSr.alloc("WS%d" % i, [128, 8, 512], BF16) for i in range(4)]
    BIG = Region(56320)
    TMP = Region(20480)
    identf = alloc("identf", [128, 128], F32); identb = alloc("identb", [128, 128], BF16); onesb = alloc("onesb", [128, 128], BF16)
    modT = alloc("modT", [128, 2, 72, 2], F32); scT = alloc("scT", [128, 8, 3], BF16); cvT = alloc("cvT", [128, 8, 3], F32)
    modsl = alloc("modsl", [128, 108], F32); modall = alloc("modall", [128, 4, 108], F32); mtmp = alloc("mtmp", [128, 4, 18], F32); sel = alloc("sel", [128, 2], F32)
    bmodT = alloc("bmodT", [128, 2, 18], F32)
    lnT = alloc("lnT", [128, 2, 2, 3, 8], F32); lnA = alloc("lnA", [128, 2, 2, 3, 8], F32)
    gqk = alloc("gqk", [128, 2, 6, 64], F32); gsub = alloc("gsub", [128, 2], F32); neglam = alloc("neglam", [128, 2], F32)
    lamv = alloc("lamv", [128, 2, 4, 64], F32); lamt = alloc("lamt", [128, 4], F32); lamp = alloc("lamp", [128, 64], F32)
    ropec = alloc("ropec", [128, 8, 64], F32); ropes = alloc("ropes", [128, 8, 64], F32)
    KTc = alloc("KTc", [128, 6, 256], BF16); Vc = alloc("Vc", [128, 2, 768], BF16)
    C256 = alloc("C256", [128, 2, 256], BF16); S256 = alloc("S256", [128, 2, 256], BF16); BCS = alloc("BCS", [128, 4, 128], BF16)
    wf = alloc("wf", [128, 2, 256], BF16)
    epsln = alloc("epsln", [128, 1], F32); epsrms = alloc("epsrms", [128, 1], F32)
    ps = [nc.alloc_psum_tensor("ps%d" % i, [128, 512], F32) for i in range(8)]

    bps = [Buf("ps%d" % i) for i in range(8)]
    bx = [[Buf() for _ in range(3)] for _ in range(8)]
    bh = [[Buf() for _ in range(3)] for _ in range(8)]
    bws = [Buf() for _ in range(4)]
    bufs = {}

    def bf(name):
        if name not in bufs:
            bufs[name] = Buf(name)
        return bufs[name]

    def A(eng, fn, r=(), w=()):
        return P.add(eng, fn, r, w)

    def mm(out, lhsT, rhs, st, sp, r, w):
        return P.add('pe', lambda e: e.matmul(out, lhsT=lhsT, rhs=rhs, start=st, stop=sp), r, w)

    def tr(out, in_, ident, r, w):
        return P.add('pe', lambda e: e.transpose(out=out, in_=in_, identity=ident), r, w)

    def ld(q, dst, src, w, r=()):
        return P.dma(q, lambda e: e.dma_start(out=dst, in_=src), r=r, w=w)

    def fence():
        last = [P.ops[e][-1] for e in ENGS if P.ops[e]]
        for e in ('pe', 'act', 'dve', 'pool', 'sp'):
            op = P.add(e, lambda eng: eng.nop(), (), ())
            op.deps = list(last)
        return

    def tbs(tb):
        return slice(tb * 512, (tb + 1) * 512)

    def mod(l, i, c, v):
        return modT[:, l, i * 8 + c, v:v + 1]

    MIXT = HB
    outs = []
    wsc = [0]

    def next_ws():
        s = wsc[0] % 4
        wsc[0] += 1
        return s

    bankc = [0]
    unitc = [0]

    def next_bank():
        b = bankc[0] % 8
        bankc[0] += 1
        return b

    ld('sp', identf[:], identf_d[:, :], [bf('identf')]); ld('sp', identb[:], identb_d[:, :], [bf('identb')])
    ld('sp', onesb[:], onesb_d[:, :], [bf('onesb')]); ld('sp', cvT[:], cvT_d[:, :, :], [bf('cvT')])
    ld('sp', bmodT[:], bmodT_d[:, :, :], [bf('bmodT')]); ld('sp', sel[:], sel_d[:, :], [bf('sel')]); ld('sp', lnT[:], lnT_d[:, :, :, :, :], [bf('lnT')])
    ld('sp', gqk[:], gqk_d[:, :, :, :], [bf('gqk')]); ld('sp', gsub[:], gsubT_d[:, :], [bf('gsub')])
    ld('sp', lamv[:], lamv_d[:, :, :, :], [bf('lamv')]); ld('sp', ropec[:], ropec_d[:, :, :], [bf('rope')])
    ld('sp', ropes[:], ropes_d[:, :, :], [bf('rope')])
    ld('sp', C256[:], C256d.rearrange("(t p) n -> p t n", p=128), [bf('C256')])
    ld('sp', S256[:], S256d.rearrange("(t p) n -> p t n", p=128), [bf('S256')])
    ld('sp', BCS[:], BCSd.rearrange("k p n -> p k n"), [bf('BCS')])
    A('dve', lambda e: e.memset(epsln[:], LN_EPS), w=[bf('eps')])
    A('dve', lambda e: e.memset(epsrms[:], RMS_EPS), w=[bf('eps')])
    A('act', lambda e: e.activation(out=scT[:], in_=cvT[:], func=AF.Silu), r=[bf('cvT')], w=[bf('scT')])
    A('dve', lambda e: e.tensor_scalar(out=lnA[:], in0=lnT[:], scalar1=ALPHA, scalar2=None, op0=ALU.mult), r=[bf('lnT')], w=[bf('lnA')])
    for l in range(2):
        for j in range(2):
            A('dve', lambda e, l=l, j=j: e.tensor_tensor(out=lamp[:], in0=lamv[:, l, 2 * j, :], in1=lamv[:, l, 2 * j + 1, :], op=ALU.mult),
              r=[bf('lamv')], w=[bf('lamp')])
            A('dve', lambda e, l=l, j=j: e.tensor_reduce(out=lamt[:, 2 * l + j:2 * l + j + 1], in_=lamp[:], op=ALU.add, axis=AX.X),
              r=[bf('lamp')], w=[bf('lamt')])
    A('act', lambda e: e.activation(out=lamt[:], in_=lamt[:], func=AF.Exp), r=[bf('lamt')], w=[bf('lamt')])
    for l in range(2):
        A('dve', lambda e, l=l: e.tensor_tensor(out=neglam[:, l:l + 1], in0=lamt[:, 2 * l + 1:2 * l + 2], in1=lamt[:, 2 * l:2 * l + 1], op=ALU.subtract),
          r=[bf('lamt')], w=[bf('neglam')])
        A('dve', lambda e, l=l: e.tensor_scalar(out=neglam[:, l:l + 1], in0=neglam[:, l:l + 1], scalar1=-LAM_INIT[l], scalar2=None, op0=ALU.add),
          r=[bf('neglam')], w=[bf('neglam')])
        A('dve', lambda e, l=l: e.tensor_scalar(out=gsub[:, l:l + 1], in0=gsub[:, l:l + 1], scalar1=1.0 - LAM_INIT[l], scalar2=None, op0=ALU.mult),
          r=[bf('gsub')], w=[bf('gsub')])

    for l in range(2):
        wv = wmod[l].rearrange("(c p) n -> p c n", p=128)
        for (c0, ncol) in ((0, 512), (512, 512), (1024, 512), (1536, 512), (2048, 256)):
            s_ = next_ws()
            ld('pool', WS[s_][:, :, 0:ncol], wv[:, :, c0:c0 + ncol], [bws[s_]])
            for cc in range(ncol // 128):
                col = (l * 18 + c0 // 128 + cc) * 3
                for k in range(8):
                    mm(ps[0][:, col:col + 3], WS[s_][:, k, cc * 128:(cc + 1) * 128], scT[:, k, :], k == 0, k == 7,
                       [bws[s_], bf('scT')], [bps[0]])
    A('dve', lambda e: e.tensor_tensor(out=modsl[:].rearrange("p (m v) -> p m v", v=3), in0=ps[0][:, 0:108].rearrange("p (m v) -> p m v", v=3),
                                       in1=bmodT[:].rearrange("p l m -> p (l m)").unsqueeze(2).to_broadcast([128, 36, 3]), op=ALU.add),
      r=[bps[0], bf('bmodT')], w=[bf('modsl')])
    P.dma('sp', lambda e: e.dma_start(out=gin_mod[:, :], in_=modsl[:]), r=[bf('modsl')], w=[bf('gin_mod')])
    ccm = P.dma('pool', lambda e: e.collective_compute("AllGather", ALU.bypass, replica_groups=RG, ins=[gin_mod.opt()], outs=[gout_mod.opt()]),
                r=[bf('gin_mod')], w=[bf('gout_mod')], inc=1)
    ccm.own = ('cc', 8)
    P.add('pool', lambda e: e.nop(), [bf('gout_mod')], [bf('gout_mod')])
    ld('sp', modall[:], gout_mod.rearrange("(r p) n -> p r n", p=128), [bf('modall')], r=[bf('gout_mod')])
    modall5 = modall[:].rearrange("p r (l m v) -> p r l m v", l=2, m=18, v=3)
    for l in range(2):
        mv = [modT[:, l, :, vv].rearrange("p (r m) -> p r m", m=18) for vv in range(2)]
        A('dve', lambda e, l=l, mv=mv: e.tensor_copy(out=mv[0], in_=modall5[:, :, l, :, 0]), r=[bf('modall')], w=[bf('mod')])
        A('dve', lambda e, l=l: e.tensor_scalar(out=mtmp[:], in0=modall5[:, :, l, :, 1], scalar1=sel[:, 0:1], scalar2=None, op0=ALU.mult),
          r=[bf('modall'), bf('sel')], w=[bf('mtmp')])
        A('dve', lambda e, l=l, mv=mv: e.scalar_tensor_tensor(out=mv[1], in0=modall5[:, :, l, :, 2], scalar=sel[:, 1:2], in1=mtmp[:], op0=ALU.mult, op1=ALU.add),
          r=[bf('modall'), bf('sel'), bf('mtmp')], w=[bf('mod')])
        for i in (1, 4, 7):
            A('dve', lambda e, l=l, i=i: e.tensor_scalar(out=modT[:, l, i * 8:(i + 1) * 8, :], in0=modT[:, l, i * 8:(i + 1) * 8, :],
                                                         scalar1=1.0, scalar2=1.0 / ALPHA, op0=ALU.add, op1=ALU.mult),
              r=[bf('mod')], w=[bf('mod')])
        for i in (2, 8):
            A('dve', lambda e, l=l, i=i: e.tensor_scalar(out=modT[:, l, i * 8:(i + 1) * 8, :], in0=modT[:, l, i * 8:(i + 1) * 8, :],
                                                         scalar1=0.5, scalar2=None, op0=ALU.mult),
              r=[bf('mod')], w=[bf('mod')])

    TMP.reset()
    xstg = [TMP.alloc("xstg", [128, 1024], F32) for _ in range(2)]
    for t in range(12):
        st = xstg[t % 2]
        ld('sp', st[:], xin[t * 128:(t + 1) * 128, :], [bf('xstg%d' % (t % 2))])
        tb = t // 4
        for half in range(2):
            b = next_bank()
            for cc in range(4):
                c = half * 4 + cc
                tr(ps[b][:, cc * 128:(cc + 1) * 128], st[:, c * 128:(c + 1) * 128], identf[:],
                   [bf('xstg%d' % (t % 2)), bf('identf')], [bps[b]])
            dst = XRES[:, half * 4:half * 4 + 4, t * 128:(t + 1) * 128]
            src = ps[b][:].rearrange("p (c t) -> p c t", t=128)
            wl = [bx[half * 4 + cc][tb] for cc in range(4)]
            if half == 0:
                A('act', lambda e, dst=dst, src=src: e.activation(out=dst, in_=src, func=AF.Copy, scale=ALPHA), r=[bps[b]], w=wl)
            else:
                A('dve', lambda e, dst=dst, src=src: e.tensor_scalar(out=dst, in0=src, scalar1=ALPHA, scalar2=None, op0=ALU.mult), r=[bps[b]], w=wl)

    def layernorm(l, i, final):
        TMP.reset()
        mean_t = TMP.alloc("mean", [128, 512], F32); t1 = TMP.alloc("t1", [128, 512], F32)
        rstd_t = TMP.alloc("rstd", [128, 512], F32)
        tm2 = [TMP.alloc("tm2", [128, 512], F32) for _ in range(2)]
        ZSQ = TMP.alloc("zsq", [128, 8, 512], BF16)
        lnp = lnT if final else lnA
        for tb in range(3):
            for c in range(8):
                A('act', lambda e, c=c, tb=tb: e.activation(out=HB[:, c, tbs(tb)], in_=XRES[:, c, tbs(tb)], func=AF.Copy),
                  r=[bx[c][tb]], w=[bh[c][tb]])
                A('act', lambda e, c=c, tb=tb: e.activation(out=ZSQ[:, c, :], in_=XRES[:, c, tbs(tb)], func=AF.Square),
                  r=[bx[c][tb]], w=[bf('zsq%d' % c)])
            b_s = next_bank(); b_q = next_bank()
            for c in range(8):
                mm(ps[b_s][:], onesb[:], HB[:, c, tbs(tb)], c == 0, c == 7, [bh[c][tb], bf('onesb')], [bps[b_s]])
            for c in range(8):
                mm(ps[b_q][:], onesb[:], ZSQ[:, c, :], c == 0, c == 7, [bf('zsq%d' % c), bf('onesb')], [bps[b_q]])
            A('act', lambda e, b_s=b_s: e.activation(out=mean_t[:], in_=ps[b_s][:], func=AF.Copy, scale=1.0 / 1024), r=[bps[b_s]], w=[bf('mean')])
            A('dve', lambda e: e.tensor_tensor(out=t1[:], in0=mean_t[:], in1=mean_t[:], op=ALU.mult), r=[bf('mean')], w=[bf('t1')])
            A('dve', lambda e, b_q=b_q: e.scalar_tensor_tensor(out=t1[:], in0=ps[b_q][:], scalar=1.0 / 1024, in1=t1[:], op0=ALU.mult, op1=ALU.subtract),
              r=[bps[b_q], bf('t1')], w=[bf('t1')])
            A('act', lambda e: e.activation(out=t1[:], in_=t1[:], func=AF.Sqrt, bias=epsln[:, 0:1], scale=1.0), r=[bf('t1'), bf('eps')], w=[bf('t1')])
            b_r = next_bank()
            A('dve', lambda e, b_r=b_r: e.reciprocal(out=ps[b_r][:], in_=t1[:]), r=[bf('t1')], w=[bps[b_r]])
            for c in range(8):
                tm = tm2[c % 2]; btm = bf('tm2%d' % (c % 2))
                if c in (2, 5, 7):
                    A('dve', lambda e, c=c, tb=tb, tm=tm, b_s=b_s: e.scalar_tensor_tensor(out=tm[:], in0=ps[b_s][:], scalar=-1.0 / 1024, in1=XRES[:, c, tbs(tb)],
                                                                                     op0=ALU.mult, op1=ALU.add),
                      r=[bx[c][tb], bps[b_s]], w=[btm])
                else:
                    A('pool', lambda e, c=c, tb=tb, tm=tm: e.tensor_tensor(out=tm[:], in0=XRES[:, c, tbs(tb)], in1=mean_t[:], op=ALU.subtract),
                      r=[bx[c][tb], bf('mean')], w=[btm])
                A('dve', lambda e, tm=tm, b_r=b_r: e.tensor_tensor(out=tm[:], in0=tm[:], in1=ps[b_r][:], op=ALU.mult), r=[btm, bps[b_r]], w=[btm])
                A('act', lambda e, c=c, tb=tb, tm=tm: e.activation(out=XRES[:, c, tbs(tb)], in_=tm[:], func=AF.Identity,
                                                                   scale=lnp[:, 0, l, i, c:c + 1], bias=lnp[:, 1, l, i, c:c + 1]),
                  r=[btm, bf('lnT'), bf('lnA')], w=[bx[c][tb]])

    def modulate(l, si):
        for tb in range(3):
            v = 0 if tb == 0 else 1
            for c in range(8):
                A('act', lambda e, c=c, tb=tb, v=v: e.activation(out=HB[:, c, tbs(tb)], in_=XRES[:, c, tbs(tb)], func=AF.Identity,
                                                                 scale=mod(l, 3 * si + 1, c, v), bias=mod(l, 3 * si, c, v)),
                  r=[bx[c][tb], bf('mod')], w=[bh[c][tb]])

    FGROUPS = ((0, 4), (4, 4), (8, 3))
    pf = {}

    def ffn_prefetch(l, si, wgu, wd):
        BIG.reset()
        AH = BIG.alloc("AH", [128, 11, 1536], BF16); WD = BIG.alloc("WD", [128, 11, 1024], BF16)
        bwd = [bf('wd%d' % g) for g in range(3)]
        wguv = wgu[l].rearrange("(c p) n -> p c n", p=128)
        wdv = wd[l].rearrange("(j p) n -> p j n", p=128)
        slots = []
        for gi, (a, n) in enumerate(FGROUPS[:2]):
            sg_ = next_ws(); su_ = next_ws()
            col0 = a * 128; ncol = n * 128
            ld('pool', WS[sg_][:, :, 0:ncol], wguv[:, :, col0:col0 + ncol], [bws[sg_]])
            ld('pool', WS[su_][:, :, 0:ncol], wguv[:, :, FF + col0:FF + col0 + ncol], [bws[su_]])
            slots.append((sg_, su_))
        for gi, (a, n) in enumerate(FGROUPS):
            ld('pool', WD[:, a:a + n, :], wdv[:, a:a + n, :], [bwd[gi]])
        pf[(l, si)] = (AH, WD, slots)

    def mix_prefetch(l):
        winv = w_in[l].rearrange("(c p) n -> p c n", p=128)
        for g in range(4):
            ld('pool', WS[g][:], winv[:, :, g * 512:(g + 1) * 512], [bws[g]])
        wsc[0] = 0
        pf[('mix', l)] = True

    def ffn(l, si, wgu, wd, ln_i, final, after=None):
        modulate(l, si)
        TMP.reset()
        pre = pf.pop((l, si), None)
        if pre is None:
            BIG.reset()
            AH = BIG.alloc("AH", [128, 11, 1536], BF16); WD = BIG.alloc("WD", [128, 11, 1024], BF16)
            pslots = []
        else:
            AH, WD, pslots = pre
        sgt = [TMP.alloc("sg", [128, 512], F32) for _ in range(4)]
        ba = [[bf('ah%d_%d' % (j, tb)) for tb in range(3)] for j in range(11)]
        bwd = [bf('wd%d' % g) for g in range(3)]
        groups = FGROUPS
        wguv = wgu[l].rearrange("(c p) n -> p c n", p=128)
        wdv = wd[l].rearrange("(j p) n -> p j n", p=128)
        prc = 0
        for half in range(2):
            j0 = half * 11
            if not (half == 0 and pre is not None):
                for gi, (a, n) in enumerate(groups):
                    ld('pool', WD[:, a:a + n, :], wdv[:, j0 + a:j0 + a + n, :], [bwd[gi]])
            for gi, (a, n) in enumerate(groups):
                col0 = (j0 + a) * 128; ncol = n * 128
                if half == 0 and gi < len(pslots):
                    sg_, su_ = pslots[gi]
                else:
                    sg_ = next_ws(); su_ = next_ws()
                    ld('pool', WS[sg_][:, :, 0:ncol], wguv[:, :, col0:col0 + ncol], [bws[sg_]])
                    ld('pool', WS[su_][:, :, 0:ncol], wguv[:, :, FF + col0:FF + col0 + ncol], [bws[su_]])
                for jj in range(n):
                    j = a + jj
                    for tb in range(3):
                        pr = prc % 4; prc += 1
                        gps = ps[2 * pr]; ups = ps[2 * pr + 1]
                        for k in range(8):
                            mm(gps[:], WS[sg_][:, k, jj * 128:(jj + 1) * 128], HB[:, k, tbs(tb)], k == 0, k == 7, [bws[sg_], bh[k][tb]], [bps[2 * pr]])
                        for k in range(8):
                            mm(ups[:], WS[su_][:, k, jj * 128:(jj + 1) * 128], HB[:, k, tbs(tb)], k == 0, k == 7, [bws[su_], bh[k][tb]], [bps[2 * pr + 1]])
                        A('act', lambda e, pr=pr, gps=gps: e.activation(out=sgt[pr][:], in_=gps[:], func=AF.Silu), r=[bps[2 * pr]], w=[bf('sg%d' % pr)])
                        A('dve', lambda e, pr=pr, ups=ups, j=j, tb=tb: e.tensor_tensor(out=AH[:, j, tbs(tb)], in0=sgt[pr][:], in1=ups[:], op=ALU.mult),
                          r=[bf('sg%d' % pr), bps[2 * pr + 1]], w=[ba[j][tb]])
            for tb in range(3):
                v = 0 if tb == 0 else 1
                for m in range(8):
                    b = next_bank()
                    for j in range(11):
                        mm(ps[b][:], WD[:, j, m * 128:(m + 1) * 128], AH[:, j, tbs(tb)], j == 0, j == 10, [bwd[min(j // 4, 2)], ba[j][tb]], [bps[b]])
                    A('dve', lambda e, b=b, m=m, tb=tb, v=v: e.scalar_tensor_tensor(out=XRES[:, m, tbs(tb)], in0=ps[b][:], scalar=mod(l, 3 * si + 2, m, v),
                                                                                 in1=XRES[:, m, tbs(tb)], op0=ALU.mult, op1=ALU.add),
                      r=[bps[b], bx[m][tb], bf('mod')], w=[bx[m][tb]])
        if after is not None:
            after()
        layernorm(l, ln_i, final)
        fence()

    def store_y():
        TMP.reset()
        ystg = [TMP.alloc("ystg", [128, 1024], F32) for _ in range(2)]
        for t in range(12):
            tb = t // 4
            st = ystg[t % 2]; bst = bf('ystg%d' % (t % 2))
            for half in range(2):
                b = next_bank()
                for cc in range(4):
                    c = half * 4 + cc
                    tr(ps[b][:, cc * 128:(cc + 1) * 128], XRES[:, c, t * 128:(t + 1) * 128], identf[:], [bx[c][tb], bf('identf')], [bps[b]])
                if half == 0:
                    A('act', lambda e, b=b, st=st: e.activation(out=st[:, 0:512], in_=ps[b][:], func=AF.Copy), r=[bps[b]], w=[bst])
                else:
                    A('dve', lambda e, b=b, st=st: e.tensor_copy(out=st[:, 512:1024], in_=ps[b][:]), r=[bps[b]], w=[bst])
            outs.append(P.dma('sp', lambda e, st=st, t=t: e.dma_start(out=y[t * 128:(t + 1) * 128, :], in_=st[:]), r=[bst]))

    def rope(src3, dst3, H, ts, rset, rbufs, wbufs):
        r1, r2, n1, n2 = rset
        cosb = ropec[:, ts, :].unsqueeze(1).to_broadcast([128, H, 64])
        ssv = ropes[:, ts, :].rearrange("p (a b i) -> p a b i", a=2, b=2)
        r1v = r1[:, 0:H * 64].rearrange("p (h d) -> p h d", d=64)
        r2v = r2[:, 0:H * 64].rearrange("p (h a b i) -> p h a b i", a=2, b=2, i=16)
        s5 = src3.rearrange("p h (a b i) -> p h a b i", a=2, b=2)
        A('dve', lambda e: e.tensor_tensor(out=r1v, in0=src3, in1=cosb, op=ALU.mult), r=rbufs + [bf('rope')], w=[bf(n1)])
        for blk in range(2):
            A('dve', lambda e, blk=blk: e.tensor_tensor(out=r2v[:, :, :, blk, :], in0=s5[:, :, :, 1 - blk, :],
                                                        in1=ssv[:, :, blk, :].unsqueeze(1).to_broadcast([128, H, 2, 16]), op=ALU.mult),
              r=rbufs + [bf('rope')], w=[bf(n2)])
        A('pool', lambda e: e.tensor_tensor(out=dst3, in0=r1v, in1=r2[:, 0:H * 64].rearrange("p (h d) -> p h d", d=64), op=ALU.add),
          r=[bf(n1), bf(n2)], w=wbufs)

    def attn_unit(l, kind, segs, oc, tok0, T, qz_eng='pool', pool_share=False):
        pT, rec, t1, t2, sq, Qzs, accP = T
        S = [0, 1, 2, 3]
        acc = [(4, 5), (6, 7)]
        nq = sum(sg[5] for sg in segs)
        u = unitc[0] % 2
        unitc[0] += 1
        Qz = Qzs[u]
        for (KT, V, nkt, Qc, col0, ncols) in segs:
            for g in range(2):
                if qz_eng == 'pool':
                    A('pool', lambda e, g=g, Qc=Qc, col0=col0, ncols=ncols: e.tensor_copy(out=Qz[g][g * 64:(g + 1) * 64, col0:col0 + ncols], in_=Qc[g * 64:(g + 1) * 64, :]),
                      r=[bf('QT'), bf('qzero')], w=[bf('Qz%d_%d' % (u, g))])
                else:
                    A('act', lambda e, g=g, Qc=Qc, col0=col0, ncols=ncols: e.activation(out=Qz[g][g * 64:(g + 1) * 64, col0:col0 + ncols], in_=Qc[g * 64:(g + 1) * 64, :], func=AF.Copy),
                      r=[bf('QT'), bf('qzero')], w=[bf('Qz%d_%d' % (u, g))])
        its = [(si, kt, g) for si, sg in enumerate(segs) for kt in range(sg[2]) for g in range(2)]
        LOOK = 3

        def emit_s(i):
            si, kt, g = its[i]
            KT, V, nkt, Qc, col0, ncols = segs[si]
            kap, kb_ = KT(kt)
            sb = S[i % 4]; pi = i % 4
            mm(ps[sb][:, 0:ncols], kap, Qz[g][:, col0:col0 + ncols], True, True, kb_ + [bf('Qz%d_%d' % (u, g))], [bps[sb]])
            A('act', lambda e, sb=sb, pi=pi, ncols=ncols: e.activation(out=pT[pi][:, 0:ncols], in_=ps[sb][:, 0:ncols], func=AF.Exp, scale=0.125),
              r=[bps[sb]], w=[bf('pT%d' % pi)])

        def emit_pv(i):
            si, kt, g = its[i]
            KT, V, nkt, Qc, col0, ncols = segs[si]
            vap, vb_ = V(kt)
            pi = i % 4
            ob, db = acc[g]
            mm(ps[ob][:, col0:col0 + ncols], vap, pT[pi][:, 0:ncols], kt == 0, kt == nkt - 1, vb_ + [bf('pT%d' % pi)], [bps[ob]])
            if pool_share and g == 1 and kt % 2 == 1:
                if kt == 1:
                    A('pool', lambda e, pi=pi, col0=col0, ncols=ncols: e.tensor_copy(out=accP[:, col0:col0 + ncols], in_=pT[pi][:, 0:ncols]),
                      r=[bf('pT%d' % pi)], w=[bf('accP')])
                else:
                    A('pool', lambda e, pi=pi, col0=col0, ncols=ncols: e.tensor_tensor(out=accP[:, col0:col0 + ncols], in0=accP[:, col0:col0 + ncols],
                                                                                   in1=pT[pi][:, 0:ncols], op=ALU.add),
                      r=[bf('pT%d' % pi)], w=[bf('accP')])
            elif kt == 0:
                A('dve', lambda e, db=db, pi=pi, col0=col0, ncols=ncols: e.tensor_copy(out=ps[db][:, col0:col0 + ncols], in_=pT[pi][:, 0:ncols]),
                  r=[bf('pT%d' % pi)], w=[bps[db]])
            else:
                A('dve', lambda e, db=db, pi=pi, col0=col0, ncols=ncols: e.tensor_tensor(out=ps[db][:, col0:col0 + ncols], in0=ps[db][:, col0:col0 + ncols],
                                                                                         in1=pT[pi][:, 0:ncols], op=ALU.add),
                  r=[bf('pT%d' % pi)], w=[bps[db]])
        for i in range(min(LOOK, len(its))):
            emit_s(i)
        for i in range(len(its)):
            emit_pv(i)
            if i + LOOK < len(its):
                emit_s(i + LOOK)
        dst = MIXT[:, oc, tok0:tok0 + nq]
        DB = [1, 2]
        if pool_share:
            A('dve', lambda e: e.tensor_tensor(out=ps[acc[1][1]][:, 0:nq], in0=ps[acc[1][1]][:, 0:nq], in1=accP[:, 0:nq], op=ALU.add),
              r=[bf('accP')], w=[bps[acc[1][1]]])
        for g in range(2):
            db = acc[g][1]
            A('act', lambda e, g=g, db=db: e.activation(out=pT[g][:, 0:nq], in_=ps[db][:, 0:nq], func=AF.Copy), r=[bps[db]], w=[bf('pT%d' % g)])
            mm(ps[DB[g]][:, 0:nq], onesb[:], pT[g][:, 0:nq], True, True, [bf('onesb'), bf('pT%d' % g)], [bps[DB[g]]])
        if kind == 'A':
            for g in range(2):
                ob, db = acc[g]
                sl = slice(g * 64, (g + 1) * 64)
                A('act', lambda e, g=g, sl=sl: e.activation(out=rec[sl, 0:nq], in_=ps[DB[g]][sl, 0:nq], func=AF.Ln), r=[bps[DB[g]]], w=[bf('rec%d' % g)])
                A('act', lambda e, g=g, sl=sl: e.activation(out=rec[sl, 0:nq], in_=rec[sl, 0:nq], func=AF.Exp, scale=-1.0), r=[bf('rec%d' % g)], w=[bf('rec%d' % g)])
                A('dve', lambda e, ob=ob, sl=sl: e.tensor_tensor(out=MIXT[sl, oc, tok0:tok0 + nq], in0=ps[ob][sl, 0:nq], in1=rec[sl, 0:nq], op=ALU.mult),
                  r=[bps[ob], bf('rec%d' % g)], w=[bf('MIXT%d_%d' % (oc, tok0))])
        else:
            A('act', lambda e: e.activation(out=rec[:, 0:nq], in_=ps[1][:, 0:nq], func=AF.Ln), r=[bps[1]], w=[bf('rec0')])
            A('act', lambda e: e.activation(out=rec[:, 0:nq], in_=rec[:, 0:nq], func=AF.Exp, scale=-1.0), r=[bf('rec0')], w=[bf('rec0')])
            A('dve', lambda e: e.tensor_tensor(out=t1[:, 0:nq], in0=ps[4][:, 0:nq], in1=rec[:, 0:nq], op=ALU.mult), r=[bps[4], bf('rec0')], w=[bf('at1')])
            A('act', lambda e: e.activation(out=rec[:, 0:nq], in_=ps[2][:, 0:nq], func=AF.Ln), r=[bps[2], bf('at1')], w=[bf('rec0')])
            A('act', lambda e: e.activation(out=rec[:, 0:nq], in_=rec[:, 0:nq], func=AF.Exp, scale=-1.0), r=[bf('rec0')], w=[bf('rec0')])
            A('dve', lambda e: e.tensor_tensor(out=t2[:, 0:nq], in0=ps[6][:, 0:nq], in1=rec[:, 0:nq], op=ALU.mult), r=[bps[6], bf('rec0')], w=[bf('at2')])
            A('dve', lambda e: e.scalar_tensor_tensor(out=t2[:, 0:nq], in0=t2[:, 0:nq], scalar=neglam[:, l:l + 1], in1=t1[:, 0:nq], op0=ALU.mult, op1=ALU.add),
              r=[bf('at1'), bf('at2'), bf('neglam')], w=[bf('at2')])
            A('act', lambda e: e.activation(out=sq[:, 0:nq], in_=t2[:, 0:nq], func=AF.Square), r=[bf('at2')], w=[bf('asq')])
            mm(ps[0][:, 0:nq], onesb[:], sq[:, 0:nq], True, True, [bf('onesb'), bf('asq')], [bps[0]])
            A('act', lambda e: e.activation(out=t1[:, 0:nq], in_=ps[0][:, 0:nq], func=AF.Sqrt, bias=epsrms[:, 0:1], scale=1.0 / 128),
              r=[bps[0], bf('eps')], w=[bf('at1')])
            A('dve', lambda e: e.reciprocal(out=rec[:, 0:nq], in_=t1[:, 0:nq]), r=[bf('at1')], w=[bf('rec0')])
            A('dve', lambda e: e.tensor_tensor(out=t2[:, 0:nq], in0=t2[:, 0:nq], in1=rec[:, 0:nq], op=ALU.mult), r=[bf('rec0'), bf('at2')], w=[bf('at2')])
            A('act', lambda e: e.activation(out=dst, in_=t2[:, 0:nq], func=AF.Identity, scale=gsub[:, l:l + 1]), r=[bf('at2'), bf('gsub')], w=[bf('MIXT%d_%d' % (oc, tok0))])

    def mixbufs(k, tb):
        if k >= 6:
            return [bf('MIXF%d_%d' % (k - 6, tb))]
        return [bf('MIXT%d_%d' % (k, tb * 512))]

    def mix(l):
        modulate(l, 1)
        BIG.reset(); TMP.reset()
        QT = BIG.alloc("QT", [128, 6, 1536], BF16)
        WOUT = BIG.alloc("WOUT", [128, 8, 1024], BF16)
        WX = BIG.alloc("WX", [128, 8, 256], BF16)
        KTL = BIG.alloc("KTL", [128, 6, 512], BF16)
        VL = BIG.alloc("VL", [128, 4, 1024], BF16)
        tsq = TMP.alloc("tsq", [128, 384], F32); tn = tsq
        bufs['tn'] = bf('tsq')
        ss6 = TMP.alloc("ss6", [128, 6], F32); sd6 = TMP.alloc("sd6", [128, 6], F32); rs6 = TMP.alloc("rs6", [128, 6], F32)
        r1 = TMP.alloc("r1", [128, 512], F32); r2 = TMP.alloc("r2", [128, 512], F32)
        qk_tok = TMP.alloc("qk_tok", [128, 1536], F32); vu_tok = TMP.alloc("vu_tok", [128, 1024], BF16)
        kt_stage = TMP.alloc("kt_stage", [128, 6, 128], BF16)
        f32a = TMP.alloc("f32a", [128, 512], F32); f32b = f32a; vtmp = TMP.alloc("vtmp", [128, 128], F32)
        r2b = TMP.alloc("r2b", [128, 512], F32)
        rsets = [(r1, r2, 'r1', 'r2'), (f32a, r2b, 'f32a', 'r2b')]
        rcnt = [0]

        def nr():
            rcnt[0] += 1
            return rsets[rcnt[0] % 2]

        winv = w_in[l].rearrange("(c p) n -> p c n", p=128)
        if not pf.pop(('mix', l), False):
            for g in range(4):
                ld('pool', WS[g][:], winv[:, :, g * 512:(g + 1) * 512], [bws[g]])
            wsc[0] = 0
        ld('pool', WX[:], winv[:, :, 2048:2304], [bf('WX')])
        cktf = BIG.alloc("cktf", [128, 768], F32)
        cavv = cav[l].rearrange("(t p) n -> p t n", p=128)
        for kvh in range(2):
            for dup in range(2):
                o = kvh * 128 + dup * 64
                ld('pool', Vc[:, :, o:o + 64], cavv[:, :, kvh * 64:(kvh + 1) * 64], [bf('Vc')])
        ld('pool', Vc[:, :, 256:768], cbv[l].rearrange("(t p) n -> p t n", p=128), [bf('Vc')])
        for t in range(2):
            for kvh in range(2):
                for dup in range(2):
                    o = kvh * 128 + dup * 64
                    ld('pool', cktf[:, o:o + 64], cak[l, t * 128:(t + 1) * 128, kvh * 64:(kvh + 1) * 64], [bf('ckt')])
            ld('pool', cktf[:, 256:768], cbk[l, t * 128:(t + 1) * 128, :], [bf('ckt')])
            for half in range(2):
                b = next_bank()
                nchunk = 4 if half == 0 else 2
                for cc in range(nchunk):
                    c = half * 4 + cc
                    tr(ps[b][:, cc * 128:(cc + 1) * 128], cktf[:, c * 128:(c + 1) * 128], identf[:], [bf('ckt'), bf('identf')], [bps[b]])
                A('dve', lambda e, b=b, t=t, half=half, nchunk=nchunk: e.tensor_copy(
                    out=KTc[:, half * 4:half * 4 + nchunk, t * 128:(t + 1) * 128],
                    in_=ps[b][:, 0:nchunk * 128].rearrange("p (c t) -> p c t", t=128)), r=[bps[b]], w=[bf('KTc')])
        ld('pool', WOUT[:], w_out[l].rearrange("(c p) n -> p c n", p=128), [bf('WOUT')])
        ld('pool', wf[:], w_f[l].rearrange("(c p) n -> p c n", p=128), [bf('wf')])

        BMAP = {0: 0, 3: 1, 4: 2, 1: 3, 2: 4}
        tn3 = tn[:].rearrange("p (h d) -> p h d", d=64)
        qa_dst = qk_tok[:, 0:256].rearrange("p (h d) -> p h d", d=64)
        ka_dst = qk_tok[:, 768:1024].rearrange("p (k u d) -> p k u d", k=2, u=2)
        qb_dst = qk_tok[:, 256:768]; kb_dst = qk_tok[:, 1024:1536]

        def proj_mm(t, groups):
            tb = t // 4
            tsl = slice(t * 128, (t + 1) * 128)
            for g in groups:
                b = BMAP[g]
                ncol = 512 if g < 4 else 256
                for k in range(8):
                    rhs = WS[g][:, k, :] if g < 4 else WX[:, k, :]
                    mm(ps[b][:, 0:ncol], HB[:, k, tsl], rhs, k == 0, k == 7, [bh[k][tb], bws[g] if g < 4 else bf('WX')], [bps[b]])

        def post_a(t):
            prompt = t < 4
            b0, b3, b4 = BMAP[0], BMAP[3], BMAP[4]
            vdst = VL[:, t, :] if prompt else vu_tok[:]
            bvd = bf('VL') if prompt else bf('vu_tok')
            A('act', lambda e: e.activation(out=tsq[:], in_=ps[b0][:, 0:384], func=AF.Square), r=[bps[b0]], w=[bf('tsq')])
            A('dve', lambda e: e.tensor_reduce(out=ss6[:], in_=tsq[:].rearrange("p (h d) -> p h d", d=64), op=ALU.add, axis=AX.X), r=[bf('tsq')], w=[bf('ss6')])
            A('act', lambda e: e.activation(out=sd6[:], in_=ss6[:], func=AF.Sqrt, bias=epsrms[:, 0:1], scale=1.0 / 64), r=[bf('ss6'), bf('eps')], w=[bf('sd6')])
            A('dve', lambda e: e.reciprocal(out=rs6[:], in_=sd6[:]), r=[bf('sd6')], w=[bf('rs6')])
            A('dve', lambda e: e.tensor_tensor(out=tn3, in0=ps[b0][:, 0:384].rearrange("p (h d) -> p h d", d=64),
                                               in1=rs6[:].unsqueeze(2).to_broadcast([128, 6, 64]), op=ALU.mult), r=[bps[b0], bf('rs6')], w=[bf('tn')])
            A('dve', lambda e: e.tensor_tensor(out=tn3, in0=tn3, in1=gqk[:, l], op=ALU.mult), r=[bf('tn'), bf('gqk')], w=[bf('tn')])
            if prompt:
                outs.append(P.dma('sp', lambda e, t=t: e.dma_start(out=nak[l, t * 128:(t + 1) * 128, :], in_=tn[:, 256:384]), r=[bf('tn')]))
                A('act', lambda e: e.activation(out=vtmp[:], in_=ps[b0][:, 384:512], func=AF.Copy), r=[bps[b0]], w=[bf('vtmp')])
                outs.append(P.dma('sp', lambda e, t=t: e.dma_start(out=nav[l, t * 128:(t + 1) * 128, :], in_=vtmp[:]), r=[bf('vtmp')]))
                A('pool', lambda e: e.tensor_copy(out=qa_dst, in_=tn3[:, 0:4, :]), r=[bf('tn')], w=[bf('qk_tok')])
                for u in range(2):
                    A('pool', lambda e, u=u: e.tensor_copy(out=ka_dst[:, :, u, :], in_=tn3[:, 4:6, :]), r=[bf('tn')], w=[bf('qk_tok')])
            else:
                rope(tn3[:, 0:4, :], qa_dst, 4, t - 4, nr(), [bf('tn')], [bf('qk_tok')])
                for u in range(2):
                    rope(tn3[:, 4:6, :], ka_dst[:, :, u, :], 2, t - 4, nr(), [bf('tn')], [bf('qk_tok')])
            va_dst = vdst[:, 0:256].rearrange("p (k u d) -> p k u d", k=2, u=2)
            A('dve', lambda e, va_dst=va_dst: e.tensor_copy(
                out=va_dst, in_=ps[b0][:, 384:512].rearrange("p (k d) -> p k d", d=64).unsqueeze(2).to_broadcast([128, 2, 2, 64])),
              r=[bps[b0]], w=[bvd])
            if prompt:
                A('act', lambda e: e.activation(out=f32b[:], in_=ps[b3][:], func=AF.Copy), r=[bps[b3]], w=[bf('f32a')])
                outs.append(P.dma('sp', lambda e, t=t: e.dma_start(out=nbv[l, t * 128:(t + 1) * 128, :], in_=f32b[:]), r=[bf('f32a')]))
                A('pool', lambda e, vdst=vdst: e.tensor_copy(out=vdst[:, 256:768], in_=f32b[:]), r=[bf('f32a')], w=[bvd])
            else:
                A('act', lambda e, vdst=vdst: e.activation(out=vdst[:, 256:768], in_=ps[b3][:], func=AF.Copy), r=[bps[b3]], w=[bvd])
            A('dve', lambda e, vdst=vdst: e.tensor_copy(out=vdst[:, 768:1024], in_=ps[b4][:, 0:256]), r=[bps[b4]], w=[bvd])

        def post_b(t):
            prompt = t < 4
            b1, b2 = BMAP[1], BMAP[2]
            if prompt:
                A('act', lambda e: e.activation(out=qb_dst, in_=ps[b1][:], func=AF.Copy), r=[bps[b1]], w=[bf('qk_tok')])
                A('act', lambda e: e.activation(out=f32a[:], in_=ps[b2][:], func=AF.Copy), r=[bps[b2]], w=[bf('f32a')])
                outs.append(P.dma('sp', lambda e, t=t: e.dma_start(out=nbk[l, t * 128:(t + 1) * 128, :], in_=f32a[:]), r=[bf('f32a')]))
                A('pool', lambda e: e.tensor_copy(out=kb_dst, in_=f32a[:]), r=[bf('f32a')], w=[bf('qk_tok')])
            else:
                rope(ps[b1][:].rearrange("p (h d) -> p h d", d=64), qb_dst.rearrange("p (h d) -> p h d", d=64), 8, t - 4, nr(), [bps[b1]], [bf('qk_tok')])
                rope(ps[b2][:].rearrange("p (h d) -> p h d", d=64), kb_dst.rearrange("p (h d) -> p h d", d=64), 8, t - 4, nr(), [bps[b2]], [bf('qk_tok')])

        def trans(t):
            prompt = t < 4
            tsl = slice(t * 128, (t + 1) * 128)
            for grp in range(3):
                b = 5 + grp
                for cc in range(4):
                    c = grp * 4 + cc
                    tr(ps[b][:, cc * 128:(cc + 1) * 128], qk_tok[:, c * 128:(c + 1) * 128], identf[:], [bf('qk_tok'), bf('identf')], [bps[b]])
                src = ps[b][:].rearrange("p (c t) -> p c t", t=128)
                if grp == 0:
                    A('act', lambda e, src=src, tsl=tsl: e.activation(out=QT[:, 0:4, tsl], in_=src, func=AF.Copy), r=[bps[b]], w=[bf('QT')])
                elif grp == 1:
                    A('dve', lambda e, src=src, tsl=tsl: e.tensor_copy(out=QT[:, 4:6, tsl], in_=src[:, 0:2, :]), r=[bps[b]], w=[bf('QT')])
                    kd = KTL[:, 0:2, tsl] if prompt else kt_stage[:, 0:2, :]
                    A('dve', lambda e, src=src, kd=kd: e.tensor_copy(out=kd, in_=src[:, 2:4, :]), r=[bps[b]], w=[bf('KTL') if prompt else bf('kt_stage')])
                else:
                    kd = KTL[:, 2:6, tsl] if prompt else kt_stage[:, 2:6, :]
                    A('act', lambda e, src=src, kd=kd: e.activation(out=kd, in_=src, func=AF.Copy), r=[bps[b]], w=[bf('KTL') if prompt else bf('kt_stage')])
            if not prompt:
                ts = t - 4
                for h in range(2):
                    P.dma('sp', lambda e, ts=ts, h=h: e.dma_start(out=gin_kt[h].rearrange("(c p) n -> p c n", p=128)[:, :, ts * 128:(ts + 1) * 128],
                                                                  in_=kt_stage[:, 3 * h:3 * h + 3, :]),
                          r=[bf('kt_stage')], w=[bf('gin_kt%d' % h)])
                hh, t4 = ts // 4, ts % 4
                P.dma('sp', lambda e, hh=hh, t4=t4: e.dma_start(out=gin_vu[hh][t4 * 128:(t4 + 1) * 128, :], in_=vu_tok[:]),
                      r=[bf('vu_tok')], w=[bf('gin_vu%d' % hh)])

        proj_mm(0, (0, 3, 4)); proj_mm(0, (1, 2))
        for t in range(12):
            post_a(t)
            if t + 1 < 12:
                proj_mm(t + 1, (0, 3, 4))
            post_b(t)
            trans(t)
            if t + 1 < 12:
                proj_mm(t + 1, (1, 2))
        fence()
        for h in range(2):
            cc = P.dma('pool', lambda e, h=h: e.collective_compute("AllGather", ALU.bypass, replica_groups=RG, ins=[gin_kt[h].opt()], outs=[gout_kt[h].opt()]),
                       r=[bf('gin_kt%d' % h)], w=[bf('gout_kt%d' % h)], inc=1)
            cc.own = ('cc', 4 * l + h)
            cc = P.dma('pool', lambda e, h=h: e.collective_compute("AllGather", ALU.bypass, replica_groups=RG, ins=[gin_vu[h].opt()], outs=[gout_vu[h].opt()]),
                       r=[bf('gin_vu%d' % h)], w=[bf('gout_vu%d' % h)], inc=1)
            cc.own = ('cc', 4 * l + 2 + h)
        gob = [bf('gout_kt0'), bf('gout_kt1'), bf('gout_vu0'), bf('gout_vu1')]
        WSr.reset(); TMP.reset()
        ktj = [WSr.alloc("ktj", [128, 4, 1024], BF16) for _ in range(2)]
        vj = [WSr.alloc("vj", [128, 32, 128], BF16) for _ in range(2)]
        pT = [TMP.alloc("pT", [128, 512], BF16) for _ in range(4)]
        T = (pT, TMP.alloc("rec", [128, 512], F32), TMP.alloc("at1", [128, 512], F32), TMP.alloc("at2", [128, 512], F32),
             TMP.alloc("asq", [128, 512], BF16),
             [[TMP.alloc("Qz", [128, 512], BF16) for _ in range(2)] for _ in range(2)],
             TMP.alloc("accP", [128, 512], F32))
        for u_ in range(2):
            for g_ in range(2):
                A('dve', lambda e, u_=u_, g_=g_: e.memset(T[5][u_][g_][:], 0.0), w=[bf('qzero')])
        for job in range(6):
            kind = 'A' if job < 2 else 'B'
            vcol = job * 128 if job < 2 else 256 + (job - 2) * 128
            segs = []
            for s_ in range(2):
                KTf = lambda kt, s_=s_, job=job: (KTL[:, job, s_ * 256 + kt * 128:s_ * 256 + (kt + 1) * 128], [bf('KTL')])
                Vf = lambda kt, s_=s_, vcol=vcol: (VL[:, s_ * 2 + kt, vcol:vcol + 128], [bf('VL')])
                segs.append((KTf, Vf, 2, QT[:, job, s_ * 256:(s_ + 1) * 256], s_ * 256, 256))
            attn_unit(l, kind, segs, job, 0, T, qz_eng='act')
        P.add('pool', lambda e: e.nop(), gob, gob)
        gk = [g_.rearrange("(r c p) n -> p r c n", c=3, p=128) for g_ in gout_kt]
        gv = [g_.rearrange("(t p) n -> p t n", p=128) for g_ in gout_vu]
        for job in range(6):
            kind = 'A' if job < 2 else 'B'
            vcol = job * 128 if job < 2 else 256 + (job - 2) * 128
            sl = job % 2
            ld('pool', ktj[sl][:], gk[job // 3][:, :, job % 3, :], [bf('ktj%d' % sl)], r=gob)
            for h in range(2):
                ld('pool', vj[sl][:, 16 * h:16 * h + 16, :], gv[h][:, :, vcol:vcol + 128], [bf('vj%d_%d' % (sl, h))], r=gob)

            def KTf(kt, job=job, sl=sl):
                if kt < 2:
                    return KTc[:, job, kt * 128:(kt + 1) * 128], [bf('KTc')]
                k2 = kt - 2
                h, r, t4 = k2 // 16, (k2 % 16) // 4, k2 % 4
                t = h * 4 + t4
                return ktj[sl][:, r, t * 128:(t + 1) * 128], [bf('ktj%d' % sl)]

            def Vf(kt, vcol=vcol, sl=sl):
                if kt < 2:
                    return Vc[:, kt, vcol:vcol + 128], [bf('Vc')]
                return vj[sl][:, kt - 2, :], [bf('vj%d_%d' % (sl, (kt - 2) // 16))]
            for qb in range(2):
                tok0 = 512 + qb * 512
                attn_unit(l, kind, [(KTf, Vf, 34, QT[:, job, tok0:tok0 + 512], 0, 512)], job, tok0, T, pool_share=True)
        fence()
        if MIXSUB <= 4:
            return
        WSr.reset(); TMP.reset()
        tabC = [WSr.alloc("tabC", [128, 4, 1024], BF16) for _ in range(2)]
        tabS = [WSr.alloc("tabS", [128, 4, 1024], BF16) for _ in range(2)]
        ug = TMP.alloc("ug", [128, 32, 256], BF16)
        XT = nc.alloc_sbuf_tensor_at("XT_%d" % l, [128, 2, 1536], BF16, offset=BIG.base)
        YT = nc.alloc_sbuf_tensor_at("YT_%d" % l, [128, 2, 1536], BF16, offset=BIG.base + 6144)
        FT = nc.alloc_sbuf_tensor_at("FT_%d" % l, [128, 2, 1536], BF16, offset=BIG.base + 12288)
        for h in range(2):
            ld('pool', ug[:, 16 * h:16 * h + 16, :], gv[h][:, :, 768:1024], [bf('ug%d' % h)], r=gob)
        CLv = CLd.rearrange("(t p) n -> p t n", p=128); SLv = SLd.rearrange("(t p) n -> p t n", p=128)
        for grp in range(8):
            sl = grp % 2
            ld('pool', tabC[sl][:], CLv[:, grp * 4:(grp + 1) * 4, :], [bf('tabC%d' % sl)])
            ld('pool', tabS[sl][:], SLv[:, grp * 4:(grp + 1) * 4, :], [bf('tabS%d' % sl)])
            for fc in range(2):
                for lb in range(2):
                    for ti, (tab, bt) in enumerate(((tabC[sl], 'tabC%d' % sl), (tabS[sl], 'tabS%d' % sl))):
                        b = ti * 4 + fc * 2 + lb
                        for i in range(4):
                            ui = (grp % 2) * 16 + (grp // 2) * 4 + i
                            mm(ps[b][:], ug[:, ui, fc * 128:(fc + 1) * 128], tab[:, i, lb * 512:(lb + 1) * 512],
                               grp == 0 and i == 0, grp == 7 and i == 3, [bf('ug%d' % (grp % 2)), bf(bt)], [bps[b]])
        for fc in range(2):
            for lb in range(2):
                A('act', lambda e, fc=fc, lb=lb: e.activation(out=XT[:, fc, 512 + lb * 512:1024 + lb * 512], in_=ps[fc * 2 + lb][:], func=AF.Copy),
                  r=[bps[fc * 2 + lb]], w=[bf('XT%d_%d' % (fc, 1 + lb))])
                A('dve', lambda e, fc=fc, lb=lb: e.tensor_copy(out=YT[:, fc, 512 + lb * 512:1024 + lb * 512], in_=ps[4 + fc * 2 + lb][:]),
                  r=[bps[4 + fc * 2 + lb]], w=[bf('YT%d_%d' % (fc, 1 + lb))])
        for s in range(2):
            for fc in range(2):
                bX = next_bank(); bY = next_bank()
                for lt in range(2):
                    mm(ps[bX][:, 0:256], VL[:, s * 2 + lt, 768 + fc * 128:768 + (fc + 1) * 128], C256[:, lt, :], lt == 0, lt == 1, [bf('VL'), bf('C256')], [bps[bX]])
                for lt in range(2):
                    mm(ps[bY][:, 0:256], VL[:, s * 2 + lt, 768 + fc * 128:768 + (fc + 1) * 128], S256[:, lt, :], lt == 0, lt == 1, [bf('VL'), bf('S256')], [bps[bY]])
                A('act', lambda e, s=s, fc=fc, bX=bX: e.activation(out=XT[:, fc, s * 256:(s + 1) * 256], in_=ps[bX][:, 0:256], func=AF.Copy), r=[bps[bX]], w=[bf('XT%d_0' % fc)])
                A('dve', lambda e, s=s, fc=fc, bY=bY: e.tensor_copy(out=YT[:, fc, s * 256:(s + 1) * 256], in_=ps[bY][:, 0:256]), r=[bps[bY]], w=[bf('YT%d_0' % fc)])
        for tb in range(3):
            ti = 2 if tb == 0 else 0
            for fc in range(2):
                b = next_bank()
                mm(ps[b][:], BCS[:, ti, :], XT[:, fc, tbs(tb)], True, False, [bf('BCS'), bf('XT%d_%d' % (fc, tb))], [bps[b]])
                mm(ps[b][:], BCS[:, ti + 1, :], YT[:, fc, tbs(tb)], False, True, [bf('BCS'), bf('YT%d_%d' % (fc, tb))], [bps[b]])
                A('act' if fc == 0 else 'dve',
                  (lambda e, b=b, fc=fc, tb=tb: e.activation(out=FT[:, fc, tbs(tb)], in_=ps[b][:], func=AF.Copy)) if fc == 0 else
                  (lambda e, b=b, fc=fc, tb=tb: e.tensor_copy(out=FT[:, fc, tbs(tb)], in_=ps[b][:])), r=[bps[b]], w=[bf('FT%d_%d' % (fc, tb))])
            for oc in range(2):
                b = next_bank()
                for fc in range(2):
                    mm(ps[b][:], wf[:, fc, oc * 128:(oc + 1) * 128], FT[:, fc, tbs(tb)], fc == 0, fc == 1, [bf('wf'), bf('FT%d_%d' % (fc, tb))], [bps[b]])
                A('act' if oc == 0 else 'dve',
                  (lambda e, b=b, oc=oc, tb=tb: e.activation(out=MIXT[:, 6 + oc, tbs(tb)], in_=ps[b][:], func=AF.Copy)) if oc == 0 else
                  (lambda e, b=b, oc=oc, tb=tb: e.tensor_copy(out=MIXT[:, 6 + oc, tbs(tb)], in_=ps[b][:])), r=[bps[b]], w=[bf('MIXF%d_%d' % (oc, tb))])
        if MIXDBG:
            for tb in range(3):
                for c in range(8):
                    A('dve', lambda e, c=c, tb=tb: e.tensor_copy(out=XRES[:, c, tbs(tb)], in_=MIXT[:, c, tbs(tb)]), r=mixbufs(c, tb) + [bx[c][tb]], w=[bx[c][tb]])
            fence()
            WSr.reset()
            for i in range(4):
                WS[i] = WSr.alloc("WS%d" % i, [128, 8, 512], BF16)
            return
        for tb in range(3):
            v = 0 if tb == 0 else 1
            for m in range(8):
                b = next_bank()
                for k in range(8):
                    mm(ps[b][:], WOUT[:, k, m * 128:(m + 1) * 128], MIXT[:, k, tbs(tb)], k == 0, k == 7, [bf('WOUT')] + mixbufs(k, tb), [bps[b]])
                A('dve', lambda e, b=b, m=m, tb=tb, v=v: e.scalar_tensor_tensor(out=XRES[:, m, tbs(tb)], in0=ps[b][:], scalar=mod(l, 5, m, v),
                                                                             in1=XRES[:, m, tbs(tb)], op0=ALU.mult, op1=ALU.add),
                  r=[bps[b], bx[m][tb], bf('mod')], w=[bx[m][tb]])
        fence()
        WSr.reset()
        for i in range(4):
            WS[i] = WSr.alloc("WS%d" % i, [128, 8, 512], BF16)
        ffn_prefetch(l, 2, w2gu, w2d)
        layernorm(l, 1, False)
        fence()

    ffn_prefetch(0, 0, w1gu, w1d)
    fence()
    done = False
    for l in range(2):
        if stage <= 4 * l + 0:
            break
        ffn(l, 0, w1gu, w1d, 0, stage == 4 * l + 1, after=(lambda l=l: mix_prefetch(l)))
        if stage <= 4 * l + 1:
            break
        mix(l)
        if stage <= 4 * l + 2:
            break
        ffn(l, 2, w2gu, w2d, 2, l == 1 or stage == 4 * l + 3, after=((lambda: ffn_prefetch(1, 0, w1gu, w1d)) if l == 0 else None))
        if stage <= 4 * l + 3:
            break
    store_y()
    fin = P.add('sp', lambda e: e.nop(), (), ())
    fin.deps = [o for o in outs if o is not None]
    P.plan()
    sems = {e: es.enter_context(nc.semaphore("s_" + e)) for e in ENGS}
    dsems = {}
    for e in ('pool', 'sp'):
        for i in range(P.n_dma_sems):
            dsems[(e, i)] = es.enter_context(nc.semaphore("d_%s%d" % (e, i)))
    for i in range(9):
        dsems[('cc', i)] = es.enter_context(nc.semaphore("cc%d" % i))
    block = es.enter_context(nc.Block())
    P.emit(nc, block, sems, dsems)
    es.close()
    return nc


_NC_CACHE = {}


def _rope_tables(length):
    rows = length // 64
    row = np.repeat(np.arange(rows), 64).astype(np.float32)
    col = np.tile(np.arange(64), rows).astype(np.float32)
    inv = (1.0 / (10000.0 ** (np.arange(0, 32, 2, dtype=np.float32) / 32.0))).astype(np.float32)
    ar = row[:, None] * inv
    ac = col[:, None] * inv
    cos = np.concatenate([np.cos(ar), np.cos(ar), np.cos(ac), np.cos(ac)], -1).astype(np.float32)
    sin = np.concatenate([np.sin(ar), np.sin(ar), np.sin(ac), np.sin(ac)], -1).astype(np.float32)
    sgn = np.where((np.arange(64) % 32) < 16, -1.0, 1.0).astype(np.float32)
    return cos, sin * sgn


def _consts():
    bfl = ml_dtypes.bfloat16
    l4 = np.arange(4096, dtype=np.int64)
    ph = (l4[:, None] * l4[None, :]) % 4096
    ang = ph.astype(np.float64) * (2.0 * np.pi / 4096)
    CL = np.cos(ang).astype(bfl)
    SL = np.sin(ang).astype(bfl)
    l2 = np.arange(256, dtype=np.int64)
    a2 = ((l2[:, None] * l2[None, :]) % 256).astype(np.float64) * (2.0 * np.pi / 256)
    C256 = np.cos(a2).astype(bfl)
    S256 = np.sin(a2).astype(bfl)
    c = np.arange(64, dtype=np.int64)
    a3 = ((c[:, None] * c[None, :]) % 64).astype(np.float64) * (2.0 * np.pi / 64)
    C64 = np.cos(a3)
    S64 = np.sin(a3)
    z = np.zeros((64, 64))
    bd = lambda m: np.block([[m, z], [z, m]])
    BCS = np.stack([bd(C64) / 512.0, -bd(S64) / 512.0, bd(C64) / 128.0, -bd(S64) / 128.0]).astype(bfl)
    return CL, SL, C256, S256, BCS


def kernel(x_prompt, x_sample, cache_a_k, cache_a_v, cache_b_k, cache_b_v, c, c_ctx,
           w_mod, b_mod, w_in, g_qa, g_ka, lam_q1, lam_k1, lam_q2, lam_k2, g_subln,
           w_fourier, w_out, w_ffn1_gu, w_ffn1_down, w_ffn2_gu, w_ffn2_down, ln_g, ln_b, _stage=99):
    f32 = np.float32
    A_ = lambda a: np.ascontiguousarray(np.asarray(a, dtype=f32))
    x_prompt = A_(x_prompt); x_sample = A_(x_sample)
    cache_a_k = A_(cache_a_k); cache_a_v = A_(cache_a_v); cache_b_k = A_(cache_b_k); cache_b_v = A_(cache_b_v)
    c = A_(c); c_ctx = A_(c_ctx)
    if _stage not in _NC_CACHE:
        _NC_CACHE[_stage] = build(_stage)
    nc = _NC_CACHE[_stage]
    CL, SL, C256, S256, BCS = _consts()
    cos, sins = _rope_tables(4096)
    bfl = ml_dtypes.bfloat16
    shared = {
        "w_in": A_(w_in), "w_out": A_(w_out), "w_f": A_(w_fourier),
        "w1gu": A_(w_ffn1_gu), "w1d": A_(w_ffn1_down), "w2gu": A_(w_ffn2_gu), "w2d": A_(w_ffn2_down),
        "lnT": np.ascontiguousarray(np.stack([A_(ln_g), A_(ln_b)]).reshape(2, 2, 3, 8, 128).transpose(4, 0, 1, 2, 3)),
        "gqk": np.ascontiguousarray(np.broadcast_to(
            np.concatenate([np.repeat(A_(g_qa)[:, None, :], 4, 1), np.repeat(A_(g_ka)[:, None, :], 2, 1)], 1)[None], (128, 2, 6, 64))),
        "gsubT": np.ascontiguousarray(A_(g_subln).T),
        "lamv": np.ascontiguousarray(np.broadcast_to(np.stack([A_(lam_q1), A_(lam_k1), A_(lam_q2), A_(lam_k2)], 1)[None], (128, 2, 4, 64))),
        "C256": C256, "S256": S256, "BCS": BCS,
        "identf": np.eye(128, dtype=f32), "identb": np.eye(128, dtype=f32).astype(bfl), "onesb": np.ones((128, 128), dtype=bfl),
    }
    w_mod_f = A_(w_mod)
    bmodT_full = A_(b_mod).reshape(2, 72, 128).transpose(2, 0, 1)
    in_maps = []
    for i in range(8):
        b, r = i // 4, i % 4
        xin = np.concatenate([x_prompt[2 * i], x_prompt[2 * i + 1], x_sample[b, r * 1024:(r + 1) * 1024]], 0)
        cv = np.stack([c_ctx, c[0], c[1]], 0)
        m = dict(shared)
        m["xin"] = np.ascontiguousarray(xin)
        m["cvT"] = np.ascontiguousarray(cv.reshape(3, 8, 128).transpose(2, 1, 0))
        m["wmod"] = np.ascontiguousarray(w_mod_f[:, :, 2304 * r:2304 * (r + 1)])
        m["bmodT"] = np.ascontiguousarray(bmodT_full[:, :, 18 * r:18 * (r + 1)])
        selv = np.zeros((128, 2), f32); selv[:, b] = 1.0
        m["sel"] = selv
        m["ropec"] = np.ascontiguousarray(cos[r * 1024:(r + 1) * 1024].reshape(8, 128, 64).transpose(1, 0, 2))
        m["ropes"] = np.ascontiguousarray(sins[r * 1024:(r + 1) * 1024].reshape(8, 128, 64).transpose(1, 0, 2))
        m["cak"] = np.ascontiguousarray(cache_a_k[b].reshape(2, 256, 128)); m["cav"] = np.ascontiguousarray(cache_a_v[b].reshape(2, 256, 128))
        m["cbk"] = np.ascontiguousarray(cache_b_k[b].reshape(2, 256, 512)); m["cbv"] = np.ascontiguousarray(cache_b_v[b].reshape(2, 256, 512))
        m["CL"] = np.ascontiguousarray(CL[:, r * 1024:(r + 1) * 1024]); m["SL"] = np.ascontiguousarray(SL[:, r * 1024:(r + 1) * 1024])
        in_maps.append(m)
    res = run_bass_kernel_spmd(nc, in_maps[:NCORES], core_ids=list(range(NCORES)))
    y_prompt = np.zeros((16, 256, 1024), f32); y_sample = np.zeros((2, 4096, 1024), f32)
    nak = np.zeros((16, 2, 256, 2, 64), f32); nav = np.zeros((16, 2, 256, 2, 64), f32)
    nbk = np.zeros((16, 2, 256, 4, 128), f32); nbv = np.zeros((16, 2, 256, 4, 128), f32)
    for i in range(NCORES):
        b, r = i // 4, i % 4
        o = res.results[i]
        yy = np.asarray(o["y"], dtype=f32)
        y_prompt[2 * i] = yy[0:256]; y_prompt[2 * i + 1] = yy[256:512]
        y_sample[b, r * 1024:(r + 1) * 1024] = yy[512:1536]
        for s in range(2):
            for l in range(2):
                nak[2 * i + s, l] = np.asarray(o["nak"])[l, s * 256:(s + 1) * 256].reshape(256, 2, 64)
                nav[2 * i + s, l] = np.asarray(o["nav"])[l, s * 256:(s + 1) * 256].reshape(256, 2, 64)
                nbk[2 * i + s, l] = np.asarray(o["nbk"])[l, s * 256:(s + 1) * 256].reshape(256, 4, 128)
                nbv[2 * i + s, l] = np.asarray(o["nbv"])[l, s * 256:(s + 1) * 256].reshape(256, 4, 128)
    return (y_prompt, y_sample, nak, nav, nbk, nbv)
```

```python
import math
import numpy as np
import ml_dtypes
from contextlib import ExitStack
import concourse.bass as bass
import concourse.mybir as mybir
from concourse.bass_utils import run_bass_kernel_spmd

F32 = mybir.dt.float32
BF16 = mybir.dt.bfloat16
AF = mybir.ActivationFunctionType
ALU = mybir.AluOpType
AX = mybir.AxisListType

ENGS = ('pe', 'act', 'dve', 'pool', 'sp')


class Buf:
    __slots__ = ('name', 'w', 'r')

    def __init__(self, name=''):
        self.name = name
        self.w = None
        self.r = []


class Op:
    __slots__ = ('eng', 'fn', 'deps', 'idx', 'signal', 'count', 'is_dma', 'sem', 'val',
                 'waits', 'known_after', 'id', 'inc', 'own')

    def __init__(self, eng, fn, is_dma, inc=16):
        self.eng = eng
        self.fn = fn
        self.is_dma = is_dma
        self.deps = []
        self.signal = False
        self.count = 0
        self.sem = None
        self.val = 0
        self.waits = []
        self.known_after = None
        self.inc = inc
        self.own = None


class Prog:
    def __init__(self, n_dma_sems=12):
        self.ops = {e: [] for e in ENGS}
        self.all = []
        self.n_dma_sems = n_dma_sems

    def add(self, eng, fn, r=(), w=(), dma=False, inc=16):
        op = Op(eng, fn, dma, inc)
        op.id = len(self.all)
        deps = {}
        for b in r:
            if b.w is not None:
                deps[b.w.id] = b.w
        for b in w:
            if b.w is not None:
                deps[b.w.id] = b.w
            for x in b.r:
                deps[x.id] = x
        for b in r:
            b.r.append(op)
        for b in w:
            b.w = op
            b.r = []
        op.deps = list(deps.values())
        op.idx = len(self.ops[eng])
        self.ops[eng].append(op)
        self.all.append(op)
        return op

    def dma(self, queue, fn, r=(), w=(), inc=16):
        return self.add(queue, fn, r, w, dma=True, inc=inc)

    def barrier(self, bufs):
        pass

    def plan(self):
        known = {e: {} for e in ENGS}
        waited_dma = {e: set() for e in ENGS}
        ring = {e: [None] * self.n_dma_sems for e in ENGS}
        ring_pos = {e: 0 for e in ENGS}
        ring_val = {e: [0] * self.n_dma_sems for e in ENGS}
        for op in self.all:
            e = op.eng
            kn = known[e]
            waits = []
            changed = False
            for d in sorted(op.deps, key=lambda d: (not d.is_dma and d.eng == e)):
                if d.is_dma:
                    if d.id in waited_dma[e]:
                        continue
                    waited_dma[e].add(d.id)
                    waits.append(('dma', d))
                    if d.known_after:
                        for k, v in d.known_after.items():
                            if kn.get(k, -1) < v:
                                if not changed:
                                    kn = dict(kn)
                                    changed = True
                                kn[k] = v
                else:
                    if d.eng == e and e == 'pe' and not op.is_dma:
                        continue
                    if kn.get(d.eng, -1) >= d.idx:
                        continue
                    d.signal = True
                    waits.append(('cmp', d))
                    if not changed:
                        kn = dict(kn)
                        changed = True
                    kn[d.eng] = d.idx
                    if d.known_after:
                        for k, v in d.known_after.items():
                            if kn.get(k, -1) < v:
                                kn[k] = v
            if op.is_dma and op.own is not None:
                op.sem = op.own
                op.val = op.inc
            elif op.is_dma:
                pos = ring_pos[e]
                prev = ring[e][pos]
                if prev is not None and prev.id not in waited_dma[e]:
                    waited_dma[e].add(prev.id)
                    waits.append(('dma', prev))
                ring[e][pos] = op
                ring_val[e][pos] += op.inc
                op.sem = (e, pos)
                op.val = ring_val[e][pos]
                ring_pos[e] = (pos + 1) % self.n_dma_sems
            known[e] = kn
            op.waits = waits
            op.known_after = kn
        for e in ENGS:
            c = 0
            for op in self.ops[e]:
                if op.is_dma:
                    continue
                if op.signal:
                    c += 1
                    op.count = c

    def emit(self, nc, block, sems, dma_sems):
        engobj = {'pe': block.tensor, 'act': block.scalar, 'dve': block.vector,
                  'pool': block.gpsimd, 'sp': block.sync}
        for e in ENGS:
            ops = self.ops[e]
            if not ops:
                continue

            def body(eng, ops=ops, e=e):
                for op in ops:
                    wl = {}
                    for kind, d in op.waits:
                        if kind == 'dma':
                            s = dma_sems[d.sem]
                            v = d.val
                        else:
                            s = sems[d.eng]
                            v = d.count
                        key = id(s)
                        if key not in wl or wl[key][1] < v:
                            wl[key] = (s, v)
                    for s, v in wl.values():
                        eng.wait_ge(s, v)
                    ins = op.fn(eng)
                    if op.is_dma:
                        ins.then_inc(dma_sems[op.sem], op.inc)
                    elif op.signal:
                        ins.then_inc(sems[e], 1)
            engobj[e](body)

    def final_waits(self, ops):
        return ops

U8 = mybir.dt.uint8
ALPHA = (2.0 * 2) ** 0.25
LN_EPS = 1e-5
RMS_EPS = 1e-6
FF = 2816
LAM_INIT = [0.8 - 0.6 * math.exp(-0.3 * l) for l in range(2)]
RG = [[0, 1, 2, 3], [4, 5, 6, 7]]
MIXSUB = 9
NCORES = 8
PSUB = 99
GSEL = -1
DBGLIM = -1
MIXDBG = 0


def build(stage=99):
    nc = bass.Bass("TRN2", target_bir_lowering=False)

    def din(name, shape, dt=F32):
        return nc.dram_tensor(name, list(shape), dt, kind="ExternalInput").ap()

    def dout(name, shape, dt=F32):
        return nc.dram_tensor(name, list(shape), dt, kind="ExternalOutput").ap()

    xin = din("xin", [1536, 1024]); cvT_d = din("cvT", [128, 8, 3]); wmod = din("wmod", [2, 1024, 2304]); sel_d = din("sel", [128, 2])
    bmodT_d = din("bmodT", [128, 2, 18])
    w_in = din("w_in", [2, 1024, 2304]); w_out = din("w_out", [2, 1024, 1024]); w_f = din("w_f", [2, 256, 256])
    w1gu = din("w1gu", [2, 1024, 5632]); w1d = din("w1d", [2, 2816, 1024])
    w2gu = din("w2gu", [2, 1024, 5632]); w2d = din("w2d", [2, 2816, 1024])
    lnT_d = din("lnT", [128, 2, 2, 3, 8]); gqk_d = din("gqk", [128, 2, 6, 64]); gsubT_d = din("gsubT", [128, 2])
    lamv_d = din("lamv", [128, 2, 4, 64]); ropec_d = din("ropec", [128, 8, 64]); ropes_d = din("ropes", [128, 8, 64])
    cak = din("cak", [2, 256, 128]); cav = din("cav", [2, 256, 128]); cbk = din("cbk", [2, 256, 512]); cbv = din("cbv", [2, 256, 512])
    CLd = din("CL", [4096, 1024], BF16); SLd = din("SL", [4096, 1024], BF16)
    C256d = din("C256", [256, 256], BF16); S256d = din("S256", [256, 256], BF16); BCSd = din("BCS", [4, 128, 128], BF16)
    identf_d = din("identf", [128, 128]); identb_d = din("identb", [128, 128], BF16); onesb_d = din("onesb", [128, 128], BF16)
    y = dout("y", [1536, 1024]); nak = dout("nak", [2, 512, 128]); nav = dout("nav", [2, 512, 128])
    nbk = dout("nbk", [2, 512, 512]); nbv = dout("nbv", [2, 512, 512])
    gin_kt = [nc.dram_tensor("gin_kt%d" % h, [384, 1024], BF16).ap() for h in range(2)]
    gout_kt = [nc.dram_tensor("gout_kt%d" % h, [1536, 1024], BF16).ap() for h in range(2)]
    gin_vu = [nc.dram_tensor("gin_vu%d" % h, [512, 1024], BF16).ap() for h in range(2)]
    gout_vu = [nc.dram_tensor("gout_vu%d" % h, [2048, 1024], BF16).ap() for h in range(2)]

    gin_mod = nc.dram_tensor("gin_mod", [128, 108], F32).ap()
    gout_mod = nc.dram_tensor("gout_mod", [512, 108], F32).ap()
    P = Prog()
    dbg = {'on': False, 'n': 0}
    _orig_add = P.add

    def _add_dbg(eng, fn, r=(), w=(), dma=False, inc=16):
        if dbg['on'] and DBGLIM >= 0:
            dbg['n'] += 1
            if dbg['n'] > DBGLIM:
                return None
        return _orig_add(eng, fn, r, w, dma, inc)
    P.add = _add_dbg
    es = ExitStack()
    base0 = (nc.sbuf_base + 63) // 64 * 64
    avail = nc.sbuf_top - base0 - 64
    arena = es.enter_context(nc.sbuf_tensor("arena", [128, avail], U8))
    cur = [base0]
    lim = base0 + avail

    def esz(dt):
        return 4 if dt == F32 else 2

    def alloc(name, shape, dt, at=None):
        nb = int(np.prod(shape[1:])) * esz(dt)
        nb = (nb + 63) // 64 * 64
        if at is None:
            o = cur[0]
            cur[0] += nb
            assert cur[0] <= lim, (name, cur[0], lim)
        else:
            o = at
        t = nc.alloc_sbuf_tensor_at(name + "_%d" % o, list(shape), dt, offset=o)
        return t

    class Region:
        def __init__(self, size):
            self.base = cur[0]
            self.size = size
            cur[0] += size
            assert cur[0] <= lim, ("region", cur[0], lim)
            self.p = self.base
            self.n = 0

        def reset(self):
            self.p = self.base

        def alloc(self, name, shape, dt):
            nb = int(np.prod(shape[1:])) * esz(dt)
            nb = (nb + 63) // 64 * 64
            o = self.p
            self.p += nb
            assert self.p <= self.base + self.size, (name, self.p - self.base, self.size)
            self.n += 1
            return nc.alloc_sbuf_tensor_at("%s_r%d" % (name, self.n), list(shape), dt, offset=o)

    XRES = alloc("XRES", [128, 8, 1536], F32)
    HB = alloc("HB", [128, 8, 1536], BF16)
    WSr = Region(32768)
    WS = [WSr.alloc("WS%d" % i, [128, 8, 512], BF16) for i in range(4)]
    BIG = Region(56320)
    TMP = Region(20480)
    identf = alloc("identf", [128, 128], F32); identb = alloc("identb", [128, 128], BF16); onesb = alloc("onesb", [128, 128], BF16)
    modT = alloc("modT", [128, 2, 72, 2], F32); scT = alloc("scT", [128, 8, 3], BF16); cvT = alloc("cvT", [128, 8, 3], F32)
    modsl = alloc("modsl", [128, 108], F32); modall = alloc("modall", [128, 4, 108], F32); mtmp = alloc("mtmp", [128, 4, 18], F32); sel = alloc("sel", [128, 2], F32)
    bmodT = alloc("bmodT", [128, 2, 18], F32)
    lnT = alloc("lnT", [128, 2, 2, 3, 8], F32); lnA = alloc("lnA", [128, 2, 2, 3, 8], F32)
    gqk = alloc("gqk", [128, 2, 6, 64], F32); gsub = alloc("gsub", [128, 2], F32); neglam = alloc("neglam", [128, 2], F32)
    lamv = alloc("lamv", [128, 2, 4, 64], F32); lamt = alloc("lamt", [128, 4], F32); lamp = alloc("lamp", [128, 64], F32)
    ropec = alloc("ropec", [128, 8, 64], F32); ropes = alloc("ropes", [128, 8, 64], F32)
    KTc = alloc("KTc", [128, 6, 256], BF16); Vc = alloc("Vc", [128, 2, 768], BF16)
    C256 = alloc("C256", [128, 2, 256], BF16); S256 = alloc("S256", [128, 2, 256], BF16); BCS = alloc("BCS", [128, 4, 128], BF16)
    wf = alloc("wf", [128, 2, 256], BF16)
    epsln = alloc("epsln", [128, 1], F32); epsrms = alloc("epsrms", [128, 1], F32)
    ps = [nc.alloc_psum_tensor("ps%d" % i, [128, 512], F32) for i in range(8)]

    bps = [Buf("ps%d" % i) for i in range(8)]
    bx = [[Buf() for _ in range(3)] for _ in range(8)]
    bh = [[Buf() for _ in range(3)] for _ in range(8)]
    bws = [Buf() for _ in range(4)]
    bufs = {}

    def bf(name):
        if name not in bufs:
            bufs[name] = Buf(name)
        return bufs[name]

    def A(eng, fn, r=(), w=()):
        return P.add(eng, fn, r, w)

    def mm(out, lhsT, rhs, st, sp, r, w):
        return P.add('pe', lambda e: e.matmul(out, lhsT=lhsT, rhs=rhs, start=st, stop=sp), r, w)

    def tr(out, in_, ident, r, w):
        return P.add('pe', lambda e: e.transpose(out=out, in_=in_, identity=ident), r, w)

    def ld(q, dst, src, w, r=()):
        return P.dma(q, lambda e: e.dma_start(out=dst, in_=src), r=r, w=w)

    def fence():
        last = [P.ops[e][-1] for e in ENGS if P.ops[e]]
        for e in ('pe', 'act', 'dve', 'pool', 'sp'):
            op = P.add(e, lambda eng: eng.nop(), (), ())
            op.deps = list(last)
        return

    def tbs(tb):
        return slice(tb * 512, (tb + 1) * 512)

    def mod(l, i, c, v):
        return modT[:, l, i * 8 + c, v:v + 1]

    MIXT = HB
    outs = []
    wsc = [0]

    def next_ws():
        s = wsc[0] % 4
        wsc[0] += 1
        return s

    bankc = [0]
    unitc = [0]

    def next_bank():
        b = bankc[0] % 8
        bankc[0] += 1
        return b

    ld('sp', identf[:], identf_d[:, :], [bf('identf')]); ld('sp', identb[:], identb_d[:, :], [bf('identb')])
    ld('sp', onesb[:], onesb_d[:, :], [bf('onesb')]); ld('sp', cvT[:], cvT_d[:, :, :], [bf('cvT')])
    ld('sp', bmodT[:], bmodT_d[:, :, :], [bf('bmodT')]); ld('sp', sel[:], sel_d[:, :], [bf('sel')]); ld('sp', lnT[:], lnT_d[:, :, :, :, :], [bf('lnT')])
    ld('sp', gqk[:], gqk_d[:, :, :, :], [bf('gqk')]); ld('sp', gsub[:], gsubT_d[:, :], [bf('gsub')])
    ld('sp', lamv[:], lamv_d[:, :, :, :], [bf('lamv')]); ld('sp', ropec[:], ropec_d[:, :, :], [bf('rope')])
    ld('sp', ropes[:], ropes_d[:, :, :], [bf('rope')])
    ld('sp', C256[:], C256d.rearrange("(t p) n -> p t n", p=128), [bf('C256')])
    ld('sp', S256[:], S256d.rearrange("(t p) n -> p t n", p=128), [bf('S256')])
    ld('sp', BCS[:], BCSd.rearrange("k p n -> p k n"), [bf('BCS')])
    A('dve', lambda e: e.memset(epsln[:], LN_EPS), w=[bf('eps')])
    A('dve', lambda e: e.memset(epsrms[:], RMS_EPS), w=[bf('eps')])
    A('act', lambda e: e.activation(out=scT[:], in_=cvT[:], func=AF.Silu), r=[bf('cvT')], w=[bf('scT')])
    A('dve', lambda e: e.tensor_scalar(out=lnA[:], in0=lnT[:], scalar1=ALPHA, scalar2=None, op0=ALU.mult), r=[bf('lnT')], w=[bf('lnA')])
    for l in range(2):
        for j in range(2):
            A('dve', lambda e, l=l, j=j: e.tensor_tensor(out=lamp[:], in0=lamv[:, l, 2 * j, :], in1=lamv[:, l, 2 * j + 1, :], op=ALU.mult),
              r=[bf('lamv')], w=[bf('lamp')])
            A('dve', lambda e, l=l, j=j: e.tensor_reduce(out=lamt[:, 2 * l + j:2 * l + j + 1], in_=lamp[:], op=ALU.add, axis=AX.X),
              r=[bf('lamp')], w=[bf('lamt')])
    A('act', lambda e: e.activation(out=lamt[:], in_=lamt[:], func=AF.Exp), r=[bf('lamt')], w=[bf('lamt')])
    for l in range(2):
        A('dve', lambda e, l=l: e.tensor_tensor(out=neglam[:, l:l + 1], in0=lamt[:, 2 * l + 1:2 * l + 2], in1=lamt[:, 2 * l:2 * l + 1], op=ALU.subtract),
          r=[bf('lamt')], w=[bf('neglam')])
        A('dve', lambda e, l=l: e.tensor_scalar(out=neglam[:, l:l + 1], in0=neglam[:, l:l + 1], scalar1=-LAM_INIT[l], scalar2=None, op0=ALU.add),
          r=[bf('neglam')], w=[bf('neglam')])
        A('dve', lambda e, l=l: e.tensor_scalar(out=gsub[:, l:l + 1], in0=gsub[:, l:l + 1], scalar1=1.0 - LAM_INIT[l], scalar2=None, op0=ALU.mult),
          r=[bf('gsub')], w=[bf('gsub')])

    for l in range(2):
        wv = wmod[l].rearrange("(c p) n -> p c n", p=128)
        for (c0, ncol) in ((0, 512), (512, 512), (1024, 512), (1536, 512), (2048, 256)):
            s_ = next_ws()
            ld('pool', WS[s_][:, :, 0:ncol], wv[:, :, c0:c0 + ncol], [bws[s_]])
            for cc in range(ncol // 128):
                col = (l * 18 + c0 // 128 + cc) * 3
                for k in range(8):
                    mm(ps[0][:, col:col + 3], WS[s_][:, k, cc * 128:(cc + 1) * 128], scT[:, k, :], k == 0, k == 7,
                       [bws[s_], bf('scT')], [bps[0]])
    A('dve', lambda e: e.tensor_tensor(out=modsl[:].rearrange("p (m v) -> p m v", v=3), in0=ps[0][:, 0:108].rearrange("p (m v) -> p m v", v=3),
                                       in1=bmodT[:].rearrange("p l m -> p (l m)").unsqueeze(2).to_broadcast([128, 36, 3]), op=ALU.add),
      r=[bps[0], bf('bmodT')], w=[bf('modsl')])
    P.dma('sp', lambda e: e.dma_start(out=gin_mod[:, :], in_=modsl[:]), r=[bf('modsl')], w=[bf('gin_mod')])
    ccm = P.dma('pool', lambda e: e.collective_compute("AllGather", ALU.bypass, replica_groups=RG, ins=[gin_mod.opt()], outs=[gout_mod.opt()]),
                r=[bf('gin_mod')], w=[bf('gout_mod')], inc=1)
    ccm.own = ('cc', 8)
    P.add('pool', lambda e: e.nop(), [bf('gout_mod')], [bf('gout_mod')])
    ld('sp', modall[:], gout_mod.rearrange("(r p) n -> p r n", p=128), [bf('modall')], r=[bf('gout_mod')])
    modall5 = modall[:].rearrange("p r (l m v) -> p r l m v", l=2, m=18, v=3)
    for l in range(2):
        mv = [modT[:, l, :, vv].rearrange("p (r m) -> p r m", m=18) for vv in range(2)]
        A('dve', lambda e, l=l, mv=mv: e.tensor_copy(out=mv[0], in_=modall5[:, :, l, :, 0]), r=[bf('modall')], w=[bf('mod')])
        A('dve', lambda e, l=l: e.tensor_scalar(out=mtmp[:], in0=modall5[:, :, l, :, 1], scalar1=sel[:, 0:1], scalar2=None, op0=ALU.mult),
          r=[bf('modall'), bf('sel')], w=[bf('mtmp')])
        A('dve', lambda e, l=l, mv=mv: e.scalar_tensor_tensor(out=mv[1], in0=modall5[:, :, l, :, 2], scalar=sel[:, 1:2], in1=mtmp[:], op0=ALU.mult, op1=ALU.add),
          r=[bf('modall'), bf('sel'), bf('mtmp')], w=[bf('mod')])
        for i in (1, 4, 7):
            A('dve', lambda e, l=l, i=i: e.tensor_scalar(out=modT[:, l, i * 8:(i + 1) * 8, :], in0=modT[:, l, i * 8:(i + 1) * 8, :],
                                                         scalar1=1.0, scalar2=1.0 / ALPHA, op0=ALU.add, op1=ALU.mult),
              r=[bf('mod')], w=[bf('mod')])
        for i in (2, 8):
            A('dve', lambda e, l=l, i=i: e.tensor_scalar(out=modT[:, l, i * 8:(i + 1) * 8, :], in0=modT[:, l, i * 8:(i + 1) * 8, :],
                                                         scalar1=0.5, scalar2=None, op0=ALU.mult),
              r=[bf('mod')], w=[bf('mod')])

    TMP.reset()
    xstg = [TMP.alloc("xstg", [128, 1024], F32) for _ in range(2)]
    for t in range(12):
        st = xstg[t % 2]
        ld('sp', st[:], xin[t * 128:(t + 1) * 128, :], [bf('xstg%d' % (t % 2))])
        tb = t // 4
        for half in range(2):
            b = next_bank()
            for cc in range(4):
                c = half * 4 + cc
                tr(ps[b][:, cc * 128:(cc + 1) * 128], st[:, c * 128:(c + 1) * 128], identf[:],
                   [bf('xstg%d' % (t % 2)), bf('identf')], [bps[b]])
            dst = XRES[:, half * 4:half * 4 + 4, t * 128:(t + 1) * 128]
            src = ps[b][:].rearrange("p (c t) -> p c t", t=128)
            wl = [bx[half * 4 + cc][tb] for cc in range(4)]
            if half == 0:
                A('act', lambda e, dst=dst, src=src: e.activation(out=dst, in_=src, func=AF.Copy, scale=ALPHA), r=[bps[b]], w=wl)
            else:
                A('dve', lambda e, dst=dst, src=src: e.tensor_scalar(out=dst, in0=src, scalar1=ALPHA, scalar2=None, op0=ALU.mult), r=[bps[b]], w=wl)

    def layernorm(l, i, final):
        TMP.reset()
        mean_t = TMP.alloc("mean", [128, 512], F32); t1 = TMP.alloc("t1", [128, 512], F32)
        rstd_t = TMP.alloc("rstd", [128, 512], F32)
        tm2 = [TMP.alloc("tm2", [128, 512], F32) for _ in range(2)]
        ZSQ = TMP.alloc("zsq", [128, 8, 512], BF16)
        lnp = lnT if final else lnA
        for tb in range(3):
            for c in range(8):
                A('act', lambda e, c=c, tb=tb: e.activation(out=HB[:, c, tbs(tb)], in_=XRES[:, c, tbs(tb)], func=AF.Copy),
                  r=[bx[c][tb]], w=[bh[c][tb]])
                A('act', lambda e, c=c, tb=tb: e.activation(out=ZSQ[:, c, :], in_=XRES[:, c, tbs(tb)], func=AF.Square),
                  r=[bx[c][tb]], w=[bf('zsq%d' % c)])
            b_s = next_bank(); b_q = next_bank()
            for c in range(8):
                mm(ps[b_s][:], onesb[:], HB[:, c, tbs(tb)], c == 0, c == 7, [bh[c][tb], bf('onesb')], [bps[b_s]])
            for c in range(8):
                mm(ps[b_q][:], onesb[:], ZSQ[:, c, :], c == 0, c == 7, [bf('zsq%d' % c), bf('onesb')], [bps[b_q]])
            A('act', lambda e, b_s=b_s: e.activation(out=mean_t[:], in_=ps[b_s][:], func=AF.Copy, scale=1.0 / 1024), r=[bps[b_s]], w=[bf('mean')])
            A('dve', lambda e: e.tensor_tensor(out=t1[:], in0=mean_t[:], in1=mean_t[:], op=ALU.mult), r=[bf('mean')], w=[bf('t1')])
            A('dve', lambda e, b_q=b_q: e.scalar_tensor_tensor(out=t1[:], in0=ps[b_q][:], scalar=1.0 / 1024, in1=t1[:], op0=ALU.mult, op1=ALU.subtract),
              r=[bps[b_q], bf('t1')], w=[bf('t1')])
            A('act', lambda e: e.activation(out=t1[:], in_=t1[:], func=AF.Ln, bias=epsln[:, 0:1], scale=1.0), r=[bf('t1'), bf('eps')], w=[bf('t1')])
            b_r = next_bank()
            A('act', lambda e, b_r=b_r: e.activation(out=ps[b_r][:], in_=t1[:], func=AF.Exp, scale=-0.5), r=[bf('t1')], w=[bps[b_r]])
            for c in range(8):
                tm = tm2[c % 2]; btm = bf('tm2%d' % (c % 2))
                if c in (2, 5, 7):
                    A('dve', lambda e, c=c, tb=tb, tm=tm, b_s=b_s: e.scalar_tensor_tensor(out=tm[:], in0=ps[b_s][:], scalar=-1.0 / 1024, in1=XRES[:, c, tbs(tb)],
                                                                                     op0=ALU.mult, op1=ALU.add),
                      r=[bx[c][tb], bps[b_s]], w=[btm])
                else:
                    A('pool', lambda e, c=c, tb=tb, tm=tm: e.tensor_tensor(out=tm[:], in0=XRES[:, c, tbs(tb)], in1=mean_t[:], op=ALU.subtract),
                      r=[bx[c][tb], bf('mean')], w=[btm])
                A('dve', lambda e, tm=tm, b_r=b_r: e.tensor_tensor(out=tm[:], in0=tm[:], in1=ps[b_r][:], op=ALU.mult), r=[btm, bps[b_r]], w=[btm])
                A('act', lambda e, c=c, tb=tb, tm=tm: e.activation(out=XRES[:, c, tbs(tb)], in_=tm[:], func=AF.Identity,
                                                                   scale=lnp[:, 0, l, i, c:c + 1], bias=lnp[:, 1, l, i, c:c + 1]),
                  r=[btm, bf('lnT'), bf('lnA')], w=[bx[c][tb]])

    def modulate(l, si):
        for tb in range(3):
            v = 0 if tb == 0 else 1
            for c in range(8):
                A('act', lambda e, c=c, tb=tb, v=v: e.activation(out=HB[:, c, tbs(tb)], in_=XRES[:, c, tbs(tb)], func=AF.Identity,
                                                                 scale=mod(l, 3 * si + 1, c, v), bias=mod(l, 3 * si, c, v)),
                  r=[bx[c][tb], bf('mod')], w=[bh[c][tb]])

    FGROUPS = ((0, 4), (4, 4), (8, 3))
    pf = {}

    def ffn_prefetch(l, si, wgu, wd):
        BIG.reset()
        AH = BIG.alloc("AH", [128, 11, 1536], BF16); WD = BIG.alloc("WD", [128, 11, 1024], BF16)
        bwd = [bf('wd%d' % g) for g in range(3)]
        wguv = wgu[l].rearrange("(c p) n -> p c n", p=128)
        wdv = wd[l].rearrange("(j p) n -> p j n", p=128)
        slots = []
        for gi, (a, n) in enumerate(FGROUPS[:2]):
            sg_ = next_ws(); su_ = next_ws()
            col0 = a * 128; ncol = n * 128
            ld('pool', WS[sg_][:, :, 0:ncol], wguv[:, :, col0:col0 + ncol], [bws[sg_]])
            ld('pool', WS[su_][:, :, 0:ncol], wguv[:, :, FF + col0:FF + col0 + ncol], [bws[su_]])
            slots.append((sg_, su_))
        for gi, (a, n) in enumerate(FGROUPS):
            ld('pool', WD[:, a:a + n, :], wdv[:, a:a + n, :], [bwd[gi]])
        pf[(l, si)] = (AH, WD, slots)

    def mix_prefetch(l):
        winv = w_in[l].rearrange("(c p) n -> p c n", p=128)
        for g in range(4):
            ld('pool', WS[g][:], winv[:, :, g * 512:(g + 1) * 512], [bws[g]])
        wsc[0] = 0
        pf[('mix', l)] = True

    def ffn(l, si, wgu, wd, ln_i, final, after=None):
        modulate(l, si)
        TMP.reset()
        pre = pf.pop((l, si), None)
        if pre is None:
            BIG.reset()
            AH = BIG.alloc("AH", [128, 11, 1536], BF16); WD = BIG.alloc("WD", [128, 11, 1024], BF16)
            pslots = []
        else:
            AH, WD, pslots = pre
        sgt = [TMP.alloc("sg", [128, 512], F32) for _ in range(4)]
        ba = [[bf('ah%d_%d' % (j, tb)) for tb in range(3)] for j in range(11)]
        bwd = [bf('wd%d' % g) for g in range(3)]
        groups = FGROUPS
        wguv = wgu[l].rearrange("(c p) n -> p c n", p=128)
        wdv = wd[l].rearrange("(j p) n -> p j n", p=128)
        prc = 0
        for half in range(2):
            j0 = half * 11
            if not (half == 0 and pre is not None):
                for gi, (a, n) in enumerate(groups):
                    ld('pool', WD[:, a:a + n, :], wdv[:, j0 + a:j0 + a + n, :], [bwd[gi]])
            for gi, (a, n) in enumerate(groups):
                col0 = (j0 + a) * 128; ncol = n * 128
                if half == 0 and gi < len(pslots):
                    sg_, su_ = pslots[gi]
                else:
                    sg_ = next_ws(); su_ = next_ws()
                    ld('pool', WS[sg_][:, :, 0:ncol], wguv[:, :, col0:col0 + ncol], [bws[sg_]])
                    ld('pool', WS[su_][:, :, 0:ncol], wguv[:, :, FF + col0:FF + col0 + ncol], [bws[su_]])
                for jj in range(n):
                    j = a + jj
                    for tb in range(3):
                        pr = prc % 4; prc += 1
                        gps = ps[2 * pr]; ups = ps[2 * pr + 1]
                        for k in range(8):
                            mm(gps[:], WS[sg_][:, k, jj * 128:(jj + 1) * 128], HB[:, k, tbs(tb)], k == 0, k == 7, [bws[sg_], bh[k][tb]], [bps[2 * pr]])
                        for k in range(8):
                            mm(ups[:], WS[su_][:, k, jj * 128:(jj + 1) * 128], HB[:, k, tbs(tb)], k == 0, k == 7, [bws[su_], bh[k][tb]], [bps[2 * pr + 1]])
                        A('act', lambda e, pr=pr, gps=gps: e.activation(out=sgt[pr][:], in_=gps[:], func=AF.Silu), r=[bps[2 * pr]], w=[bf('sg%d' % pr)])
                        A('dve', lambda e, pr=pr, ups=ups, j=j, tb=tb: e.tensor_tensor(out=AH[:, j, tbs(tb)], in0=sgt[pr][:], in1=ups[:], op=ALU.mult),
                          r=[bf('sg%d' % pr), bps[2 * pr + 1]], w=[ba[j][tb]])
            for tb in range(3):
                v = 0 if tb == 0 else 1
                for m in range(8):
                    b = next_bank()
                    for j in range(11):
                        mm(ps[b][:], WD[:, j, m * 128:(m + 1) * 128], AH[:, j, tbs(tb)], j == 0, j == 10, [bwd[min(j // 4, 2)], ba[j][tb]], [bps[b]])
                    A('dve', lambda e, b=b, m=m, tb=tb, v=v: e.scalar_tensor_tensor(out=XRES[:, m, tbs(tb)], in0=ps[b][:], scalar=mod(l, 3 * si + 2, m, v),
                                                                                 in1=XRES[:, m, tbs(tb)], op0=ALU.mult, op1=ALU.add),
                      r=[bps[b], bx[m][tb], bf('mod')], w=[bx[m][tb]])
        if after is not None:
            after()
        layernorm(l, ln_i, final)
        fence()

    def store_y():
        TMP.reset()
        ystg = [TMP.alloc("ystg", [128, 1024], F32) for _ in range(2)]
        for t in range(12):
            tb = t // 4
            st = ystg[t % 2]; bst = bf('ystg%d' % (t % 2))
            for half in range(2):
                b = next_bank()
                for cc in range(4):
                    c = half * 4 + cc
                    tr(ps[b][:, cc * 128:(cc + 1) * 128], XRES[:, c, t * 128:(t + 1) * 128], identf[:], [bx[c][tb], bf('identf')], [bps[b]])
                if half == 0:
                    A('act', lambda e, b=b, st=st: e.activation(out=st[:, 0:512], in_=ps[b][:], func=AF.Copy), r=[bps[b]], w=[bst])
                else:
                    A('dve', lambda e, b=b, st=st: e.tensor_copy(out=st[:, 512:1024], in_=ps[b][:]), r=[bps[b]], w=[bst])
            outs.append(P.dma('sp', lambda e, st=st, t=t: e.dma_start(out=y[t * 128:(t + 1) * 128, :], in_=st[:]), r=[bst]))

    def rope(src3, dst3, H, ts, rset, rbufs, wbufs):
        r1, r2, n1, n2 = rset
        cosb = ropec[:, ts, :].unsqueeze(1).to_broadcast([128, H, 64])
        ssv = ropes[:, ts, :].rearrange("p (a b i) -> p a b i", a=2, b=2)
        r1v = r1[:, 0:H * 64].rearrange("p (h d) -> p h d", d=64)
        r2v = r2[:, 0:H * 64].rearrange("p (h a b i) -> p h a b i", a=2, b=2, i=16)
        s5 = src3.rearrange("p h (a b i) -> p h a b i", a=2, b=2)
        A('dve', lambda e: e.tensor_tensor(out=r1v, in0=src3, in1=cosb, op=ALU.mult), r=rbufs + [bf('rope')], w=[bf(n1)])
        for blk in range(2):
            A('dve', lambda e, blk=blk: e.tensor_tensor(out=r2v[:, :, :, blk, :], in0=s5[:, :, :, 1 - blk, :],
                                                        in1=ssv[:, :, blk, :].unsqueeze(1).to_broadcast([128, H, 2, 16]), op=ALU.mult),
              r=rbufs + [bf('rope')], w=[bf(n2)])
        A('pool', lambda e: e.tensor_tensor(out=dst3, in0=r1v, in1=r2[:, 0:H * 64].rearrange("p (h d) -> p h d", d=64), op=ALU.add),
          r=[bf(n1), bf(n2)], w=wbufs)

    def attn_unit(l, kind, segs, oc, tok0, T, qz_eng='pool', pool_share=False):
        pT, rec, t1, t2, sq, Qzs, accP = T
        S = [0, 1, 2, 3]
        acc = [(4, 5), (6, 7)]
        nq = sum(sg[5] for sg in segs)
        u = unitc[0] % 2
        unitc[0] += 1
        Qz = Qzs[u]
        for (KT, V, nkt, Qc, col0, ncols) in segs:
            for g in range(2):
                if qz_eng == 'pool':
                    A('pool', lambda e, g=g, Qc=Qc, col0=col0, ncols=ncols: e.tensor_copy(out=Qz[g][g * 64:(g + 1) * 64, col0:col0 + ncols], in_=Qc[g * 64:(g + 1) * 64, :]),
                      r=[bf('QT'), bf('qzero')], w=[bf('Qz%d_%d' % (u, g))])
                else:
                    A('act', lambda e, g=g, Qc=Qc, col0=col0, ncols=ncols: e.activation(out=Qz[g][g * 64:(g + 1) * 64, col0:col0 + ncols], in_=Qc[g * 64:(g + 1) * 64, :], func=AF.Copy),
                      r=[bf('QT'), bf('qzero')], w=[bf('Qz%d_%d' % (u, g))])
        its = [(si, kt, g) for si, sg in enumerate(segs) for kt in range(sg[2]) for g in range(2)]
        LOOK = 3

        def emit_s(i):
            si, kt, g = its[i]
            KT, V, nkt, Qc, col0, ncols = segs[si]
            kap, kb_ = KT(kt)
            sb = S[i % 4]; pi = i % 4
            mm(ps[sb][:, 0:ncols], kap, Qz[g][:, col0:col0 + ncols], True, True, kb_ + [bf('Qz%d_%d' % (u, g))], [bps[sb]])
            A('act', lambda e, sb=sb, pi=pi, ncols=ncols: e.activation(out=pT[pi][:, 0:ncols], in_=ps[sb][:, 0:ncols], func=AF.Exp, scale=0.125),
              r=[bps[sb]], w=[bf('pT%d' % pi)])

        def emit_pv(i):
            si, kt, g = its[i]
            KT, V, nkt, Qc, col0, ncols = segs[si]
            vap, vb_ = V(kt)
            pi = i % 4
            ob, db = acc[g]
            mm(ps[ob][:, col0:col0 + ncols], vap, pT[pi][:, 0:ncols], kt == 0, kt == nkt - 1, vb_ + [bf('pT%d' % pi)], [bps[ob]])
            if pool_share and g == 1 and kt % 2 == 1:
                if kt == 1:
                    A('pool', lambda e, pi=pi, col0=col0, ncols=ncols: e.tensor_copy(out=accP[:, col0:col0 + ncols], in_=pT[pi][:, 0:ncols]),
                      r=[bf('pT%d' % pi)], w=[bf('accP')])
                else:
                    A('pool', lambda e, pi=pi, col0=col0, ncols=ncols: e.tensor_tensor(out=accP[:, col0:col0 + ncols], in0=accP[:, col0:col0 + ncols],
                                                                                   in1=pT[pi][:, 0:ncols], op=ALU.add),
                      r=[bf('pT%d' % pi)], w=[bf('accP')])
            elif kt == 0:
                A('dve', lambda e, db=db, pi=pi, col0=col0, ncols=ncols: e.tensor_copy(out=ps[db][:, col0:col0 + ncols], in_=pT[pi][:, 0:ncols]),
                  r=[bf('pT%d' % pi)], w=[bps[db]])
            else:
                A('dve', lambda e, db=db, pi=pi, col0=col0, ncols=ncols: e.tensor_tensor(out=ps[db][:, col0:col0 + ncols], in0=ps[db][:, col0:col0 + ncols],
                                                                                         in1=pT[pi][:, 0:ncols], op=ALU.add),
                  r=[bf('pT%d' % pi)], w=[bps[db]])
        for i in range(min(LOOK, len(its))):
            emit_s(i)
        for i in range(len(its)):
            emit_pv(i)
            if i + LOOK < len(its):
                emit_s(i + LOOK)
        dst = MIXT[:, oc, tok0:tok0 + nq]
        DB = [1, 2]
        if pool_share:
            A('dve', lambda e: e.tensor_tensor(out=ps[acc[1][1]][:, 0:nq], in0=ps[acc[1][1]][:, 0:nq], in1=accP[:, 0:nq], op=ALU.add),
              r=[bf('accP')], w=[bps[acc[1][1]]])
        for g in range(2):
            db = acc[g][1]
            A('act', lambda e, g=g, db=db: e.activation(out=pT[g][:, 0:nq], in_=ps[db][:, 0:nq], func=AF.Copy), r=[bps[db]], w=[bf('pT%d' % g)])
            mm(ps[DB[g]][:, 0:nq], onesb[:], pT[g][:, 0:nq], True, True, [bf('onesb'), bf('pT%d' % g)], [bps[DB[g]]])
        if kind == 'A':
            for g in range(2):
                ob, db = acc[g]
                sl = slice(g * 64, (g + 1) * 64)
                A('act', lambda e, g=g, sl=sl: e.activation(out=rec[sl, 0:nq], in_=ps[DB[g]][sl, 0:nq], func=AF.Ln), r=[bps[DB[g]]], w=[bf('rec%d' % g)])
                A('act', lambda e, g=g, sl=sl: e.activation(out=rec[sl, 0:nq], in_=rec[sl, 0:nq], func=AF.Exp, scale=-1.0), r=[bf('rec%d' % g)], w=[bf('rec%d' % g)])
                A('dve', lambda e, ob=ob, sl=sl: e.tensor_tensor(out=MIXT[sl, oc, tok0:tok0 + nq], in0=ps[ob][sl, 0:nq], in1=rec[sl, 0:nq], op=ALU.mult),
                  r=[bps[ob], bf('rec%d' % g)], w=[bf('MIXT%d_%d' % (oc, tok0))])
        else:
            A('act', lambda e: e.activation(out=rec[:, 0:nq], in_=ps[1][:, 0:nq], func=AF.Ln), r=[bps[1]], w=[bf('rec0')])
            A('act', lambda e: e.activation(out=rec[:, 0:nq], in_=rec[:, 0:nq], func=AF.Exp, scale=-1.0), r=[bf('rec0')], w=[bf('rec0')])
            A('dve', lambda e: e.tensor_tensor(out=t1[:, 0:nq], in0=ps[4][:, 0:nq], in1=rec[:, 0:nq], op=ALU.mult), r=[bps[4], bf('rec0')], w=[bf('at1')])
            A('act', lambda e: e.activation(out=rec[:, 0:nq], in_=ps[2][:, 0:nq], func=AF.Ln), r=[bps[2], bf('at1')], w=[bf('rec0')])
            A('act', lambda e: e.activation(out=rec[:, 0:nq], in_=rec[:, 0:nq], func=AF.Exp, scale=-1.0), r=[bf('rec0')], w=[bf('rec0')])
            A('dve', lambda e: e.tensor_tensor(out=t2[:, 0:nq], in0=ps[6][:, 0:nq], in1=rec[:, 0:nq], op=ALU.mult), r=[bps[6], bf('rec0')], w=[bf('at2')])
            A('dve', lambda e: e.scalar_tensor_tensor(out=t2[:, 0:nq], in0=t2[:, 0:nq], scalar=neglam[:, l:l + 1], in1=t1[:, 0:nq], op0=ALU.mult, op1=ALU.add),
              r=[bf('at1'), bf('at2'), bf('neglam')], w=[bf('at2')])
            A('act', lambda e: e.activation(out=sq[:, 0:nq], in_=t2[:, 0:nq], func=AF.Square), r=[bf('at2')], w=[bf('asq')])
            mm(ps[0][:, 0:nq], onesb[:], sq[:, 0:nq], True, True, [bf('onesb'), bf('asq')], [bps[0]])
            A('act', lambda e: e.activation(out=t1[:, 0:nq], in_=ps[0][:, 0:nq], func=AF.Ln, bias=epsrms[:, 0:1], scale=1.0 / 128),
              r=[bps[0], bf('eps')], w=[bf('at1')])
            A('act', lambda e: e.activation(out=rec[:, 0:nq], in_=t1[:, 0:nq], func=AF.Exp, scale=-0.5), r=[bf('at1')], w=[bf('rec0')])
            A('dve', lambda e: e.tensor_tensor(out=t2[:, 0:nq], in0=t2[:, 0:nq], in1=rec[:, 0:nq], op=ALU.mult), r=[bf('rec0'), bf('at2')], w=[bf('at2')])
            A('act', lambda e: e.activation(out=dst, in_=t2[:, 0:nq], func=AF.Identity, scale=gsub[:, l:l + 1]), r=[bf('at2'), bf('gsub')], w=[bf('MIXT%d_%d' % (oc, tok0))])

    def mixbufs(k, tb):
        if k >= 6:
            return [bf('MIXF%d_%d' % (k - 6, tb))]
        return [bf('MIXT%d_%d' % (k, tb * 512))]

    def mix(l):
        modulate(l, 1)
        BIG.reset(); TMP.reset()
        QT = BIG.alloc("QT", [128, 6, 1536], BF16)
        WOUT = BIG.alloc("WOUT", [128, 8, 1024], BF16)
        WX = BIG.alloc("WX", [128, 8, 256], BF16)
        KTL = BIG.alloc("KTL", [128, 6, 512], BF16)
        VL = BIG.alloc("VL", [128, 4, 1024], BF16)
        tsq = TMP.alloc("tsq", [128, 384], F32); tn = tsq
        bufs['tn'] = bf('tsq')
        ss6 = TMP.alloc("ss6", [128, 6], F32); sd6 = TMP.alloc("sd6", [128, 6], F32); rs6 = TMP.alloc("rs6", [128, 6], F32)
        r1 = TMP.alloc("r1", [128, 512], F32); r2 = TMP.alloc("r2", [128, 512], F32)
        qk_tok = TMP.alloc("qk_tok", [128, 1536], F32); vu_tok = TMP.alloc("vu_tok", [128, 1024], BF16)
        kt_stage = TMP.alloc("kt_stage", [128, 6, 128], BF16)
        f32a = TMP.alloc("f32a", [128, 512], F32); f32b = f32a; vtmp = TMP.alloc("vtmp", [128, 128], F32)
        r2b = TMP.alloc("r2b", [128, 512], F32)
        rsets = [(r1, r2, 'r1', 'r2'), (f32a, r2b, 'f32a', 'r2b')]
        rcnt = [0]

        def nr():
            rcnt[0] += 1
            return rsets[rcnt[0] % 2]

        winv = w_in[l].rearrange("(c p) n -> p c n", p=128)
        if not pf.pop(('mix', l), False):
            for g in range(4):
                ld('pool', WS[g][:], winv[:, :, g * 512:(g + 1) * 512], [bws[g]])
            wsc[0] = 0
        ld('pool', WX[:], winv[:, :, 2048:2304], [bf('WX')])
        cktf = BIG.alloc("cktf", [128, 768], F32)
        cavv = cav[l].rearrange("(t p) n -> p t n", p=128)
        for kvh in range(2):
            for dup in range(2):
                o = kvh * 128 + dup * 64
                ld('pool', Vc[:, :, o:o + 64], cavv[:, :, kvh * 64:(kvh + 1) * 64], [bf('Vc')])
        ld('pool', Vc[:, :, 256:768], cbv[l].rearrange("(t p) n -> p t n", p=128), [bf('Vc')])
        for t in range(2):
            for kvh in range(2):
                for dup in range(2):
                    o = kvh * 128 + dup * 64
                    ld('pool', cktf[:, o:o + 64], cak[l, t * 128:(t + 1) * 128, kvh * 64:(kvh + 1) * 64], [bf('ckt')])
            ld('pool', cktf[:, 256:768], cbk[l, t * 128:(t + 1) * 128, :], [bf('ckt')])
            for half in range(2):
                b = next_bank()
                nchunk = 4 if half == 0 else 2
                for cc in range(nchunk):
                    c = half * 4 + cc
                    tr(ps[b][:, cc * 128:(cc + 1) * 128], cktf[:, c * 128:(c + 1) * 128], identf[:], [bf('ckt'), bf('identf')], [bps[b]])
                A('dve', lambda e, b=b, t=t, half=half, nchunk=nchunk: e.tensor_copy(
                    out=KTc[:, half * 4:half * 4 + nchunk, t * 128:(t + 1) * 128],
                    in_=ps[b][:, 0:nchunk * 128].rearrange("p (c t) -> p c t", t=128)), r=[bps[b]], w=[bf('KTc')])
        ld('pool', WOUT[:], w_out[l].rearrange("(c p) n -> p c n", p=128), [bf('WOUT')])
        ld('pool', wf[:], w_f[l].rearrange("(c p) n -> p c n", p=128), [bf('wf')])

        BMAP = {0: 0, 3: 1, 4: 2, 1: 3, 2: 4}
        tn3 = tn[:].rearrange("p (h d) -> p h d", d=64)
        qa_dst = qk_tok[:, 0:256].rearrange("p (h d) -> p h d", d=64)
        ka_dst = qk_tok[:, 768:1024].rearrange("p (k u d) -> p k u d", k=2, u=2)
        qb_dst = qk_tok[:, 256:768]; kb_dst = qk_tok[:, 1024:1536]

        def proj_mm(t, groups):
            tb = t // 4
            tsl = slice(t * 128, (t + 1) * 128)
            for g in groups:
                b = BMAP[g]
                ncol = 512 if g < 4 else 256
                for k in range(8):
                    rhs = WS[g][:, k, :] if g < 4 else WX[:, k, :]
                    mm(ps[b][:, 0:ncol], HB[:, k, tsl], rhs, k == 0, k == 7, [bh[k][tb], bws[g] if g < 4 else bf('WX')], [bps[b]])

        def post_a(t):
            prompt = t < 4
            b0, b3, b4 = BMAP[0], BMAP[3], BMAP[4]
            vdst = VL[:, t, :] if prompt else vu_tok[:]
            bvd = bf('VL') if prompt else bf('vu_tok')
            A('act', lambda e: e.activation(out=tsq[:], in_=ps[b0][:, 0:384], func=AF.Square), r=[bps[b0]], w=[bf('tsq')])
            A('dve', lambda e: e.tensor_reduce(out=ss6[:], in_=tsq[:].rearrange("p (h d) -> p h d", d=64), op=ALU.add, axis=AX.X), r=[bf('tsq')], w=[bf('ss6')])
            A('act', lambda e: e.activation(out=sd6[:], in_=ss6[:], func=AF.Sqrt, bias=epsrms[:, 0:1], scale=1.0 / 64), r=[bf('ss6'), bf('eps')], w=[bf('sd6')])
            A('dve', lambda e: e.reciprocal(out=rs6[:], in_=sd6[:]), r=[bf('sd6')], w=[bf('rs6')])
            A('dve', lambda e: e.tensor_tensor(out=tn3, in0=ps[b0][:, 0:384].rearrange("p (h d) -> p h d", d=64),
                                               in1=rs6[:].unsqueeze(2).to_broadcast([128, 6, 64]), op=ALU.mult), r=[bps[b0], bf('rs6')], w=[bf('tn')])
            A('dve', lambda e: e.tensor_tensor(out=tn3, in0=tn3, in1=gqk[:, l], op=ALU.mult), r=[bf('tn'), bf('gqk')], w=[bf('tn')])
            if prompt:
                outs.append(P.dma('sp', lambda e, t=t: e.dma_start(out=nak[l, t * 128:(t + 1) * 128, :], in_=tn[:, 256:384]), r=[bf('tn')]))
                A('act', lambda e: e.activation(out=vtmp[:], in_=ps[b0][:, 384:512], func=AF.Copy), r=[bps[b0]], w=[bf('vtmp')])
                outs.append(P.dma('sp', lambda e, t=t: e.dma_start(out=nav[l, t * 128:(t + 1) * 128, :], in_=vtmp[:]), r=[bf('vtmp')]))
                A('pool', lambda e: e.tensor_copy(out=qa_dst, in_=tn3[:, 0:4, :]), r=[bf('tn')], w=[bf('qk_tok')])
                for u in range(2):
                    A('pool', lambda e, u=u: e.tensor_copy(out=ka_dst[:, :, u, :], in_=tn3[:, 4:6, :]), r=[bf('tn')], w=[bf('qk_tok')])
            else:
                rope(tn3[:, 0:4, :], qa_dst, 4, t - 4, nr(), [bf('tn')], [bf('qk_tok')])
                for u in range(2):
                    rope(tn3[:, 4:6, :], ka_dst[:, :, u, :], 2, t - 4, nr(), [bf('tn')], [bf('qk_tok')])
            va_dst = vdst[:, 0:256].rearrange("p (k u d) -> p k u d", k=2, u=2)
            A('dve', lambda e, va_dst=va_dst: e.tensor_copy(
                out=va_dst, in_=ps[b0][:, 384:512].rearrange("p (k d) -> p k d", d=64).unsqueeze(2).to_broadcast([128, 2, 2, 64])),
              r=[bps[b0]], w=[bvd])
            if prompt:
                A('act', lambda e: e.activation(out=f32b[:], in_=ps[b3][:], func=AF.Copy), r=[bps[b3]], w=[bf('f32a')])
                outs.append(P.dma('sp', lambda e, t=t: e.dma_start(out=nbv[l, t * 128:(t + 1) * 128, :], in_=f32b[:]), r=[bf('f32a')]))
                A('pool', lambda e, vdst=vdst: e.tensor_copy(out=vdst[:, 256:768], in_=f32b[:]), r=[bf('f32a')], w=[bvd])
            else:
                A('act', lambda e, vdst=vdst: e.activation(out=vdst[:, 256:768], in_=ps[b3][:], func=AF.Copy), r=[bps[b3]], w=[bvd])
            A('dve', lambda e, vdst=vdst: e.tensor_copy(out=vdst[:, 768:1024], in_=ps[b4][:, 0:256]), r=[bps[b4]], w=[bvd])

        def post_b(t):
            prompt = t < 4
            b1, b2 = BMAP[1], BMAP[2]
            if prompt:
                A('act', lambda e: e.activation(out=qb_dst, in_=ps[b1][:], func=AF.Copy), r=[bps[b1]], w=[bf('qk_tok')])
                A('act', lambda e: e.activation(out=f32a[:], in_=ps[b2][:], func=AF.Copy), r=[bps[b2]], w=[bf('f32a')])
                outs.append(P.dma('sp', lambda e, t=t: e.dma_start(out=nbk[l, t * 128:(t + 1) * 128, :], in_=f32a[:]), r=[bf('f32a')]))
                A('pool', lambda e: e.tensor_copy(out=kb_dst, in_=f32a[:]), r=[bf('f32a')], w=[bf('qk_tok')])
            else:
                rope(ps[b1][:].rearrange("p (h d) -> p h d", d=64), qb_dst.rearrange("p (h d) -> p h d", d=64), 8, t - 4, nr(), [bps[b1]], [bf('qk_tok')])
                rope(ps[b2][:].rearrange("p (h d) -> p h d", d=64), kb_dst.rearrange("p (h d) -> p h d", d=64), 8, t - 4, nr(), [bps[b2]], [bf('qk_tok')])

        def trans(t):
            prompt = t < 4
            tsl = slice(t * 128, (t + 1) * 128)
            for grp in range(3):
                b = 5 + grp
                for cc in range(4):
                    c = grp * 4 + cc
                    tr(ps[b][:, cc * 128:(cc + 1) * 128], qk_tok[:, c * 128:(c + 1) * 128], identf[:], [bf('qk_tok'), bf('identf')], [bps[b]])
                src = ps[b][:].rearrange("p (c t) -> p c t", t=128)
                if grp == 0:
                    A('act', lambda e, src=src, tsl=tsl: e.activation(out=QT[:, 0:4, tsl], in_=src, func=AF.Copy), r=[bps[b]], w=[bf('QT')])
                elif grp == 1:
                    A('dve', lambda e, src=src, tsl=tsl: e.tensor_copy(out=QT[:, 4:6, tsl], in_=src[:, 0:2, :]), r=[bps[b]], w=[bf('QT')])
                    kd = KTL[:, 0:2, tsl] if prompt else kt_stage[:, 0:2, :]
                    A('dve', lambda e, src=src, kd=kd: e.tensor_copy(out=kd, in_=src[:, 2:4, :]), r=[bps[b]], w=[bf('KTL') if prompt else bf('kt_stage')])
                else:
                    kd = KTL[:, 2:6, tsl] if prompt else kt_stage[:, 2:6, :]
                    A('act', lambda e, src=src, kd=kd: e.activation(out=kd, in_=src, func=AF.Copy), r=[bps[b]], w=[bf('KTL') if prompt else bf('kt_stage')])
            if not prompt:
                ts = t - 4
                for h in range(2):
                    P.dma('sp', lambda e, ts=ts, h=h: e.dma_start(out=gin_kt[h].rearrange("(c p) n -> p c n", p=128)[:, :, ts * 128:(ts + 1) * 128],
                                                                  in_=kt_stage[:, 3 * h:3 * h + 3, :]),
                          r=[bf('kt_stage')], w=[bf('gin_kt%d' % h)])
                hh, t4 = ts // 4, ts % 4
                P.dma('sp', lambda e, hh=hh, t4=t4: e.dma_start(out=gin_vu[hh][t4 * 128:(t4 + 1) * 128, :], in_=vu_tok[:]),
                      r=[bf('vu_tok')], w=[bf('gin_vu%d' % hh)])

        proj_mm(0, (0, 3, 4)); proj_mm(0, (1, 2))
        for t in range(12):
            post_a(t)
            if t + 1 < 12:
                proj_mm(t + 1, (0, 3, 4))
            post_b(t)
            trans(t)
            if t + 1 < 12:
                proj_mm(t + 1, (1, 2))
        fence()
        for h in range(2):
            cc = P.dma('pool', lambda e, h=h: e.collective_compute("AllGather", ALU.bypass, replica_groups=RG, ins=[gin_kt[h].opt()], outs=[gout_kt[h].opt()]),
                       r=[bf('gin_kt%d' % h)], w=[bf('gout_kt%d' % h)], inc=1)
            cc.own = ('cc', 4 * l + h)
            cc = P.dma('pool', lambda e, h=h: e.collective_compute("AllGather", ALU.bypass, replica_groups=RG, ins=[gin_vu[h].opt()], outs=[gout_vu[h].opt()]),
                       r=[bf('gin_vu%d' % h)], w=[bf('gout_vu%d' % h)], inc=1)
            cc.own = ('cc', 4 * l + 2 + h)
        gob = [bf('gout_kt0'), bf('gout_kt1'), bf('gout_vu0'), bf('gout_vu1')]
        WSr.reset(); TMP.reset()
        ktj = [WSr.alloc("ktj", [128, 4, 1024], BF16) for _ in range(2)]
        vj = [WSr.alloc("vj", [128, 32, 128], BF16) for _ in range(2)]
        pT = [TMP.alloc("pT", [128, 512], BF16) for _ in range(4)]
        T = (pT, TMP.alloc("rec", [128, 512], F32), TMP.alloc("at1", [128, 512], F32), TMP.alloc("at2", [128, 512], F32),
             TMP.alloc("asq", [128, 512], BF16),
             [[TMP.alloc("Qz", [128, 512], BF16) for _ in range(2)] for _ in range(2)],
             TMP.alloc("accP", [128, 512], F32))
        for u_ in range(2):
            for g_ in range(2):
                A('dve', lambda e, u_=u_, g_=g_: e.memset(T[5][u_][g_][:], 0.0), w=[bf('qzero')])
        for job in range(6):
            kind = 'A' if job < 2 else 'B'
            vcol = job * 128 if job < 2 else 256 + (job - 2) * 128
            segs = []
            for s_ in range(2):
                KTf = lambda kt, s_=s_, job=job: (KTL[:, job, s_ * 256 + kt * 128:s_ * 256 + (kt + 1) * 128], [bf('KTL')])
                Vf = lambda kt, s_=s_, vcol=vcol: (VL[:, s_ * 2 + kt, vcol:vcol + 128], [bf('VL')])
                segs.append((KTf, Vf, 2, QT[:, job, s_ * 256:(s_ + 1) * 256], s_ * 256, 256))
            attn_unit(l, kind, segs, job, 0, T, qz_eng='act')
        P.add('pool', lambda e: e.nop(), gob, gob)
        gk = [g_.rearrange("(r c p) n -> p r c n", c=3, p=128) for g_ in gout_kt]
        gv = [g_.rearrange("(t p) n -> p t n", p=128) for g_ in gout_vu]
        for job in range(6):
            kind = 'A' if job < 2 else 'B'
            vcol = job * 128 if job < 2 else 256 + (job - 2) * 128
            sl = job % 2
            ld('pool', ktj[sl][:], gk[job // 3][:, :, job % 3, :], [bf('ktj%d' % sl)], r=gob)
            for h in range(2):
                ld('pool', vj[sl][:, 16 * h:16 * h + 16, :], gv[h][:, :, vcol:vcol + 128], [bf('vj%d_%d' % (sl, h))], r=gob)

            def KTf(kt, job=job, sl=sl):
                if kt < 2:
                    return KTc[:, job, kt * 128:(kt + 1) * 128], [bf('KTc')]
                k2 = kt - 2
                h, r, t4 = k2 // 16, (k2 % 16) // 4, k2 % 4
                t = h * 4 + t4
                return ktj[sl][:, r, t * 128:(t + 1) * 128], [bf('ktj%d' % sl)]

            def Vf(kt, vcol=vcol, sl=sl):
                if kt < 2:
                    return Vc[:, kt, vcol:vcol + 128], [bf('Vc')]
                return vj[sl][:, kt - 2, :], [bf('vj%d_%d' % (sl, (kt - 2) // 16))]
            for qb in range(2):
                tok0 = 512 + qb * 512
                attn_unit(l, kind, [(KTf, Vf, 34, QT[:, job, tok0:tok0 + 512], 0, 512)], job, tok0, T, pool_share=True)
        fence()
        if MIXSUB <= 4:
            return
        WSr.reset(); TMP.reset()
        tabC = [WSr.alloc("tabC", [128, 4, 1024], BF16) for _ in range(2)]
        tabS = [WSr.alloc("tabS", [128, 4, 1024], BF16) for _ in range(2)]
        ug = TMP.alloc("ug", [128, 32, 256], BF16)
        XT = nc.alloc_sbuf_tensor_at("XT_%d" % l, [128, 2, 1536], BF16, offset=BIG.base)
        YT = nc.alloc_sbuf_tensor_at("YT_%d" % l, [128, 2, 1536], BF16, offset=BIG.base + 6144)
        FT = nc.alloc_sbuf_tensor_at("FT_%d" % l, [128, 2, 1536], BF16, offset=BIG.base + 12288)
        for h in range(2):
            ld('pool', ug[:, 16 * h:16 * h + 16, :], gv[h][:, :, 768:1024], [bf('ug%d' % h)], r=gob)
        CLv = CLd.rearrange("(t p) n -> p t n", p=128); SLv = SLd.rearrange("(t p) n -> p t n", p=128)
        for grp in range(8):
            sl = grp % 2
            ld('pool', tabC[sl][:], CLv[:, grp * 4:(grp + 1) * 4, :], [bf('tabC%d' % sl)])
            ld('pool', tabS[sl][:], SLv[:, grp * 4:(grp + 1) * 4, :], [bf('tabS%d' % sl)])
            for fc in range(2):
                for lb in range(2):
                    for ti, (tab, bt) in enumerate(((tabC[sl], 'tabC%d' % sl), (tabS[sl], 'tabS%d' % sl))):
                        b = ti * 4 + fc * 2 + lb
                        for i in range(4):
                            ui = (grp % 2) * 16 + (grp // 2) * 4 + i
                            mm(ps[b][:], ug[:, ui, fc * 128:(fc + 1) * 128], tab[:, i, lb * 512:(lb + 1) * 512],
                               grp == 0 and i == 0, grp == 7 and i == 3, [bf('ug%d' % (grp % 2)), bf(bt)], [bps[b]])
        for fc in range(2):
            for lb in range(2):
                A('act', lambda e, fc=fc, lb=lb: e.activation(out=XT[:, fc, 512 + lb * 512:1024 + lb * 512], in_=ps[fc * 2 + lb][:], func=AF.Copy),
                  r=[bps[fc * 2 + lb]], w=[bf('XT%d_%d' % (fc, 1 + lb))])
                A('dve', lambda e, fc=fc, lb=lb: e.tensor_copy(out=YT[:, fc, 512 + lb * 512:1024 + lb * 512], in_=ps[4 + fc * 2 + lb][:]),
                  r=[bps[4 + fc * 2 + lb]], w=[bf('YT%d_%d' % (fc, 1 + lb))])
        for s in range(2):
            for fc in range(2):
                bX = next_bank(); bY = next_bank()
                for lt in range(2):
                    mm(ps[bX][:, 0:256], VL[:, s * 2 + lt, 768 + fc * 128:768 + (fc + 1) * 128], C256[:, lt, :], lt == 0, lt == 1, [bf('VL'), bf('C256')], [bps[bX]])
                for lt in range(2):
                    mm(ps[bY][:, 0:256], VL[:, s * 2 + lt, 768 + fc * 128:768 + (fc + 1) * 128], S256[:, lt, :], lt == 0, lt == 1, [bf('VL'), bf('S256')], [bps[bY]])
                A('act', lambda e, s=s, fc=fc, bX=bX: e.activation(out=XT[:, fc, s * 256:(s + 1) * 256], in_=ps[bX][:, 0:256], func=AF.Copy), r=[bps[bX]], w=[bf('XT%d_0' % fc)])
                A('dve', lambda e, s=s, fc=fc, bY=bY: e.tensor_copy(out=YT[:, fc, s * 256:(s + 1) * 256], in_=ps[bY][:, 0:256]), r=[bps[bY]], w=[bf('YT%d_0' % fc)])
        for tb in range(3):
            ti = 2 if tb == 0 else 0
            for fc in range(2):
                b = next_bank()
                mm(ps[b][:], BCS[:, ti, :], XT[:, fc, tbs(tb)], True, False, [bf('BCS'), bf('XT%d_%d' % (fc, tb))], [bps[b]])
                mm(ps[b][:], BCS[:, ti + 1, :], YT[:, fc, tbs(tb)], False, True, [bf('BCS'), bf('YT%d_%d' % (fc, tb))], [bps[b]])
                A('act' if fc == 0 else 'dve',
                  (lambda e, b=b, fc=fc, tb=tb: e.activation(out=FT[:, fc, tbs(tb)], in_=ps[b][:], func=AF.Copy)) if fc == 0 else
                  (lambda e, b=b, fc=fc, tb=tb: e.tensor_copy(out=FT[:, fc, tbs(tb)], in_=ps[b][:])), r=[bps[b]], w=[bf('FT%d_%d' % (fc, tb))])
            for oc in range(2):
                b = next_bank()
                for fc in range(2):
                    mm(ps[b][:], wf[:, fc, oc * 128:(oc + 1) * 128], FT[:, fc, tbs(tb)], fc == 0, fc == 1, [bf('wf'), bf('FT%d_%d' % (fc, tb))], [bps[b]])
                A('act' if oc == 0 else 'dve',
                  (lambda e, b=b, oc=oc, tb=tb: e.activation(out=MIXT[:, 6 + oc, tbs(tb)], in_=ps[b][:], func=AF.Copy)) if oc == 0 else
                  (lambda e, b=b, oc=oc, tb=tb: e.tensor_copy(out=MIXT[:, 6 + oc, tbs(tb)], in_=ps[b][:])), r=[bps[b]], w=[bf('MIXF%d_%d' % (oc, tb))])
        if MIXDBG:
            for tb in range(3):
                for c in range(8):
                    A('dve', lambda e, c=c, tb=tb: e.tensor_copy(out=XRES[:, c, tbs(tb)], in_=MIXT[:, c, tbs(tb)]), r=mixbufs(c, tb) + [bx[c][tb]], w=[bx[c][tb]])
            fence()
            WSr.reset()
            for i in range(4):
                WS[i] = WSr.alloc("WS%d" % i, [128, 8, 512], BF16)
            return
        for tb in range(3):
            v = 0 if tb == 0 else 1
            for m in range(8):
                b = next_bank()
                for k in range(8):
                    mm(ps[b][:], WOUT[:, k, m * 128:(m + 1) * 128], MIXT[:, k, tbs(tb)], k == 0, k == 7, [bf('WOUT')] + mixbufs(k, tb), [bps[b]])
                A('dve', lambda e, b=b, m=m, tb=tb, v=v: e.scalar_tensor_tensor(out=XRES[:, m, tbs(tb)], in0=ps[b][:], scalar=mod(l, 5, m, v),
                                                                             in1=XRES[:, m, tbs(tb)], op0=ALU.mult, op1=ALU.add),
                  r=[bps[b], bx[m][tb], bf('mod')], w=[bx[m][tb]])
        fence()
        WSr.reset()
        for i in range(4):
            WS[i] = WSr.alloc("WS%d" % i, [128, 8, 512], BF16)
        ffn_prefetch(l, 2, w2gu, w2d)
        layernorm(l, 1, False)
        fence()

    ffn_prefetch(0, 0, w1gu, w1d)
    fence()
    done = False
    for l in range(2):
        if stage <= 4 * l + 0:
            break
        ffn(l, 0, w1gu, w1d, 0, stage == 4 * l + 1, after=(lambda l=l: mix_prefetch(l)))
        if stage <= 4 * l + 1:
            break
        mix(l)
        if stage <= 4 * l + 2:
            break
        ffn(l, 2, w2gu, w2d, 2, l == 1 or stage == 4 * l + 3, after=((lambda: ffn_prefetch(1, 0, w1gu, w1d)) if l == 0 else None))
        if stage <= 4 * l + 3:
            break
    store_y()
    fin = P.add('sp', lambda e: e.nop(), (), ())
    fin.deps = [o for o in outs if o is not None]
    P.plan()
    sems = {e: es.enter_context(nc.semaphore("s_" + e)) for e in ENGS}
    dsems = {}
    for e in ('pool', 'sp'):
        for i in range(P.n_dma_sems):
            dsems[(e, i)] = es.enter_context(nc.semaphore("d_%s%d" % (e, i)))
    for i in range(9):
        dsems[('cc', i)] = es.enter_context(nc.semaphore("cc%d" % i))
    block = es.enter_context(nc.Block())
    P.emit(nc, block, sems, dsems)
    es.close()
    return nc


_NC_CACHE = {}


def _rope_tables(length):
    rows = length // 64
    row = np.repeat(np.arange(rows), 64).astype(np.float32)
    col = np.tile(np.arange(64), rows).astype(np.float32)
    inv = (1.0 / (10000.0 ** (np.arange(0, 32, 2, dtype=np.float32) / 32.0))).astype(np.float32)
    ar = row[:, None] * inv
    ac = col[:, None] * inv
    cos = np.concatenate([np.cos(ar), np.cos(ar), np.cos(ac), np.cos(ac)], -1).astype(np.float32)
    sin = np.concatenate([np.sin(ar), np.sin(ar), np.sin(ac), np.sin(ac)], -1).astype(np.float32)
    sgn = np.where((np.arange(64) % 32) < 16, -1.0, 1.0).astype(np.float32)
    return cos, sin * sgn


def _consts():
    bfl = ml_dtypes.bfloat16
    l4 = np.arange(4096, dtype=np.int64)
    ph = (l4[:, None] * l4[None, :]) % 4096
    ang = ph.astype(np.float64) * (2.0 * np.pi / 4096)
    CL = np.cos(ang).astype(bfl)
    SL = np.sin(ang).astype(bfl)
    l2 = np.arange(256, dtype=np.int64)
    a2 = ((l2[:, None] * l2[None, :]) % 256).astype(np.float64) * (2.0 * np.pi / 256)
    C256 = np.cos(a2).astype(bfl)
    S256 = np.sin(a2).astype(bfl)
    c = np.arange(64, dtype=np.int64)
    a3 = ((c[:, None] * c[None, :]) % 64).astype(np.float64) * (2.0 * np.pi / 64)
    C64 = np.cos(a3)
    S64 = np.sin(a3)
    z = np.zeros((64, 64))
    bd = lambda m: np.block([[m, z], [z, m]])
    BCS = np.stack([bd(C64) / 512.0, -bd(S64) / 512.0, bd(C64) / 128.0, -bd(S64) / 128.0]).astype(bfl)
    return CL, SL, C256, S256, BCS


def kernel(x_prompt, x_sample, cache_a_k, cache_a_v, cache_b_k, cache_b_v, c, c_ctx,
           w_mod, b_mod, w_in, g_qa, g_ka, lam_q1, lam_k1, lam_q2, lam_k2, g_subln,
           w_fourier, w_out, w_ffn1_gu, w_ffn1_down, w_ffn2_gu, w_ffn2_down, ln_g, ln_b, _stage=99):
    f32 = np.float32
    A_ = lambda a: np.ascontiguousarray(np.asarray(a, dtype=f32))
    x_prompt = A_(x_prompt); x_sample = A_(x_sample)
    cache_a_k = A_(cache_a_k); cache_a_v = A_(cache_a_v); cache_b_k = A_(cache_b_k); cache_b_v = A_(cache_b_v)
    c = A_(c); c_ctx = A_(c_ctx)
    if _stage not in _NC_CACHE:
        _NC_CACHE[_stage] = build(_stage)
    nc = _NC_CACHE[_stage]
    CL, SL, C256, S256, BCS = _consts()
    cos, sins = _rope_tables(4096)
    bfl = ml_dtypes.bfloat16
    shared = {
        "w_in": A_(w_in), "w_out": A_(w_out), "w_f": A_(w_fourier),
        "w1gu": A_(w_ffn1_gu), "w1d": A_(w_ffn1_down), "w2gu": A_(w_ffn2_gu), "w2d": A_(w_ffn2_down),
        "lnT": np.ascontiguousarray(np.stack([A_(ln_g), A_(ln_b)]).reshape(2, 2, 3, 8, 128).transpose(4, 0, 1, 2, 3)),
        "gqk": np.ascontiguousarray(np.broadcast_to(
            np.concatenate([np.repeat(A_(g_qa)[:, None, :], 4, 1), np.repeat(A_(g_ka)[:, None, :], 2, 1)], 1)[None], (128, 2, 6, 64))),
        "gsubT": np.ascontiguousarray(A_(g_subln).T),
        "lamv": np.ascontiguousarray(np.broadcast_to(np.stack([A_(lam_q1), A_(lam_k1), A_(lam_q2), A_(lam_k2)], 1)[None], (128, 2, 4, 64))),
        "C256": C256, "S256": S256, "BCS": BCS,
        "identf": np.eye(128, dtype=f32), "identb": np.eye(128, dtype=f32).astype(bfl), "onesb": np.ones((128, 128), dtype=bfl),
    }
    w_mod_f = A_(w_mod)
    bmodT_full = A_(b_mod).reshape(2, 72, 128).transpose(2, 0, 1)
    in_maps = []
    for i in range(8):
        b, r = i // 4, i % 4
        xin = np.concatenate([x_prompt[2 * i], x_prompt[2 * i + 1], x_sample[b, r * 1024:(r + 1) * 1024]], 0)
        cv = np.stack([c_ctx, c[0], c[1]], 0)
        m = dict(shared)
        m["xin"] = np.ascontiguousarray(xin)
        m["cvT"] = np.ascontiguousarray(cv.reshape(3, 8, 128).transpose(2, 1, 0))
        m["wmod"] = np.ascontiguousarray(w_mod_f[:, :, 2304 * r:2304 * (r + 1)])
        m["bmodT"] = np.ascontiguousarray(bmodT_full[:, :, 18 * r:18 * (r + 1)])
        selv = np.zeros((128, 2), f32); selv[:, b] = 1.0
        m["sel"] = selv
        m["ropec"] = np.ascontiguousarray(cos[r * 1024:(r + 1) * 1024].reshape(8, 128, 64).transpose(1, 0, 2))
        m["ropes"] = np.ascontiguousarray(sins[r * 1024:(r + 1) * 1024].reshape(8, 128, 64).transpose(1, 0, 2))
        m["cak"] = np.ascontiguousarray(cache_a_k[b].reshape(2, 256, 128)); m["cav"] = np.ascontiguousarray(cache_a_v[b].reshape(2, 256, 128))
        m["cbk"] = np.ascontiguousarray(cache_b_k[b].reshape(2, 256, 512)); m["cbv"] = np.ascontiguousarray(cache_b_v[b].reshape(2, 256, 512))
        m["CL"] = np.ascontiguousarray(CL[:, r * 1024:(r + 1) * 1024]); m["SL"] = np.ascontiguousarray(SL[:, r * 1024:(r + 1) * 1024])
        in_maps.append(m)
    res = run_bass_kernel_spmd(nc, in_maps[:NCORES], core_ids=list(range(NCORES)))
    y_prompt = np.zeros((16, 256, 1024), f32); y_sample = np.zeros((2, 4096, 1024), f32)
    nak = np.zeros((16, 2, 256, 2, 64), f32); nav = np.zeros((16, 2, 256, 2, 64), f32)
    nbk = np.zeros((16, 2, 256, 4, 128), f32); nbv = np.zeros((16, 2, 256, 4, 128), f32)
    for i in range(NCORES):
        b, r = i // 4, i % 4
        o = res.results[i]
        yy = np.asarray(o["y"], dtype=f32)
        y_prompt[2 * i] = yy[0:256]; y_prompt[2 * i + 1] = yy[256:512]
        y_sample[b, r * 1024:(r + 1) * 1024] = yy[512:1536]
        for s in range(2):
            for l in range(2):
                nak[2 * i + s, l] = np.asarray(o["nak"])[l, s * 256:(s + 1) * 256].reshape(256, 2, 64)
                nav[2 * i + s, l] = np.asarray(o["nav"])[l, s * 256:(s + 1) * 256].reshape(256, 2, 64)
                nbk[2 * i + s, l] = np.asarray(o["nbk"])[l, s * 256:(s + 1) * 256].reshape(256, 4, 128)
                nbv[2 * i + s, l] = np.asarray(o["nbv"])[l, s * 256:(s + 1) * 256].reshape(256, 4, 128)
    return (y_prompt, y_sample, nak, nav, nbk, nbv)
```
